# Optimizing a Trainium2 kernel written in Bass

```python
import math
import jax, jax.numpy as jnp
from jax import lax
import numpy as np

D_MODEL = 2048
BATCH = 16
SEQ = 2048
DEPTH = 4

HEAD_DIM = 128
BLOCK = 128
SWA_Q_HEADS = 8
SWA_KV_HEADS = 2
WINDOW = 128
SB_HEADS = 4
MLA_HEADS = 4
MLA_Q_RANK = 512
MLA_KV_RANK = 512
MLA_NOPE = 128
MLA_ROPE = 64
MLA_V = 128
ROPE_THETA = 10000.0
MEM_LEN = 256
XA_HEADS = 4
REL_BUCKETS = 32
REL_MAX_DIST = 128
D_FF = ((8 * D_MODEL + 3 * 256 - 1) // (3 * 256)) * 256
EPS = 1e-6

WA = SWA_Q_HEADS * HEAD_DIM
KVA = SWA_KV_HEADS * HEAD_DIM
WB = SB_HEADS * HEAD_DIM
WC = MLA_HEADS * MLA_V
MIX_W = WA + WB + WC
XA_W = XA_HEADS * HEAD_DIM
MLA_QK = MLA_NOPE + MLA_ROPE
IN_SIZES = (WA, KVA, KVA, WB, WB, WB, MLA_Q_RANK, MLA_KV_RANK, MLA_ROPE, 3 * D_MODEL)
IN_W = WA + 2 * KVA + 3 * WB + MLA_Q_RANK + MLA_KV_RANK + MLA_ROPE + 3 * D_MODEL

kernel_name = 'hybrid_gated_swa_stickbreak_mla_block'


def split_offsets(sizes):
    out, acc = [], 0
    for s in sizes[:-1]:
        acc += s
        out.append(acc)
    return out


def rms_norm(x, g):
    xf = x.astype(jnp.float32)
    y = xf * lax.rsqrt(jnp.mean(xf * xf, axis=-1, keepdims=True) + EPS)
    return (y * g.astype(jnp.float32)).astype(x.dtype)


def rope(x, positions):
    half = x.shape[-1] // 2
    inv = ROPE_THETA ** (-jnp.arange(half, dtype=jnp.float32) / half)
    ang = positions.astype(jnp.float32)[:, :, None, None] * inv
    cos, sin = jnp.cos(ang), jnp.sin(ang)
    xf = x.astype(jnp.float32)
    x1, x2 = xf[..., :half], xf[..., half:]
    return jnp.concatenate([x1 * cos - x2 * sin, x1 * sin + x2 * cos], axis=-1).astype(x.dtype)


def t5_bucket(rel):
    n = jnp.maximum(rel, 0)
    exact = REL_BUCKETS // 2
    nf = jnp.maximum(n, exact).astype(jnp.float32)
    large = exact + (jnp.log(nf / exact) / math.log(REL_MAX_DIST / exact)
                     * (REL_BUCKETS - exact)).astype(jnp.int32)
    large = jnp.minimum(large, REL_BUCKETS - 1)
    return jnp.where(n < exact, n, large)


def sliding_window_attention(q, k, v, bias, sinks):
    b, s, hq, d = q.shape
    hkv = k.shape[2]
    g = hq // hkv
    nb = s // BLOCK
    qb = q.reshape(b, nb, BLOCK, hkv, g, d)

    def band(t):
        tb = t.reshape(b, nb, BLOCK, hkv, d)
        prev = jnp.concatenate([jnp.zeros_like(tb[:, :1]), tb[:, :-1]], axis=1)
        return jnp.concatenate([prev, tb], axis=2)

    kb, vb = band(k), band(v)
    scores = jnp.einsum('bnqhgd,bnkhd->bnhgqk', qb, kb).astype(jnp.float32) * (d ** -0.5)
    scores = scores + bias.reshape(hkv, g, BLOCK, 2 * BLOCK).astype(jnp.float32)
    i = jnp.arange(BLOCK)[:, None]
    j = jnp.arange(2 * BLOCK)[None, :]
    rel = BLOCK + i - j
    in_window = (rel >= 0) & (rel < WINDOW)
    real_key = (jnp.arange(nb)[:, None, None] > 0) | (j >= BLOCK)[None]
    mask = in_window[None] & real_key
    scores = jnp.where(mask[None, :, None, None], scores, -jnp.inf)
    sink = sinks.astype(jnp.float32).reshape(1, 1, hkv, g, 1, 1)
    m = jnp.maximum(scores.max(axis=-1, keepdims=True), sink)
    p = jnp.exp(scores - m)
    probs = p / (p.sum(axis=-1, keepdims=True) + jnp.exp(sink - m))
    out = jnp.einsum('bnhgqk,bnkhd->bnqhgd', probs.astype(v.dtype), vb)
    return out.reshape(b, s, hq * d)


def stick_breaking_attention(q, k, v):
    b, s, h, d = q.shape
    nb = s // BLOCK
    qb = jnp.moveaxis(q.reshape(b, nb, BLOCK, h, d), 1, 0)
    key_pos = jnp.arange(s)

    def one_block(args):
        qblk, blk = args
        z = jnp.einsum('bqhd,bkhd->bhqk', qblk, k).astype(jnp.float32) * (d ** -0.5)
        qpos = blk * BLOCK + jnp.arange(BLOCK)
        earlier = key_pos[None, :] < qpos[:, None]
        log_keep = jnp.where(earlier, jax.nn.log_sigmoid(-z), 0.0)
        between = lax.cumsum(log_keep, axis=3, reverse=True) - log_keep
        weight = jnp.where(earlier, jnp.exp(jax.nn.log_sigmoid(z) + between), 0.0)
        return jnp.einsum('bhqk,bkhd->bqhd', weight.astype(v.dtype), v)

    out = lax.map(one_block, (qb, jnp.arange(nb)))
    return jnp.moveaxis(out, 0, 1).reshape(b, s, h * d)


def mla_attention(q_nope, q_rope, k_nope, k_rope, v):
    b, s, h, _ = q_nope.shape
    nb = s // BLOCK
    scale = MLA_QK ** -0.5
    qn = jnp.moveaxis(q_nope.reshape(b, nb, BLOCK, h, MLA_NOPE), 1, 0)
    qr = jnp.moveaxis(q_rope.reshape(b, nb, BLOCK, h, MLA_ROPE), 1, 0)
    key_pos = jnp.arange(s)

    def one_block(args):
        qn_b, qr_b, blk = args
        scores = (jnp.einsum('bqhn,bkhn->bhqk', qn_b, k_nope).astype(jnp.float32)
                  + jnp.einsum('bqhr,bkr->bhqk', qr_b, k_rope).astype(jnp.float32)) * scale
        qpos = blk * BLOCK + jnp.arange(BLOCK)
        causal = key_pos[None, :] <= qpos[:, None]
        p = jax.nn.softmax(jnp.where(causal, scores, -jnp.inf), axis=-1)
        return jnp.einsum('bhqk,bkhv->bqhv', p.astype(v.dtype), v)

    out = lax.map(one_block, (qn, qr, jnp.arange(nb)))
    return jnp.moveaxis(out, 0, 1).reshape(b, s, h * MLA_V)


def memory_cross_attention(xn, memn, wq, wkv, q_g, k_g, wo):
    b, s, _ = xn.shape
    m = memn.shape[1]
    q = rms_norm((xn @ wq).reshape(b, s, XA_HEADS, HEAD_DIM), q_g)
    k, v = jnp.split((memn @ wkv).reshape(b, m, XA_HEADS, 2 * HEAD_DIM), 2, axis=-1)
    k = rms_norm(k, k_g)
    scores = jnp.einsum('bqhd,bkhd->bhqk', q, k).astype(jnp.float32) * (HEAD_DIM ** -0.5)
    p = jax.nn.softmax(scores, axis=-1)
    o = jnp.einsum('bhqk,bkhd->bqhd', p.astype(v.dtype), v).reshape(b, s, XA_W)
    return o @ wo


def swiglu(hn, w_gu, w_down):
    gate, up = jnp.split(hn @ w_gu, 2, axis=-1)
    return (jax.nn.silu(gate) * up) @ w_down


def setup_inputs(seed: int = 0) -> dict:
    key = jax.random.key(seed)
    ks = jax.random.split(key, 32)

    def nrm(k, shape, fan_in):
        return jax.random.normal(k, shape, jnp.float32) * (fan_in ** -0.5)

    def gain(k, shape):
        return 1.0 + 0.05 * jax.random.normal(k, shape, jnp.float32)

    start = jax.random.randint(ks[2], (BATCH, 1), 0, 4096, dtype=jnp.int32)
    positions = (start + jnp.arange(SEQ, dtype=jnp.int32)[None, :]).astype(jnp.int32)
    return {
        'x': jax.random.normal(ks[0], (BATCH, SEQ, D_MODEL), jnp.float32),
        'mem': jax.random.normal(ks[1], (BATCH, MEM_LEN, D_MODEL), jnp.float32),
        'positions': positions,
        'rel_bias': 0.5 * jax.random.normal(ks[3], (REL_BUCKETS, SWA_Q_HEADS), jnp.float32),
        'norm_mix': gain(ks[4], (DEPTH, D_MODEL)),
        'w_in': nrm(ks[5], (DEPTH, D_MODEL, IN_W), D_MODEL),
        'swa_q_norm': gain(ks[6], (DEPTH, HEAD_DIM)),
        'swa_k_norm': gain(ks[7], (DEPTH, HEAD_DIM)),
        'swa_sinks': jax.random.normal(ks[8], (DEPTH, SWA_Q_HEADS), jnp.float32),
        'mla_cq_norm': gain(ks[9], (DEPTH, MLA_Q_RANK)),
        'mla_ckv_norm': gain(ks[10], (DEPTH, MLA_KV_RANK)),
        'mla_w_uq': nrm(ks[11], (DEPTH, MLA_Q_RANK, MLA_HEADS * MLA_QK), MLA_Q_RANK),
        'mla_w_ukv': nrm(ks[12], (DEPTH, MLA_KV_RANK, MLA_HEADS * (MLA_NOPE + MLA_V)), MLA_KV_RANK),
        'mla_q_norm': gain(ks[13], (DEPTH, MLA_QK)),
        'mla_k_norm': gain(ks[14], (DEPTH, MLA_QK)),
        'w_branch': nrm(ks[15], (DEPTH, MIX_W, D_MODEL), MIX_W),
        'w_out': nrm(ks[16], (DEPTH, D_MODEL, D_MODEL), D_MODEL),
        'norm_xa': gain(ks[17], (DEPTH, D_MODEL)),
        'norm_mem': gain(ks[18], (DEPTH, D_MODEL)),
        'xa_wq': nrm(ks[19], (DEPTH, D_MODEL, XA_W), D_MODEL),
        'xa_wkv': nrm(ks[20], (DEPTH, D_MODEL, 2 * XA_W), D_MODEL),
        'xa_q_norm': gain(ks[21], (DEPTH, HEAD_DIM)),
        'xa_k_norm': gain(ks[22], (DEPTH, HEAD_DIM)),
        'xa_wo': nrm(ks[23], (DEPTH, XA_W, D_MODEL), XA_W),
        'norm_ffn': gain(ks[24], (DEPTH, D_MODEL)),
        'ffn_w_gu': nrm(ks[25], (DEPTH, D_MODEL, 2 * D_FF), D_MODEL),
        'ffn_w_down': nrm(ks[26], (DEPTH, D_FF, D_MODEL), D_FF),
    }


def reference(x, mem, positions, rel_bias, norm_mix, w_in, swa_q_norm, swa_k_norm, swa_sinks,
              mla_cq_norm, mla_ckv_norm, mla_w_uq, mla_w_ukv, mla_q_norm, mla_k_norm,
              w_branch, w_out, norm_xa, norm_mem, xa_wq, xa_wkv, xa_q_norm, xa_k_norm, xa_wo,
              norm_ffn, ffn_w_gu, ffn_w_down):
    b, s, _ = x.shape
    offsets = split_offsets(IN_SIZES)

    i = jnp.arange(BLOCK)[:, None]
    j = jnp.arange(2 * BLOCK)[None, :]
    swa_bias = jnp.transpose(rel_bias[t5_bucket(BLOCK + i - j)], (2, 0, 1))

    for l in range(DEPTH):
        h = rms_norm(x, norm_mix[l])
        qa, ka, va, qb, kb, vb, cq, ckv, kr, gates = jnp.split(h @ w_in[l], offsets, axis=-1)

        qa = rms_norm(qa.reshape(b, s, SWA_Q_HEADS, HEAD_DIM), swa_q_norm[l])
        ka = rms_norm(ka.reshape(b, s, SWA_KV_HEADS, HEAD_DIM), swa_k_norm[l])
        va = va.reshape(b, s, SWA_KV_HEADS, HEAD_DIM)
        o_a = sliding_window_attention(qa, ka, va, swa_bias, swa_sinks[l])

        o_b = stick_breaking_attention(qb.reshape(b, s, SB_HEADS, HEAD_DIM),
                                       kb.reshape(b, s, SB_HEADS, HEAD_DIM),
                                       vb.reshape(b, s, SB_HEADS, HEAD_DIM))

        q_c = (rms_norm(cq, mla_cq_norm[l]) @ mla_w_uq[l]).reshape(b, s, MLA_HEADS, MLA_QK)
        kv_c = (rms_norm(ckv, mla_ckv_norm[l]) @ mla_w_ukv[l]).reshape(b, s, MLA_HEADS, MLA_NOPE + MLA_V)
        qg, kg = mla_q_norm[l], mla_k_norm[l]
        q_nope = rms_norm(q_c[..., :MLA_NOPE], qg[:MLA_NOPE])
        q_rope = rope(rms_norm(q_c[..., MLA_NOPE:], qg[MLA_NOPE:]), positions)
        k_nope = rms_norm(kv_c[..., :MLA_NOPE], kg[:MLA_NOPE])
        v_c = kv_c[..., MLA_NOPE:]
        k_rope = rope(rms_norm(kr, kg[MLA_NOPE:])[:, :, None, :], positions)[:, :, 0, :]
        o_c = mla_attention(q_nope, q_rope, k_nope, k_rope, v_c)

        g_a, g_b, g_c = jnp.split(jax.nn.sigmoid(gates), 3, axis=-1)
        wbr = w_branch[l]
        merged = (g_a * (o_a @ wbr[:WA])
                  + g_b * (o_b @ wbr[WA:WA + WB])
                  + g_c * (o_c @ wbr[WA + WB:]))
        x = x + merged @ w_out[l]

        x = x + memory_cross_attention(rms_norm(x, norm_xa[l]), rms_norm(mem, norm_mem[l]),
                                       xa_wq[l], xa_wkv[l], xa_q_norm[l], xa_k_norm[l], xa_wo[l])

        x = x + swiglu(rms_norm(x, norm_ffn[l]), ffn_w_gu[l], ffn_w_down[l])
    return x
```

```python
import math
from contextlib import ExitStack

import numpy as np
import concourse.bass as bass
import concourse.mybir as mybir
from concourse.bass_utils import run_bass_kernel_spmd

F32 = mybir.dt.float32
BF16 = mybir.dt.bfloat16
I32 = mybir.dt.int32
U8 = mybir.dt.uint8
AF = mybir.ActivationFunctionType
ALU = mybir.AluOpType

ENGS = ("pe", "act", "dve", "pool", "sp")
DMA_RING = 8

D = 2048
SEQ = 2048
NTT = 4
TT = 512
DEPTH = 4
IN_QKV = 4160
D_FF = 5632
EPS = 1e-6
NVEC = 80
NEG = -30000.0


class Sched:
    def __init__(self):
        self.ops = []
        self.strict = set()
        self.nobar = set()

    def op(self, eng, fn, reads=(), writes=(), dma=False, strict=False, nobar=False):
        self.ops.append((eng, fn, tuple(reads), tuple(writes), dma))
        if strict:
            self.strict.add(len(self.ops) - 1)
        if nobar:
            self.nobar.add(len(self.ops) - 1)

    def barrier(self):
        self.ops.append(("bar", None, (), (), False))

    def emit(self, nc, sems, dma_sems):
        ops = self.ops
        n = len(ops)
        last_w = {}
        readers = {}
        deps = [None] * n
        last_on = {}
        last_dma = {}
        dctr = {e: 0 for e in ENGS}
        pending_bar = {}
        for i, (eng, fn, reads, writes, dma) in enumerate(ops):
            if eng == "bar":
                bd = set(last_on.values()) | set(last_dma.values())
                for e in ENGS:
                    pending_bar[e] = bd
                last_w = {k_: v_ for k_, v_ in last_w.items() if k_[0] == "wb"}
                readers = {k_: v_ for k_, v_ in readers.items() if k_[0] == "wb"}
                deps[i] = set()
                continue
            d = set()
            pb = None if i in self.nobar else pending_bar.pop(eng, None)
            if pb:
                d.update(pb)
            for r in reads:
                w = last_w.get(r)
                if w is not None:
                    d.add(w)
            for w_ in writes:
                w = last_w.get(w_)
                if w is not None:
                    d.add(w)
                rs = readers.get(w_)
                if rs:
                    d.update(rs)
            for r in reads:
                rl = readers.setdefault(r, [])
                if not dma:
                    for q in range(len(rl)):
                        if ops[rl[q]][0] == eng and not ops[rl[q]][4]:
                            rl[q] = i
                            break
                    else:
                        rl.append(i)
                else:
                    rl.append(i)
            for w_ in writes:
                last_w[w_] = i
                readers[w_] = []
            d.discard(i)
            deps[i] = d
            if dma:
                last_dma[(eng, dctr[eng] % DMA_RING)] = i
                dctr[eng] += 1
            else:
                last_on[eng] = i
        signal = [False] * n
        for i in range(n):
            ei = ops[i][0]
            if ei == "bar":
                continue
            for j in deps[i]:
                ej, _, _, _, dj = ops[j]
                if dj:
                    continue
                if ej != ei or i in self.strict:
                    signal[j] = True
        tok = [None] * n
        cnt = {e: 0 for e in ENGS}
        dcnt = {e: 0 for e in ENGS}
        dval = {}
        ring_prev = [None] * n
        ring_hist = {e: [] for e in ENGS}
        for i, (eng, fn, reads, writes, dma) in enumerate(ops):
            if eng == "bar":
                continue
            if dma:
                k = dcnt[eng]
                slot = k % DMA_RING
                dcnt[eng] = k + 1
                v = dval.get((eng, slot), 0) + 16
                dval[(eng, slot)] = v
                tok[i] = (("d", eng, slot), v)
                h = ring_hist[eng]
                if len(h) >= DMA_RING:
                    ring_prev[i] = h[-DMA_RING]
                h.append(i)
            elif signal[i]:
                cnt[eng] += 1
                tok[i] = (("e", eng), cnt[eng])
        known = {e: {} for e in ENGS}
        clock_at = {}
        streams = {e: [] for e in ENGS}
        for i, (eng, fn, reads, writes, dma) in enumerate(ops):
            if eng == "bar":
                continue
            kn = known[eng]
            need = {}
            dl = list(deps[i])
            if ring_prev[i] is not None:
                dl.append(ring_prev[i])
            strict_w = []
            for j in dl:
                ej, _, _, _, dj = ops[j]
                if (not dj) and ej == eng:
                    if i in self.strict:
                        strict_w.append(tok[j])
                    continue
                sk, v = tok[j]
                if kn.get(sk, 0) >= v:
                    continue
                if need.get(sk, 0) < v:
                    need[sk] = v
            waits = list(strict_w)
            for sk, v in need.items():
                if kn.get(sk, 0) >= v:
                    continue
                waits.append((sk, v))
                kn[sk] = v
                snap = clock_at.get((sk, v))
                if snap:
                    for a, b in snap.items():
                        if kn.get(a, 0) < b:
                            kn[a] = b
            t = tok[i]
            if t is not None:
                if not dma:
                    kn[t[0]] = t[1]
                clock_at[t] = dict(kn)
            streams[eng].append((waits, fn, t, dma))
        self.n_waits = sum(len(w) for e in ENGS for (w, _, _, _) in streams[e])
        self.n_signal = sum(1 for s in signal if s)
        self.max_tok = dict(cnt)
        final_tokens = dict(dval)

        def semof(sk):
            if sk[0] == "e":
                return sems[sk[1]]
            return dma_sems[(sk[1], sk[2])]

        handles = {"pe": "tensor", "act": "scalar", "dve": "vector", "pool": "gpsimd", "sp": "sync"}
        with nc.Block() as block:
            for e in ENGS:
                st = streams[e]
                final = e == "sp"

                def body(h, st=st, final=final):
                    for waits, fn, t, dma in st:
                        for sk, v in waits:
                            h.wait_ge(semof(sk), v)
                        ins = fn(h)
                        if t is not None:
                            ins.then_inc(semof(t[0]), 16 if dma else 1)
                    if final:
                        for (q, slot), v in final_tokens.items():
                            h.wait_ge(dma_sems[(q, slot)], v)

                if st or final:
                    getattr(block, handles[e])(body)


class Tile:
    __slots__ = ("ap", "key")

    def __init__(self, ap, key):
        self.ap = ap
        self.key = key


class Builder:
    def __init__(self, nc, S, es, NL, NS, debug):
        self.nc, self.S, self.es, self.NL, self.NS, self.debug = nc, S, es, NL, NS, debug
        self.uid = 0
        self.ps_i = 0
        self.rot_ctr = {}
        self.ps_limit = 8
        self.act_dve = 0

    def static(self, name, shape, dt):
        t = self.es.enter_context(self.nc.sbuf_tensor(name, shape, dt))
        return t

    def arena_reset(self):
        self.S.barrier()
        self.aoff = 0

    def alloc(self, name, shape, dt):
        bpe = 4 if dt in (F32, I32) else 2
        per = int(np.prod(shape[1:])) * bpe
        off = (self.aoff + 31) // 32 * 32
        assert off + per <= self.ARENA, (name, off, per, self.ARENA)
        self.aoff = off + per
        ap = self.arena[:, off:off + per].bitcast(dt)
        if len(shape) == 3:
            ap = ap.rearrange("p (a b) -> p a b", a=shape[1])
        elif len(shape) == 4:
            ap = ap.rearrange("p (a b c) -> p a b c", a=shape[1], b=shape[2])
        if shape[0] < 128:
            ap = ap[0:shape[0]]
        self.uid += 1
        return Tile(ap, (name, self.uid))

    def ring(self, name, shape, dt, n):
        return [self.alloc(f"{name}{i}", shape, dt) for i in range(n)]

    def psum(self):
        i = self.ps_i % self.ps_limit
        self.ps_i += 1
        return Tile(self.ps[i][:, :], ("ps", i))

    def psum_at(self, i):
        return Tile(self.ps[i][:, :], ("ps", i))

    def psum_rot(self, lo, n):
        c = self.rot_ctr.get((lo, n), 0)
        self.rot_ctr[(lo, n)] = c + 1
        i = lo + c % n
        return Tile(self.ps[i][:, :], ("ps", i))

    def pipeline(self, units, oldest_first=False):
        n = len(units)
        ns = max(len(u) for u in units)
        for t in range(n + ns - 1):
            for k in (range(ns - 1, -1, -1) if oldest_first else range(ns)):
                u = t - k
                if 0 <= u < n and k < len(units[u]) and units[u][k] is not None:
                    units[u][k]()

    def norm_unit(self, mm_fn, srcs_fn, gains, outs, Dn, P=128, N=TT, extra_reads=(), post=None, pslo=0, psn=4, sslo=4, ssn=4):
        st = {}

        def s0():
            p = self.psum_rot(pslo, psn) if mm_fn is not None else None
            if mm_fn is not None:
                mm_fn(p)
            srcs = srcs_fn(p)
            sqs = []
            for (sap, skey) in srcs:
                sq = self.sqring[self.sq_i % len(self.sqring)]
                self.sq_i += 1
                self.act(sq.ap[0:P, 0:N], sap, AF.Square, [skey], [sq.key])
                sqs.append(sq)
            st["srcs"], st["sqs"] = srcs, sqs

        def s1():
            ss = self.psum_rot(sslo, ssn)
            sqs = st["sqs"]
            for c, sq in enumerate(sqs):
                self.mm(ss.ap[0:P, 0:N], self.ones_bf[0:P, 0:P], sq.ap[0:P, 0:N], c == 0, c == len(sqs) - 1, [sq.key], [ss.key])
            rs = self.rsring[self.rs_i % len(self.rsring)]
            self.rs_i += 1
            self.act(rs.ap[0:P, 0:N], ss.ap[0:P, 0:N], AF.Ln, [ss.key], [rs.key], scale=1.0 / Dn, bias=self.eps_col[0:P, :])
            self.act(rs.ap[0:P, 0:N], rs.ap[0:P, 0:N], AF.Exp, [rs.key], [rs.key], scale=-0.5)
            st["rs"] = rs

        def s2():
            rs = st["rs"]
            for (sap, skey), g, (oap, okey) in zip(st["srcs"], gains, outs):
                self.stt(oap, sap, g, rs.ap[0:P, 0:N], ALU.mult, ALU.mult, [skey, rs.key] + list(extra_reads), [okey])
            if post is not None:
                post()

        return [s0, s1, s2]

    def mm(self, out, lhsT, rhs, start, stop, reads, writes):
        self.S.op("pe", lambda t: t.matmul(out, lhsT=lhsT, rhs=rhs, start=start, stop=stop), reads, writes)

    def tr(self, out, in_, ident, reads, writes):
        self.S.op("pe", lambda t: t.transpose(out, in_, ident), reads, writes)

    def act(self, out, in_, func, reads, writes, scale=1.0, bias=None, accum_out=None, strict=False):
        kw = {}
        if bias is not None:
            kw["bias"] = bias
        if accum_out is not None:
            kw["accum_out"] = accum_out
        self.S.op("act", lambda a: a.activation(out, in_, func, scale=scale, **kw), reads, writes, strict=strict)

    def tt(self, eng, out, in0, in1, op, reads, writes):
        self.S.op(eng, lambda v: v.tensor_tensor(out, in0, in1, op), reads, writes)

    def ts(self, eng, out, in0, s1, s2, op0, op1, reads, writes):
        if op1 is None:
            self.S.op(eng, lambda v: v.tensor_scalar(out, in0, s1, None, op0), reads, writes)
        else:
            self.S.op(eng, lambda v: v.tensor_scalar(out, in0, s1, s2, op0, op1), reads, writes)

    def stt(self, out, in0, scalar, in1, op0, op1, reads, writes):
        self.S.op("dve", lambda v: v.scalar_tensor_tensor(out, in0, scalar, in1, op0, op1), reads, writes)

    def copy(self, eng, out, in_, reads, writes):
        if eng == "act":
            self.S.op("act", lambda a: a.activation(out, in_, AF.Copy), reads, writes)
        else:
            self.S.op(eng, lambda v: v.tensor_copy(out, in_), reads, writes)

    def evac(self, out, in_, reads, writes):
        self.act_dve += 1
        self.copy("act" if self.act_dve % 2 else "dve", out, in_, reads, writes)

    def recip(self, out, in_, reads, writes):
        self.S.op("dve", lambda v: v.reciprocal(out, in_), reads, writes)

    def dma(self, q, out, in_, reads, writes, nobar=False):
        self.S.op(q, lambda g: g.dma_start(out=out, in_=in_), reads, writes, dma=True, nobar=nobar)

    def pnorm(self, srcs, gains, outs, Dn, P=128, N=TT, extra_reads=(), post_scale=1.0):
        ss = self.psum()
        nsrc = len(srcs)
        for c, (sap, skey) in enumerate(srcs):
            sq = self.sqring[self.sq_i % len(self.sqring)]
            self.sq_i += 1
            self.act(sq.ap[0:P, 0:N], sap, AF.Square, [skey], [sq.key])
            self.mm(ss.ap[0:P, 0:N], self.ones_bf[0:P, 0:P], sq.ap[0:P, 0:N], c == 0, c == nsrc - 1,
                    [sq.key], [ss.key])
        rs = self.rsring[self.rs_i % len(self.rsring)]
        self.rs_i += 1
        self.act(rs.ap[0:P, 0:N], ss.ap[0:P, 0:N], AF.Ln, [ss.key], [rs.key], scale=1.0 / Dn, bias=self.eps_col[0:P, :])
        if post_scale != 1.0:
            self.act(rs.ap[0:P, 0:N], rs.ap[0:P, 0:N], AF.Exp, [rs.key], [rs.key], scale=-0.5,
                     bias=self.lnps_col[0:P, :])
        else:
            self.act(rs.ap[0:P, 0:N], rs.ap[0:P, 0:N], AF.Exp, [rs.key], [rs.key], scale=-0.5)
        for (sap, skey), g, (oap, okey) in zip(srcs, gains, outs):
            self.stt(oap, sap, g, rs.ap[0:P, 0:N], ALU.mult, ALU.mult, [skey, rs.key] + list(extra_reads), [okey])

    def wload(self, src_ap, ncols, kc=16):
        i = self.w_i % len(self.wb)
        self.w_i += 1
        t = self.wb[i]
        view = t.ap[:, 0:kc * ncols].rearrange("p (c n) -> p c n", c=kc)
        self.dma("pool", view, src_ap.rearrange("(c p) n -> p c n", p=128), [], [t.key], nobar=True)
        return view, t.key


def norm_steps(b, x_d, tsl, xkey, gains, hf, xp, xpc, rsn, K_VEC, sqr, sqc):
    ss = b.psum_at(7)
    st = {}

    def p1a(g):
        def f():
            xq = xp[xpc[0] % len(xp)]
            xpc[0] += 1
            b.dma("sp", xq.ap, x_d[:, g * 4:(g + 1) * 4, tsl], [xkey], [xq.key])
            sqs = []
            for j in range(4):
                sq = sqr[sqc[0] % len(sqr)]
                sqc[0] += 1
                b.act(sq.ap, xq.ap[:, j, :], AF.Square, [xq.key], [sq.key])
                sqs.append(sq)
            st[g] = sqs
        return f

    def p1b(g):
        def f():
            for j, sq in enumerate(st[g]):
                b.mm(ss.ap, b.ones_bf, sq.ap, g == 0 and j == 0, g == 3 and j == 3, [sq.key], [ss.key])
        return f

    def pr():
        b.act(rsn.ap, ss.ap, AF.Ln, [ss.key], [rsn.key], scale=1.0 / D, bias=b.eps_col)
        b.act(rsn.ap, rsn.ap, AF.Exp, [rsn.key], [rsn.key], scale=-0.5)

    def p2(g):
        def f():
            xq = xp[xpc[0] % len(xp)]
            xpc[0] += 1
            b.dma("sp", xq.ap, x_d[:, g * 4:(g + 1) * 4, tsl], [xkey], [xq.key])
            for j in range(4):
                c = g * 4 + j
                b.stt(hf.ap[:, c, :], xq.ap[:, j, :], gains[c], rsn.ap, ALU.mult, ALU.mult, [xq.key, rsn.key, K_VEC], [hf.key])
        return f

    def seq(*fs):
        def f():
            for x in fs:
                x()
        return f

    steps = [p1a(0), seq(p1a(1), p1b(0)), seq(p1a(2), p1b(1)), seq(p1a(3), p1b(2)), seq(p1b(3), pr)]
    steps += [p2(g) for g in range(4)]
    return steps


def vcol(b, l, j):
    return b.vec[:, l * NVEC + j:l * NVEC + j + 1]


def build_program(NL=DEPTH, NS=2, debug=False):
    nc = bass.Bass("TRN2", target_bir_lowering=False)
    S = Sched()
    es = ExitStack()
    b = Builder(nc, S, es, NL, NS, debug)

    def din(name, shape, dt=F32):
        return nc.dram_tensor(name, list(shape), dt, kind="ExternalInput").ap()

    def dscr(name, shape, dt):
        return nc.dram_tensor(name, list(shape), dt, kind=("ExternalOutput" if debug else "Internal")).ap()

    x_d = din("x", [NS, SEQ, D])
    mem_d = din("mem", [NS, 256, D])
    pos_d = din("pos", [NS, 64, SEQ], I32)
    biasT_d = din("biasT", [128, 2 * 8 * 128])
    sinks_d = din("sinks", [NL, 1, 1024])
    vecs_d = din("vecs", [128, NL * NVEC])
    NCST = 128 * 4 + 256 + 64 + 4
    cst_d = din("consts", [128, NCST])
    win_d = din("w_in", [NL, D, IN_QKV])
    wpc_d = din("w_pc", [NL, 16, D, 512])
    wout_d = din("w_out", [NL, D, D])
    wuq_d = din("w_uq", [NL, 512, 768])
    wukv_d = din("w_ukv", [NL, 512, 1024])
    xwq_d = din("xa_wq", [NL, D, 512])
    xwkv_d = din("xa_wkv", [NL, D, 1024])
    xwo_d = din("xa_wo", [NL, 512, D])
    wgu_d = din("w_gu", [NL, 22, D, 512])
    wdn_d = din("w_down", [NL, D_FF, D])
    out_d = nc.dram_tensor("out", [NS, SEQ, D], F32, kind="ExternalOutput").ap()

    xT_d = dscr("xT", [NS, 128, 16, SEQ], F32)
    memT_d = dscr("memT", [NS, 128, 16, 256], F32)
    rope_d = dscr("ropeT", [NS, 2, 64, SEQ], F32)
    QA_d = dscr("QA", [8, 128, SEQ], BF16)
    KA_d = dscr("KA", [2, 128, SEQ], BF16)
    VA_d = dscr("VA", [SEQ, 256], BF16)
    QB_d = dscr("QB", [4, 128, SEQ], BF16)
    KB_d = dscr("KB", [4, 128, SEQ], BF16)
    VB_d = dscr("VB", [SEQ, 512], BF16)
    CQ_d = dscr("CQ", [4, 128, SEQ], F32)
    CKV_d = dscr("CKV", [4, 128, SEQ], F32)
    KRr_d = dscr("KRr", [64, SEQ], F32)
    QN_d = dscr("QN", [4, 128, SEQ], BF16)
    QR_d = dscr("QR", [4, 64, SEQ], BF16)
    KN_d = dscr("KN", [4, 128, SEQ], BF16)
    KR_d = dscr("KR", [64, SEQ], BF16)
    VC_d = dscr("VC", [SEQ, 512], BF16)
    OT_d = dscr("OT", [16, 128, SEQ], BF16)

    if debug:
        dbg1_d = dscr("dbg_x1", [128, 16, SEQ], F32)
        dbg2_d = dscr("dbg_x2", [128, 16, SEQ], F32)
    cst = b.static("cst", [128, NCST], F32)
    vec = b.static("vec", [128, NL * NVEC], F32)
    b.vec = vec
    cbf = b.static("cbf", [128, 128 * 5], BF16)
    onesneg = b.static("onesneg", [128, 128], BF16)
    biasM = b.static("biasM", [128, 2, 8, 128], F32)
    esinkB = b.static("esinkB", [128, 1024], F32)
    small = b.static("small", [128, 8], F32)
    hT_flat = b.static("hT", [128, 4 * 16 * TT], BF16)
    wb_t = [b.static(f"wb{i}", [128, 16 * 512], BF16) for i in range(3)]
    b.wb = [Tile(t[:, :], ("wb", i)) for i, t in enumerate(wb_t)]
    b.w_i = 0
    used = NCST * 4 + NL * NVEC * 4 + 640 * 2 + 256 + 8192 + 4096 + 32 + 65536 + 3 * 16384
    b.ARENA = (nc.sbuf_bytes_remaining - 1024) // 32 * 32
    b.arena = b.static("arena", [128, b.ARENA], U8)
    b.aoff = 0
    b.ps = [es.enter_context(nc.psum_tensor(f"ps{i}", [128, 512], F32)) for i in range(8)]
    sems = {e: es.enter_context(nc.semaphore(f"s_{e}")) for e in ENGS}
    dsems = {(e, k): es.enter_context(nc.semaphore(f"d_{e}{k}")) for e in ("sp", "pool", "act") for k in range(DMA_RING)}

    ident_f = cst[:, 0:128]
    uineg_f = cst[:, 128:256]
    tri_incl_f = cst[:, 256:384]
    tri_strict_f = cst[:, 384:512]
    maskneg_f = cst[:, 512:768]
    rrot_f = cst[0:64, 768:832]
    invf = cst[0:64, 832:833]
    ident_bf = cbf[:, 0:128]
    b.ones_bf = cbf[:, 128:256]
    uineg_bf = cbf[:, 256:384]
    tri_incl_bf = cbf[:, 384:512]
    tri_strict_bf = cbf[:, 512:640]
    b.eps_col = small[:, 0:1]
    b.lnps_col = small[:, 1:2]
    b.one_col = small[:, 2:3]
    K_CST, K_VEC, K_CBF = ("cst",), ("vec",), ("cbf",)

    hT = [Tile(hT_flat[:, tt * 16 * TT:(tt + 1) * 16 * TT].rearrange("p (c t) -> p c t", c=16), ("hT", tt))
          for tt in range(NTT)]

    def phase_begin(nsq=4, nrs=3):
        b.arena_reset()
        b.sqring = b.ring("sq", [128, TT], BF16, nsq)
        b.rsring = b.ring("rs", [128, TT], F32, nrs)
        b.sq_i = 0
        b.rs_i = 0

    phase_begin()
    b.dma("sp", cst[:, :], cst_d, [], [K_CST])
    b.dma("sp", vec[:, :], vecs_d, [], [K_VEC])
    S.op("dve", lambda v: v.memset(small[:, 0:1], EPS), [], [("small",)])
    S.op("dve", lambda v: v.memset(small[:, 1:2], 0.0), [], [("small",)])
    S.op("dve", lambda v: v.memset(small[:, 2:3], 1.0), [], [("small",)])
    S.op("dve", lambda v: v.tensor_copy(cbf[:, 0:128], ident_f), [K_CST], [K_CBF])
    S.op("dve", lambda v: v.memset(cbf[:, 128:256], 1.0), [], [K_CBF])
    S.op("dve", lambda v: v.memset(onesneg[:, :], -1.0), [], [K_CBF])
    S.op("dve", lambda v: v.tensor_copy(cbf[:, 256:640], cst[:, 128:512]), [K_CST], [K_CBF])
    bt = b.alloc("biasT", [128, 2, 8, 128], F32)
    b.dma("sp", bt.ap.rearrange("p a b c -> p (a b c)"), biasT_d, [], [bt.key])
    for c in range(2):
        mk = maskneg_f[:, c * 128:(c + 1) * 128].unsqueeze(1).broadcast_to([128, 8, 128])
        b.tt("dve", biasM[:, c, :, :], bt.ap[:, c, :, :], mk, ALU.add, [bt.key, K_CST], [("biasM",)])

    phase_begin()
    xin = b.ring("xin", [128, D], F32, 2)
    xst = b.ring("xst", [128, 16, 128], F32, 2)
    for s in range(NS):
        for tc in range(16):
            xi = xin[tc % 2]
            xo = xst[tc % 2]
            b.dma("sp", xi.ap, x_d[s, tc * 128:(tc + 1) * 128, :], [], [xi.key])
            for g in range(4):
                p = b.psum()
                for j in range(4):
                    c = g * 4 + j
                    b.tr(p.ap[:, j * 128:(j + 1) * 128], xi.ap[:, c * 128:(c + 1) * 128], ident_f, [xi.key, K_CST], [p.key])
                b.evac(xo.ap[:, g * 4:(g + 1) * 4, :], p.ap.rearrange("p (a b) -> p a b", a=4), [p.key], [xo.key])
            b.dma("sp", xT_d[s, :, :, tc * 128:(tc + 1) * 128], xo.ap, [xo.key], [("xT", s, tc // 4)])
        for mc in range(2):
            xi = xin[mc % 2]
            xo = xst[mc % 2]
            b.dma("sp", xi.ap, mem_d[s, mc * 128:(mc + 1) * 128, :], [], [xi.key])
            junk = b.alloc("junk", [128, D], BF16) if (s == 0 and mc == 0) else junk
            ssq = b.alloc("ssq", [128, 4], F32) if (s == 0 and mc == 0) else ssq
            b.act(junk.ap, xi.ap, AF.Square, [xi.key], [junk.key, ssq.key], accum_out=ssq.ap[:, 0:1])
            b.act(ssq.ap[:, 1:2], ssq.ap[:, 0:1], AF.Ln, [ssq.key], [ssq.key], scale=1.0 / D, bias=b.eps_col, strict=True)
            b.act(ssq.ap[:, 2:3], ssq.ap[:, 1:2], AF.Exp, [ssq.key], [ssq.key], scale=-0.5, strict=True)
            b.ts("dve", xi.ap, xi.ap, ssq.ap[:, 2:3], None, ALU.mult, None, [xi.key, ssq.key], [xi.key])
            for g in range(4):
                p = b.psum()
                for j in range(4):
                    c = g * 4 + j
                    b.tr(p.ap[:, j * 128:(j + 1) * 128], xi.ap[:, c * 128:(c + 1) * 128], ident_f, [xi.key, K_CST], [p.key])
                b.evac(xo.ap[:, g * 4:(g + 1) * 4, :], p.ap.rearrange("p (a b) -> p a b", a=4), [p.key], [xo.key])
            b.dma("sp", memT_d[s, :, :, mc * 128:(mc + 1) * 128], xo.ap, [xo.key], [("memT", s)])
        if s == 0:
            QS = 512
            posi = b.alloc("posi", [64, QS], I32)
            ang = b.alloc("ang", [64, QS], F32)
            kf = b.alloc("kf", [64, QS], F32)
            ki = b.alloc("ki", [64, QS], I32)
            rr = b.alloc("rr", [64, QS], F32)
            msk = b.alloc("msk", [64, QS], F32)
        C1 = 6.28125
        C2 = 2 * math.pi - C1
        for qq in range(SEQ // QS):
            qsl = slice(qq * QS, (qq + 1) * QS)
            b.dma("sp", posi.ap, pos_d[s][:, qsl], [], [posi.key])
            b.copy("dve", ang.ap, posi.ap, [posi.key], [ang.key])
            b.ts("dve", ang.ap, ang.ap, invf, None, ALU.mult, None, [ang.key, K_CST], [ang.key])
            for which, shift in ((1, 0.0), (0, math.pi / 2)):
                b.ts("dve", kf.ap, ang.ap, shift, 1.0 / (2 * math.pi), ALU.add, ALU.mult, [ang.key], [kf.key])
                b.copy("dve", ki.ap, kf.ap, [kf.key], [ki.key])
                b.copy("dve", kf.ap, ki.ap, [ki.key], [kf.key])
                b.stt(rr.ap, kf.ap, -C1, ang.ap, ALU.mult, ALU.add, [kf.key, ang.key], [rr.key])
                b.stt(rr.ap, kf.ap, -C2, rr.ap, ALU.mult, ALU.add, [kf.key, rr.key], [rr.key])
                if shift:
                    b.ts("dve", rr.ap, rr.ap, shift, None, ALU.add, None, [rr.key], [rr.key])
                b.ts("dve", msk.ap, rr.ap, math.pi, -2 * math.pi, ALU.is_gt, ALU.mult, [rr.key], [msk.key])
                b.tt("dve", rr.ap, rr.ap, msk.ap, ALU.add, [rr.key, msk.key], [rr.key])
                b.ts("dve", msk.ap, rr.ap, -math.pi, 2 * math.pi, ALU.is_lt, ALU.mult, [rr.key], [msk.key])
                b.tt("dve", rr.ap, rr.ap, msk.ap, ALU.add, [rr.key, msk.key], [rr.key])
                b.ts("dve", rr.ap, rr.ap, math.pi, -math.pi, ALU.min, ALU.max, [rr.key], [rr.key])
                b.act(msk.ap, rr.ap, AF.Sin, [rr.key], [msk.key])
                b.dma("sp", rope_d[s, which][:, qsl], msk.ap, [msk.key], [("rope", s)])

    for l in range(NL):
        phase_begin()
        sk = b.alloc("sk", [1, 1024], F32)
        b.dma("sp", sk.ap, sinks_d[l], [], [sk.key])
        b.act(sk.ap, sk.ap, AF.Exp, [sk.key], [sk.key])
        onesrow = b.alloc("onesrow", [1, 128], F32)
        S.op("dve", lambda v, o=onesrow: v.memset(o.ap, 1.0), [], [onesrow.key])
        for hh in range(2):
            p = b.psum()
            b.mm(p.ap, onesrow.ap, sk.ap[0:1, hh * 512:(hh + 1) * 512], True, True, [sk.key, onesrow.key], [p.key])
            b.evac(esinkB[:, hh * 512:(hh + 1) * 512], p.ap, [p.key], [("esinkB",)])

        for s in range(NS):
            phase_begin()
            xt = b.alloc("xt", [128, 16, TT], F32)
            for tt in range(NTT):
                for hf in range(2):
                    b.dma("sp", xt.ap[:, hf * 8:(hf + 1) * 8, :], xT_d[s, :, hf * 8:(hf + 1) * 8, tt * TT:(tt + 1) * TT],
                          [("xT", s, tt)], [xt.key])
                b.pnorm([(xt.ap[:, c, :], xt.key) for c in range(16)], [vcol(b, l, c) for c in range(16)],
                        [(hT[tt].ap[:, c, :], hT[tt].key) for c in range(16)], D, extra_reads=[K_VEC])
            stg = b.ring("stg", [128, TT], BF16, 4)
            stf = b.ring("stf", [128, TT], F32, 3)
            si = [0, 0]
            blocks = [(0, 512, "qa", 0), (512, 512, "qa", 4), (1024, 256, "ka", 0), (1536, 512, "qb", 0),
                      (2048, 512, "kb", 0), (3072, 512, "cq", 0), (3584, 512, "ckv", 0), (4096, 64, "kr", 0)]
            for (c0, ncols, kind, hbase) in blocks:
                wv, wk = b.wload(win_d[l][:, c0:c0 + ncols], ncols)
                for j in range(max(1, ncols // 128)):
                    M = min(128, ncols)
                    for tt in range(NTT):
                        p = b.psum()
                        for k in range(16):
                            b.mm(p.ap[0:M, :], wv[:, k, j * 128:j * 128 + M], hT[tt].ap[:, k, :], k == 0, k == 15,
                                 [wk, hT[tt].key], [p.key])
                        tsl = slice(tt * TT, (tt + 1) * TT)
                        if kind in ("qa", "ka"):
                            o = stg[si[0] % 4]
                            si[0] += 1
                            b.pnorm([(p.ap, p.key)], [vcol(b, l, 64 if kind == "qa" else 65)], [(o.ap, o.key)], 128,
                                    extra_reads=[K_VEC])
                            dst = QA_d[hbase + j] if kind == "qa" else KA_d[j]
                            b.dma("sp", dst[:, tsl], o.ap, [o.key], [(kind, hbase + j, tt)])
                        elif kind in ("qb", "kb"):
                            o = stg[si[0] % 4]
                            si[0] += 1
                            if kind == "qb":
                                b.act(o.ap, p.ap, AF.Copy, [p.key], [o.key], scale=128 ** -0.5)
                            else:
                                b.evac(o.ap, p.ap, [p.key], [o.key])
                            dst = QB_d[j] if kind == "qb" else KB_d[j]
                            b.dma("sp", dst[:, tsl], o.ap, [o.key], [(kind, j, tt)])
                        else:
                            o = stf[si[1] % 3]
                            si[1] += 1
                            b.evac(o.ap[0:M, :], p.ap[0:M, :], [p.key], [o.key])
                            dst = {"cq": CQ_d, "ckv": CKV_d}.get(kind)
                            if kind == "kr":
                                b.dma("sp", KRr_d[:, tsl], o.ap[0:64, :], [o.key], [("krr", tt)])
                            else:
                                b.dma("sp", dst[j][:, tsl], o.ap, [o.key], [(kind, j, tt)])
            for (c0, ncols, dst, nm) in ((1280, 256, VA_d, "va"), (2560, 512, VB_d, "vb")):
                wv, wk = b.wload(win_d[l][:, c0:c0 + ncols], ncols)
                for tc in range(16):
                    p = b.psum()
                    for k in range(16):
                        b.mm(p.ap[:, 0:ncols], hT[tc // 4].ap[:, k, (tc % 4) * 128:(tc % 4 + 1) * 128], wv[:, k, :],
                             k == 0, k == 15, [wk, hT[tc // 4].key], [p.key])
                    o = stg[si[0] % 4]
                    si[0] += 1
                    b.evac(o.ap[:, 0:ncols], p.ap[:, 0:ncols], [p.key], [o.key])
                    b.dma("sp", dst[tc * 128:(tc + 1) * 128, :], o.ap[:, 0:ncols], [o.key], [(nm, tc // 4)])

            phase_begin(nsq=9, nrs=3)
            wuq = b.alloc("wuq", [128, 4, 768], BF16)
            wukv = b.alloc("wukv", [128, 4, 1024], BF16)
            b.dma("pool", wuq.ap, wuq_d[l].rearrange("(c p) n -> p c n", p=128), [], [wuq.key])
            b.dma("pool", wukv.ap, wukv_d[l].rearrange("(c p) n -> p c n", p=128), [], [wukv.key])
            cqr = b.alloc("cqr", [128, 4, TT], F32)
            ckvr = b.alloc("ckvr", [128, 4, TT], F32)
            krr = b.alloc("krr", [64, TT], F32)
            cs = b.alloc("cs", [64, 2, TT], F32)
            cqn = b.alloc("cqn", [128, 4, TT], BF16)
            ckvn = b.alloc("ckvn", [128, 4, TT], BF16)
            stg = b.ring("stg", [128, TT], BF16, 5)
            rn = b.ring("rn", [64, TT], F32, 3)
            r1 = b.ring("r1", [64, TT], F32, 2)
            sic = [0, 0, 0]

            def rope_post(src, dst_ap, dkey):
                def f():
                    p = b.psum_rot(0, 4)
                    b.mm(p.ap[0:64, :], rrot_f, src.ap, True, True, [src.key, K_CST], [p.key])
                    t1 = r1[sic[1] % 2]
                    sic[1] += 1
                    b.tt("dve", t1.ap, src.ap, cs.ap[:, 0, :], ALU.mult, [src.key, cs.key], [t1.key])
                    b.tt("dve", src.ap, p.ap[0:64, :], cs.ap[:, 1, :], ALU.mult, [p.key, cs.key, src.key], [src.key])
                    o = stg[sic[0] % 5]
                    sic[0] += 1
                    b.tt("dve", o.ap[0:64, :], t1.ap, src.ap, ALU.add, [t1.key, src.key], [o.key])
                    b.dma("sp", dst_ap, o.ap[0:64, :], [o.key], [dkey])
                return f

            for tt in range(NTT):
                tsl = slice(tt * TT, (tt + 1) * TT)
                b.dma("sp", cqr.ap, CQ_d[:, :, tsl].rearrange("c p t -> p c t"), [("cq", c, tt) for c in range(4)], [cqr.key])
                b.dma("sp", ckvr.ap, CKV_d[:, :, tsl].rearrange("c p t -> p c t"), [("ckv", c, tt) for c in range(4)], [ckvr.key])
                b.dma("sp", krr.ap, KRr_d[:, tsl], [("krr", tt)], [krr.key])
                b.dma("sp", cs.ap, rope_d[s, :, :, tsl].rearrange("w p t -> p w t"), [("rope", s)], [cs.key])
                units = []
                units.append(b.norm_unit(None, lambda p: [(cqr.ap[:, c, :], cqr.key) for c in range(4)],
                                         [vcol(b, l, 68 + c) for c in range(4)], [(cqn.ap[:, c, :], cqn.key) for c in range(4)], 512,
                                         extra_reads=[K_VEC]))
                units.append(b.norm_unit(None, lambda p: [(ckvr.ap[:, c, :], ckvr.key) for c in range(4)],
                                         [vcol(b, l, 72 + c) for c in range(4)], [(ckvn.ap[:, c, :], ckvn.key) for c in range(4)], 512,
                                         extra_reads=[K_VEC]))
                qk = rn[2]
                units.append(b.norm_unit(None, lambda p: [(krr.ap, krr.key)], [vcol(b, l, 79)[0:64, :]], [(qk.ap, qk.key)], 64, P=64,
                                         extra_reads=[K_VEC], post=rope_post(qk, KR_d[:, tsl], ("kr", tt))))
                b.pipeline(units)
                units = []
                for h in range(4):
                    def mm_qn(p, h=h):
                        for c in range(4):
                            b.mm(p.ap, wuq.ap[:, c, h * 192:h * 192 + 128], cqn.ap[:, c, :], c == 0, c == 3, [wuq.key, cqn.key], [p.key])

                    def mm_qr(p, h=h):
                        for c in range(4):
                            b.mm(p.ap[0:64, :], wuq.ap[:, c, h * 192 + 128:h * 192 + 192], cqn.ap[:, c, :], c == 0, c == 3,
                                 [wuq.key, cqn.key], [p.key])

                    def mm_kn(p, h=h):
                        for c in range(4):
                            b.mm(p.ap, wukv.ap[:, c, h * 256:h * 256 + 128], ckvn.ap[:, c, :], c == 0, c == 3, [wukv.key, ckvn.key], [p.key])

                    o1 = stg[sic[0] % 5]
                    sic[0] += 1
                    units.append(b.norm_unit(mm_qn, lambda p: [(p.ap, p.key)], [vcol(b, l, 76)], [(o1.ap, o1.key)], 128, extra_reads=[K_VEC],
                                             post=(lambda o1=o1, h=h: b.dma("sp", QN_d[h][:, tsl], o1.ap, [o1.key], [("qn", h, tt)]))))
                    q = rn[h % 2]
                    units.append(b.norm_unit(mm_qr, lambda p: [(p.ap[0:64, :], p.key)], [vcol(b, l, 77)[0:64, :]], [(q.ap, q.key)], 64, P=64,
                                             extra_reads=[K_VEC], post=rope_post(q, QR_d[h][:, tsl], ("qr", h, tt))))
                    o2 = stg[sic[0] % 5]
                    sic[0] += 1
                    units.append(b.norm_unit(mm_kn, lambda p: [(p.ap, p.key)], [vcol(b, l, 78)], [(o2.ap, o2.key)], 128, extra_reads=[K_VEC],
                                             post=(lambda o2=o2, h=h: b.dma("sp", KN_d[h][:, tsl], o2.ap, [o2.key], [("kn", h, tt)]))))
                for tq in range(4):
                    def vunit(tq=tq):
                        p = b.psum_rot(0, 4)
                        for c in range(4):
                            rhs = wukv.ap[:, c, :].rearrange("p (h two d) -> p h two d", h=4, two=2)[:, :, 1, :]
                            b.mm(p.ap.rearrange("p (h d) -> p h d", h=4), ckvn.ap[:, c, tq * 128:(tq + 1) * 128], rhs, c == 0, c == 3,
                                 [wukv.key, ckvn.key], [p.key])
                        o = stg[sic[0] % 5]
                        sic[0] += 1
                        b.evac(o.ap, p.ap, [p.key], [o.key])
                        r0 = (tt * 4 + tq) * 128
                        b.dma("sp", VC_d[r0:r0 + 128, :], o.ap, [o.key], [("vc", tt)])
                    units.append([vunit])
                b.pipeline(units)

            phase_begin()
            Qt = b.ring("Qt", [128, 8, TT], BF16, 2)
            Kt = b.ring("Kt", [128, 2, 640], BF16, 2)
            Vt = b.ring("Vt", [128, 5, 256], BF16, 2)
            OAt = b.ring("OAt", [128, 8, TT], BF16, 2)
            tf = b.ring("tf", [128, TT], F32, 4)
            Pt = b.ring("Pt", [128, TT], BF16, 6)
            rd = b.ring("rd", [128, TT], F32, 2)
            cic = [0, 0]
            scale_a = 128 ** -0.5
            units = []
            for tt in range(NTT):
                for bl in range(4):
                    for kvh in range(2):
                        st = {}

                        def s0(tt=tt, bl=bl, kvh=kvh, st=st):
                            tsl = slice(tt * TT, (tt + 1) * TT)
                            qt, kt, vt, oa = Qt[tt % 2], Kt[tt % 2], Vt[tt % 2], OAt[tt % 2]
                            t0 = tt * TT - 128 if tt > 0 else 0
                            ln = (tt + 1) * TT - t0
                            if bl == 0 and kvh == 0:
                                b.dma("sp", qt.ap, QA_d[:, :, tsl].rearrange("h p t -> p h t"), [("qa", h, tt) for h in range(8)], [qt.key])
                                b.dma("sp", kt.ap[:, :, 0:ln], KA_d[:, :, t0:t0 + ln].rearrange("h p t -> p h t"),
                                      [("ka", h, t_) for h in range(2) for t_ in range(max(0, tt - 1), tt + 1)], [kt.key])
                                b.dma("sp", vt.ap[:, 0:ln // 128, :], VA_d[t0:t0 + ln, :].rearrange("(c p) v -> p c v", p=128),
                                      [("va", t_) for t_ in range(max(0, tt - 1), tt + 1)], [vt.key])
                            gb = tt * 4 + bl
                            cur = gb * 128 - t0
                            chunks = ([] if gb == 0 else [(0, cur - 128)]) + [(1, cur)]
                            pts = []
                            for (c, off) in chunks:
                                sp_ = b.psum_rot(0, 4)
                                b.mm(sp_.ap.rearrange("p (h i) -> p h i", h=4), kt.ap[:, kvh, off:off + 128],
                                     qt.ap[:, kvh * 4:(kvh + 1) * 4, bl * 128:(bl + 1) * 128], True, True, [kt.key, qt.key], [sp_.key])
                                t_ = tf[cic[0] % 4]
                                pt = Pt[cic[0] % 6]
                                cic[0] += 1
                                b.stt(t_.ap.rearrange("p (h i) -> p h i", h=4), sp_.ap.rearrange("p (h i) -> p h i", h=4), scale_a,
                                      biasM[:, c, kvh * 4:(kvh + 1) * 4, :], ALU.mult, ALU.add, [sp_.key, ("biasM",)], [t_.key])
                                b.act(pt.ap, t_.ap, AF.Exp, [t_.key], [pt.key])
                                pts.append((pt, off))
                            st["pts"] = pts

                        def s1(tt=tt, bl=bl, kvh=kvh, st=st):
                            tsl = slice(tt * TT, (tt + 1) * TT)
                            vt, oa = Vt[tt % 2], OAt[tt % 2]
                            pts = st["pts"]
                            po = b.psum_rot(4, 2)
                            pd = b.psum_rot(6, 2)
                            for i_, (pt, off) in enumerate(pts):
                                b.mm(po.ap, vt.ap[:, off // 128, kvh * 128:(kvh + 1) * 128], pt.ap, i_ == 0, i_ == len(pts) - 1,
                                     [vt.key, pt.key], [po.key])
                            for i_, (pt, off) in enumerate(pts):
                                b.mm(pd.ap, b.ones_bf, pt.ap, i_ == 0, i_ == len(pts) - 1, [pt.key, K_CBF], [pd.key])
                            r_ = rd[cic[1] % 2]
                            cic[1] += 1
                            b.tt("dve", r_.ap, pd.ap, esinkB[:, kvh * 512:(kvh + 1) * 512], ALU.add, [pd.key, ("esinkB",)], [r_.key])
                            b.recip(r_.ap, r_.ap, [r_.key], [r_.key])
                            b.tt("dve", oa.ap[:, kvh * 4:(kvh + 1) * 4, bl * 128:(bl + 1) * 128],
                                 po.ap.rearrange("p (h i) -> p h i", h=4), r_.ap.rearrange("p (h i) -> p h i", h=4), ALU.mult,
                                 [po.key, r_.key], [oa.key])
                            if bl == 3 and kvh == 1:
                                b.dma("sp", OT_d[0:8, :, tsl].rearrange("h p t -> p h t"), oa.ap, [oa.key], [("ot", c, tt) for c in range(8)])

                        units.append([s0, s1])
            b.pipeline(units)

            phase_begin()
            Kb = b.ring("Kb", [128, SEQ], BF16, 2)
            Vb = b.ring("Vb", [128, 16, 128], BF16, 2)
            Qb = b.ring("Qb", [128, TT], BF16, 2)
            SPr = [b.ring(f"SP{i}_", [128, TT], BF16, 16) for i in range(2)]
            ef = b.ring("ef", [128, TT], F32, 3)
            Wt = b.ring("Wt", [128, TT], BF16, 5)
            ob = b.ring("ob", [128, TT], BF16, 2)
            wic = [0, 0]
            items = [(h, qt) for h in range(4) for qt in range(NTT)]
            Aunits = []
            Bunits = []
            for it, (h, qt) in enumerate(items):
                kb_, vb_ = Kb[h % 2], Vb[h % 2]
                qb_ = Qb[it % 2]
                SP = SPr[it % 2]
                nk = 4 * qt + 4
                c0s = [max(0, (kc - 4 * qt) * 128) for kc in range(nk)]
                A = []
                for kc in range(nk):
                    def a0(h=h, qt=qt, kc=kc, kb_=kb_, vb_=vb_, qb_=qb_, SP=SP, c0s=c0s):
                        if kc == 0:
                            if qt == 0:
                                b.dma("sp", kb_.ap, KB_d[h], [("kb", h, t_) for t_ in range(4)], [kb_.key])
                                b.dma("sp", vb_.ap, VB_d[:, h * 128:(h + 1) * 128].rearrange("(c p) v -> p c v", p=128),
                                      [("vb", t_) for t_ in range(4)], [vb_.key])
                            b.dma("sp", qb_.ap, QB_d[h][:, qt * TT:(qt + 1) * TT], [("qb", h, qt)], [qb_.key])
                        c0 = c0s[kc]
                        z = b.psum_rot(5, 3)
                        b.mm(z.ap[:, c0:], kb_.ap[:, kc * 128:(kc + 1) * 128], qb_.ap[:, c0:], True, True, [kb_.key, qb_.key], [z.key])
                        e = ef[wic[1] % 3]
                        wic[1] += 1
                        sp_ = SP[kc]
                        b.act(e.ap[:, c0:], z.ap[:, c0:], AF.Exp, [z.key], [e.key])
                        b.act(sp_.ap[:, c0:], e.ap[:, c0:], AF.Ln, [e.key], [sp_.key], bias=b.one_col)
                        if kc >= 4 * qt:
                            b.tt("pool", sp_.ap[:, c0:c0 + 128], sp_.ap[:, c0:c0 + 128], tri_strict_bf, ALU.mult,
                                 [sp_.key, K_CBF], [sp_.key])
                    A.append([a0])
                Bq = []
                po_i = it % 2
                order = list(range(nk - 1, -1, -1))
                for idx, kc in enumerate(order):
                    st = {}

                    def b0(h=h, qt=qt, kc=kc, kb_=kb_, qb_=qb_, SP=SP, c0s=c0s, nk=nk, st=st):
                        c0 = c0s[kc]
                        a = b.psum_rot(2, 3)
                        b.mm(a.ap[:, c0:], kb_.ap[:, kc * 128:(kc + 1) * 128], qb_.ap[:, c0:], True, False, [kb_.key, qb_.key], [a.key])
                        later = list(range(kc + 1, nk))
                        b.mm(a.ap[:, c0:], uineg_bf, SP[kc].ap[:, c0:], False, len(later) == 0, [SP[kc].key, K_CBF], [a.key])
                        for li, k2 in enumerate(later):
                            c2 = max(c0, c0s[k2])
                            b.mm(a.ap[:, c2:], onesneg[:, :], SP[k2].ap[:, c2:], False, li == len(later) - 1,
                                 [SP[k2].key, K_CBF], [a.key])
                        w = Wt[wic[0] % 5]
                        wic[0] += 1
                        b.act(w.ap[:, c0:], a.ap[:, c0:], AF.Exp, [a.key], [w.key])
                        if kc >= 4 * qt:
                            b.tt("pool", w.ap[:, c0:c0 + 128], w.ap[:, c0:c0 + 128], tri_strict_bf, ALU.mult, [w.key, K_CBF], [w.key])
                        st["w"] = w

                    def b2(h=h, qt=qt, kc=kc, vb_=vb_, c0s=c0s, idx=idx, nk=nk, st=st, po_i=po_i, it=it):
                        c0 = c0s[kc]
                        w = st["w"]
                        po = b.psum_at(po_i)
                        b.mm(po.ap[:, c0:], vb_.ap[:, kc, :], w.ap[:, c0:], idx == 0, idx == nk - 1, [vb_.key, w.key], [po.key])
                        if idx == nk - 1:
                            o = ob[it % 2]
                            b.evac(o.ap, po.ap, [po.key], [o.key])
                            b.dma("sp", OT_d[8 + h][:, qt * TT:(qt + 1) * TT], o.ap, [o.key], [("ot", 8 + h, qt)])

                    Bq.append([b0, None, b2])
                Aunits.append(A)
                Bunits.append(Bq)
            allu = list(Aunits[0])
            for it in range(len(items)):
                Bq = Bunits[it]
                An = Aunits[it + 1] if it + 1 < len(items) else []
                i1 = i2 = 0
                while i1 < len(Bq) or i2 < len(An):
                    if i1 < len(Bq):
                        allu.append(Bq[i1])
                        i1 += 1
                    if i2 < len(An):
                        allu.append(An[i2])
                        i2 += 1
            b.pipeline(allu)

            phase_begin()
            Kn = b.ring("Kn", [128, SEQ], BF16, 2)
            Kr = b.alloc("Kr", [64, SEQ], BF16)
            Vc = b.ring("Vc", [128, 16, 128], BF16, 2)
            Qn = b.ring("Qn", [128, TT], BF16, 2)
            Qr = b.ring("Qr", [64, TT], BF16, 2)
            Pm = b.ring("Pm", [128, TT], BF16, 6)
            rdm = b.ring("rdm", [128, TT], F32, 2)
            om = b.ring("om", [128, TT], BF16, 2)
            b.dma("sp", Kr.ap, KR_d, [("kr", t_) for t_ in range(4)], [Kr.key])
            scale_c = 192 ** -0.5
            pic = [0]
            units = []
            for it, (h, qt) in enumerate([(h, qt) for h in range(4) for qt in range(NTT)]):
                kn_, vc_ = Kn[h % 2], Vc[h % 2]
                qn_, qr_ = Qn[it % 2], Qr[it % 2]
                nk = 4 * qt + 4
                for kc in range(nk):
                    st = {}

                    def m0(h=h, qt=qt, kc=kc, kn_=kn_, vc_=vc_, qn_=qn_, qr_=qr_, st=st):
                        if kc == 0:
                            if qt == 0:
                                b.dma("sp", kn_.ap, KN_d[h], [("kn", h, t_) for t_ in range(4)], [kn_.key])
                                b.dma("sp", vc_.ap, VC_d[:, h * 128:(h + 1) * 128].rearrange("(c p) v -> p c v", p=128),
                                      [("vc", t_) for t_ in range(4)], [vc_.key])
                            b.dma("sp", qn_.ap, QN_d[h][:, qt * TT:(qt + 1) * TT], [("qn", h, qt)], [qn_.key])
                            b.dma("sp", qr_.ap, QR_d[h][:, qt * TT:(qt + 1) * TT], [("qr", h, qt)], [qr_.key])
                        c0 = max(0, (kc - 4 * qt) * 128)
                        sc = b.psum_rot(4, 4)
                        b.mm(sc.ap[:, c0:], kn_.ap[:, kc * 128:(kc + 1) * 128], qn_.ap[:, c0:], True, False, [kn_.key, qn_.key], [sc.key])
                        b.mm(sc.ap[:, c0:], Kr.ap[:, kc * 128:(kc + 1) * 128], qr_.ap[:, c0:], False, True, [Kr.key, qr_.key], [sc.key])
                        pm = Pm[pic[0] % 6]
                        pic[0] += 1
                        b.act(pm.ap[:, c0:], sc.ap[:, c0:], AF.Exp, [sc.key], [pm.key], scale=scale_c)
                        if kc >= 4 * qt:
                            b.tt("pool", pm.ap[:, c0:c0 + 128], pm.ap[:, c0:c0 + 128], tri_incl_bf, ALU.mult, [pm.key, K_CBF], [pm.key])
                        st["pm"] = pm

                    def m2(h=h, qt=qt, kc=kc, vc_=vc_, nk=nk, st=st, it=it):
                        c0 = max(0, (kc - 4 * qt) * 128)
                        pm = st["pm"]
                        po = b.psum_at(it % 2)
                        pd = b.psum_at(2 + it % 2)
                        b.mm(po.ap[:, c0:], vc_.ap[:, kc, :], pm.ap[:, c0:], kc == 0, kc == nk - 1, [vc_.key, pm.key], [po.key])
                        b.mm(pd.ap[:, c0:], b.ones_bf, pm.ap[:, c0:], kc == 0, kc == nk - 1, [pm.key, K_CBF], [pd.key])
                        if kc == nk - 1:
                            r_ = rdm[it % 2]
                            b.recip(r_.ap, pd.ap, [pd.key], [r_.key])
                            o = om[it % 2]
                            b.tt("dve", o.ap, po.ap, r_.ap, ALU.mult, [po.key, r_.key], [o.key])
                            b.dma("sp", OT_d[12 + h][:, qt * TT:(qt + 1) * TT], o.ap, [o.key], [("ot", 12 + h, qt)])

                    units.append([m0, None, m2])
            b.pipeline(units)

            phase_begin()
            OTt = b.alloc("OTt", [128, 16, TT], BF16)
            mg = b.alloc("mg", [128, 16, TT], BF16)
            gt = b.ring("gt", [128, TT], F32, 4)
            tm = b.ring("tm", [128, TT], F32, 4)
            xp = b.ring("xp", [128, 4, TT], F32, 2)
            gi = 0
            def otbuf(tt):
                return OTt if tt == 0 else Tile(hT[tt - 1].ap, hT[tt - 1].key)

            def otload(tt):
                ot = otbuf(tt)
                b.dma("sp", ot.ap, OT_d[:, :, tt * TT:(tt + 1) * TT].rearrange("c p t -> p c t"), [("ot", c, tt) for c in range(16)], [ot.key])

            otload(0)
            for tt in range(NTT):
                tsl = slice(tt * TT, (tt + 1) * TT)
                OTc = otbuf(tt)
                for n in range(16):
                    wv, wk = b.wload(wpc_d[l, n], 512)
                    gts = []
                    for g3 in range(3):
                        p = b.psum()
                        for k in range(16):
                            b.mm(p.ap, wv[:, k, g3 * 128:(g3 + 1) * 128], hT[tt].ap[:, k, :], k == 0, k == 15, [wk, hT[tt].key], [p.key])
                        g_ = gt[gi % 4]
                        gi += 1
                        b.act(g_.ap, p.ap, AF.Sigmoid, [p.key], [g_.key])
                        gts.append(g_)
                    kr_ = [(0, 8), (8, 12), (12, 16)]
                    tms = []
                    for g3 in range(3):
                        p = b.psum()
                        k0, k1 = kr_[g3]
                        for k in range(k0, k1):
                            b.mm(p.ap, wv[:, k, 384:512], OTc.ap[:, k, :], k == k0, k == k1 - 1, [wk, OTc.key], [p.key])
                        if g3 < 2:
                            t_ = tm[(n * 2 + g3) % 4]
                            b.tt("dve", t_.ap, p.ap, gts[g3].ap, ALU.mult, [p.key, gts[g3].key], [t_.key])
                            tms.append(t_)
                        else:
                            b.tt("dve", gts[2].ap, p.ap, gts[2].ap, ALU.mult, [p.key, gts[2].key], [gts[2].key])
                    b.tt("dve", tms[0].ap, tms[0].ap, tms[1].ap, ALU.add, [tms[0].key, tms[1].key], [tms[0].key])
                    b.tt("dve", mg.ap[:, n, :], tms[0].ap, gts[2].ap, ALU.add, [tms[0].key, gts[2].key], [mg.key])
                if tt + 1 < NTT:
                    otload(tt + 1)
                for nb in range(4):
                    wv, wk = b.wload(wout_d[l][:, nb * 512:(nb + 1) * 512], 512)
                    xq = xp[nb % 2]
                    b.dma("sp", xq.ap, xT_d[s, :, nb * 4:(nb + 1) * 4, tsl], [("xT", s, tt)], [xq.key])
                    for j in range(4):
                        p = b.psum()
                        for k in range(16):
                            b.mm(p.ap, wv[:, k, j * 128:(j + 1) * 128], mg.ap[:, k, :], k == 0, k == 15, [wk, mg.key], [p.key])
                        b.tt("dve", xq.ap[:, j, :], xq.ap[:, j, :], p.ap, ALU.add, [xq.key, p.key], [xq.key])
                    b.dma("sp", xT_d[s, :, nb * 4:(nb + 1) * 4, tsl], xq.ap, [xq.key], [("xT", s, tt)])

            phase_begin(nsq=4, nrs=2)
            if debug and l == 0 and s == 0:
                b.dma("sp", dbg1_d, xT_d[s], [("xT", s, t_) for t_ in range(4)], [("dbg1",)])
            hx = [Tile(hT[i].ap, ("hx", i)) for i in range(4)]
            b.ps_limit = 7
            memn = b.alloc("memn", [128, 16, 256], BF16)
            Km = b.alloc("Km", [128, 4, 256], BF16)
            Vm = b.alloc("Vm", [128, 2, 512], BF16)
            Qx = b.ring("Qx", [128, 4, TT], BF16, 2)
            Ox = b.ring("Ox", [128, 4, TT], BF16, 2)
            Px = b.ring("Px", [128, TT], BF16, 2)
            nsqr = b.ring("nsq", [128, TT], BF16, 8)
            nsqc = [0]
            rdx = b.ring("rdx", [128, TT], F32, 2)
            xp = b.ring("xp", [128, 4, TT], F32, 3)
            rsn = b.alloc("rsn", [128, TT], F32)
            xpc = [0]
            normq = []
            for t_ in range(NTT):
                for f in norm_steps(b, xT_d[s], slice(t_ * TT, (t_ + 1) * TT), ("xT", s, t_), [vcol(b, l, 16 + c) for c in range(16)],
                                    hx[t_], xp, xpc, rsn, K_VEC, nsqr, nsqc):
                    normq.append((t_, f))

            def npop(k):
                for _ in range(k):
                    if normq:
                        normq.pop(0)[1]()

            def nensure(t_):
                while normq and normq[0][0] <= t_:
                    normq.pop(0)[1]()

            for g in range(4):
                m_ = Tile(xp[g % 2].ap[:, :, 0:256], xp[g % 2].key)
                b.dma("sp", m_.ap, memT_d[s, :, g * 4:(g + 1) * 4, :], [("memT", s)], [m_.key])
                for j in range(4):
                    c = g * 4 + j
                    b.ts("dve", memn.ap[:, c, :], m_.ap[:, j, :], vcol(b, l, 32 + c), None, ALU.mult, None, [m_.key, K_VEC], [memn.key])
            for blk in range(2):
                wv, wk = b.wload(xwkv_d[l][:, blk * 512:(blk + 1) * 512], 512)
                for hh in range(2):
                    h = blk * 2 + hh
                    p = b.psum()
                    for k in range(16):
                        b.mm(p.ap[:, 0:256], wv[:, k, hh * 256:hh * 256 + 128], memn.ap[:, k, :], k == 0, k == 15, [wk, memn.key], [p.key])
                    b.pnorm([(p.ap[:, 0:256], p.key)], [vcol(b, l, 67)], [(Km.ap[:, h, :], Km.key)], 128, N=256, extra_reads=[K_VEC])
                    npop(3)
                for mc in range(2):
                    p = b.psum()
                    for k in range(16):
                        rhs = wv[:, k, :].rearrange("p (h two d) -> p h two d", h=2, two=2)[:, :, 1, :]
                        b.mm(p.ap[:, 0:256].rearrange("p (h d) -> p h d", h=2), memn.ap[:, k, mc * 128:(mc + 1) * 128], rhs,
                             k == 0, k == 15, [wk, memn.key], [p.key])
                    b.evac(Vm.ap[:, mc, blk * 256:(blk + 1) * 256], p.ap[:, 0:256], [p.key], [Vm.key])
                    npop(3)
            wqv, wqk = b.wload(xwq_d[l], 512)
            woi = b.w_i % len(b.wb)
            b.w_i += 1
            wot = b.wb[woi]
            wov = wot.ap.rearrange("p (c n) -> p c n", c=4)
            for q4 in range(4):
                b.dma("pool", wov[:, :, q4 * 512:(q4 + 1) * 512],
                      xwo_d[l][:, q4 * 512:(q4 + 1) * 512].rearrange("(c p) n -> p c n", p=128), [], [wot.key])
            scale_x = 128 ** -0.5
            pxc = [0]
            units = []
            for tt in range(NTT):
                def st1(tt=tt):
                    nensure(tt)
                    hxx = hx[tt]
                    qx = Qx[tt % 2]
                    us = []
                    for h in range(4):
                        def mmq(p, h=h):
                            for k in range(16):
                                b.mm(p.ap, wqv[:, k, h * 128:(h + 1) * 128], hxx.ap[:, k, :], k == 0, k == 15, [wqk, hxx.key], [p.key])
                        us.append(b.norm_unit(mmq, lambda p: [(p.ap, p.key)], [vcol(b, l, 66)], [(qx.ap[:, h, :], qx.key)], 128,
                                              extra_reads=[K_VEC], pslo=0, psn=3, sslo=3, ssn=2))
                    b.pipeline(us)
                    npop(3)

                def st2(tt=tt):
                    qx = Qx[tt % 2]
                    ox = Ox[tt % 2]
                    for h in range(4):
                        pts = []
                        for mc in range(2):
                            sc = b.psum_rot(0, 7)
                            b.mm(sc.ap, Km.ap[:, h, mc * 128:(mc + 1) * 128], qx.ap[:, h, :], True, True, [Km.key, qx.key], [sc.key])
                            px = Px[pxc[0] % 2]
                            pxc[0] += 1
                            b.act(px.ap, sc.ap, AF.Exp, [sc.key], [px.key], scale=scale_x)
                            pts.append(px)
                        po = b.psum_rot(0, 7)
                        pd = b.psum_rot(0, 7)
                        for mc in range(2):
                            b.mm(po.ap, Vm.ap[:, mc, h * 128:(h + 1) * 128], pts[mc].ap, mc == 0, mc == 1, [Vm.key, pts[mc].key], [po.key])
                        for mc in range(2):
                            b.mm(pd.ap, b.ones_bf, pts[mc].ap, mc == 0, mc == 1, [pts[mc].key, K_CBF], [pd.key])
                        r_ = rdx[h % 2]
                        b.recip(r_.ap, pd.ap, [pd.key], [r_.key])
                        b.tt("dve", ox.ap[:, h, :], po.ap, r_.ap, ALU.mult, [po.key, r_.key], [ox.key])

                def st3(tt=tt):
                    tsl = slice(tt * TT, (tt + 1) * TT)
                    ox = Ox[tt % 2]
                    for nb in range(4):
                        xq = xp[xpc[0] % 3]
                        xpc[0] += 1
                        b.dma("sp", xq.ap, xT_d[s, :, nb * 4:(nb + 1) * 4, tsl], [("xT", s, tt)], [xq.key])
                        for j in range(4):
                            n = nb * 4 + j
                            p = b.psum_rot(0, 7)
                            for h in range(4):
                                b.mm(p.ap, wov[:, h, n * 128:(n + 1) * 128], ox.ap[:, h, :], h == 0, h == 3, [wot.key, ox.key], [p.key])
                            b.tt("dve", xq.ap[:, j, :], xq.ap[:, j, :], p.ap, ALU.add, [xq.key, p.key], [xq.key])
                        b.dma("sp", xT_d[s, :, nb * 4:(nb + 1) * 4, tsl], xq.ap, [xq.key], [("xT", s, tt)])

                units.append([st1, st2, st3])
            b.pipeline(units, oldest_first=True)
            b.ps_limit = 8

            phase_begin(nsq=1, nrs=1)
            if debug and l == 0 and s == 0:
                b.dma("sp", dbg2_d, xT_d[s], [("xT", s, t_) for t_ in range(4)], [("dbg2",)])
            b.ps_limit = 7
            hfA = Tile(hT_flat[:, 0:16 * TT].rearrange("p (c t) -> p c t", c=16), ("hfA",))
            actt = Tile(hT_flat[:, 16 * TT:(16 + 44) * TT].rearrange("p (c t) -> p c t", c=44), ("actt",))
            hfB = b.alloc("hfB", [128, 16, TT], BF16)
            hfs = [hfA, hfB]
            xp = b.ring("xp", [128, 4, TT], F32, 3)
            rsn = b.alloc("rsn", [128, TT], F32)
            nsqr = b.ring("nsq", [128, TT], BF16, 8)
            nsqc = [0]
            sg = b.ring("sg", [128, TT], F32, 3)
            xpc = [0]
            sgi = 0

            def pf_norm(tt):
                return norm_steps(b, xT_d[s], slice(tt * TT, (tt + 1) * TT), ("xT", s, tt), [vcol(b, l, 48 + c) for c in range(16)],
                                  hfs[tt % 2], xp, xpc, rsn, K_VEC, nsqr, nsqc)

            for f in pf_norm(0):
                f()
            for tt in range(NTT):
                tsl = slice(tt * TT, (tt + 1) * TT)
                hf_t = hfs[tt % 2]
                nxt = pf_norm(tt + 1) if tt + 1 < NTT else []
                for blk in range(22):
                    wv, wk = b.wload(wgu_d[l, blk], 512)
                    for jj in range(2):
                        j = blk * 2 + jj
                        pg = b.psum_rot(0, 7)
                        pu = b.psum_rot(0, 7)
                        for k in range(16):
                            b.mm(pg.ap, wv[:, k, jj * 128:(jj + 1) * 128], hf_t.ap[:, k, :], k == 0, k == 15, [wk, hf_t.key], [pg.key])
                        for k in range(16):
                            b.mm(pu.ap, wv[:, k, 256 + jj * 128:256 + (jj + 1) * 128], hf_t.ap[:, k, :], k == 0, k == 15, [wk, hf_t.key], [pu.key])
                        s_ = sg[sgi % 3]
                        sgi += 1
                        b.act(s_.ap, pg.ap, AF.Silu, [pg.key], [s_.key])
                        b.tt("dve", actt.ap[:, j, :], s_.ap, pu.ap, ALU.mult, [s_.key, pu.key], [actt.key])
                    if blk >= 4 and blk % 2 == 0 and nxt:
                        nxt.pop(0)()
                for nb in range(4):
                    pss = [b.psum_rot(0, 7) for _ in range(4)]
                    for kp in range(3):
                        kc = 16 if kp < 2 else 12
                        wv, wk = b.wload(wdn_d[l][kp * 2048:kp * 2048 + kc * 128, nb * 512:(nb + 1) * 512], 512, kc=kc)
                        for j in range(4):
                            for k in range(kc):
                                kk = kp * 16 + k
                                b.mm(pss[j].ap, wv[:, k, j * 128:(j + 1) * 128], actt.ap[:, kk, :], kk == 0, kk == 43, [wk, actt.key], [pss[j].key])
                        if nxt:
                            nxt.pop(0)()
                    xq = xp[xpc[0] % 3]
                    xpc[0] += 1
                    b.dma("sp", xq.ap, xT_d[s, :, nb * 4:(nb + 1) * 4, tsl], [("xT", s, tt)], [xq.key])
                    for j in range(4):
                        b.tt("dve", xq.ap[:, j, :], xq.ap[:, j, :], pss[j].ap, ALU.add, [xq.key, pss[j].key], [xq.key])
                    b.dma("sp", xT_d[s, :, nb * 4:(nb + 1) * 4, tsl], xq.ap, [xq.key], [("xT", s, tt)])
                while nxt:
                    nxt.pop(0)()
            b.ps_limit = 8

    phase_begin()
    xi2 = b.ring("xi2", [128, 16, 128], F32, 2)
    xo2 = b.ring("xo2", [128, D], F32, 2)
    for s in range(NS):
        for tc in range(16):
            xi = xi2[tc % 2]
            xo = xo2[tc % 2]
            b.dma("sp", xi.ap, xT_d[s, :, :, tc * 128:(tc + 1) * 128], [("xT", s, tc // 4)], [xi.key])
            for g in range(4):
                p = b.psum()
                for j in range(4):
                    c = g * 4 + j
                    b.tr(p.ap[:, j * 128:(j + 1) * 128], xi.ap[:, c, :], ident_f, [xi.key, K_CST], [p.key])
                b.evac(xo.ap[:, g * 512:(g + 1) * 512], p.ap, [p.key], [xo.key])
            b.dma("sp", out_d[s, tc * 128:(tc + 1) * 128, :], xo.ap, [xo.key], [("out", s, tc)])

    S.emit(nc, sems, dsems)
    es.close()
    return nc, S


def t5_bucket_np(rel):
    n = np.maximum(rel, 0)
    exact = 16
    nf = np.maximum(n, exact).astype(np.float32)
    large = exact + (np.log(nf / np.float32(exact)) / np.float32(math.log(128 / exact)) * np.float32(32 - exact)).astype(np.int32)
    large = np.minimum(large, 31)
    return np.where(n < exact, n, large)


def make_consts():
    NC = 128 * 4 + 256 + 64 + 4
    c = np.zeros((128, NC), np.float32)
    c[:, 0:128] = np.eye(128, dtype=np.float32)
    j = np.arange(128)[:, None]
    k = np.arange(128)[None, :]
    c[:, 128:256] = np.where(j >= k, -1.0, 0.0)
    c[:, 256:384] = np.where(k >= j, 1.0, 0.0)
    c[:, 384:512] = np.where(k > j, 1.0, 0.0)
    c[:, 512:640] = np.where(j > k, 0.0, NEG)
    c[:, 640:768] = np.where(j <= k, 0.0, NEG)
    rr = np.zeros((64, 64), np.float32)
    for m in range(64):
        if m < 32:
            rr[m + 32, m] = -1.0
        else:
            rr[m - 32, m] = 1.0
    c[0:64, 768:832] = rr
    half = 32
    inv = (np.float32(10000.0) ** (-np.arange(half, dtype=np.float32) / np.float32(half))).astype(np.float32)
    c[0:64, 832] = np.concatenate([inv, inv])
    return c


def prep_shared(inp, NL):
    f = lambda a: np.ascontiguousarray(np.asarray(a, dtype=np.float32))
    sh = {}
    rel_bias = f(inp["rel_bias"])
    jj = np.arange(128)[:, None]
    ii = np.arange(128)[None, :]
    bt = np.zeros((128, 2, 8, 128), np.float32)
    for c in range(2):
        rel = 128 + ii - (jj + 128 * c)
        bk = t5_bucket_np(rel)
        bt[:, c, :, :] = np.transpose(rel_bias[bk], (0, 2, 1))
    sh["biasT"] = bt.reshape(128, -1)
    sh["sinks"] = np.ascontiguousarray(np.repeat(f(inp["swa_sinks"])[:NL, None, :, None], 128, axis=3).reshape(NL, 1, 1024))
    vecs = np.zeros((128, NL * NVEC), np.float32)
    for l in range(NL):
        o = l * NVEC
        vecs[:, o:o + 16] = f(inp["norm_mix"])[l].reshape(16, 128).T
        vecs[:, o + 16:o + 32] = f(inp["norm_xa"])[l].reshape(16, 128).T
        vecs[:, o + 32:o + 48] = f(inp["norm_mem"])[l].reshape(16, 128).T
        vecs[:, o + 48:o + 64] = f(inp["norm_ffn"])[l].reshape(16, 128).T
        vecs[:, o + 64] = f(inp["swa_q_norm"])[l]
        vecs[:, o + 65] = f(inp["swa_k_norm"])[l]
        vecs[:, o + 66] = f(inp["xa_q_norm"])[l]
        vecs[:, o + 67] = f(inp["xa_k_norm"])[l]
        vecs[:, o + 68:o + 72] = f(inp["mla_cq_norm"])[l].reshape(4, 128).T
        vecs[:, o + 72:o + 76] = f(inp["mla_ckv_norm"])[l].reshape(4, 128).T
        qg = f(inp["mla_q_norm"])[l]
        kg = f(inp["mla_k_norm"])[l]
        vecs[:, o + 76] = qg[:128]
        vecs[0:64, o + 77] = qg[128:]
        vecs[:, o + 78] = kg[:128]
        vecs[0:64, o + 79] = kg[128:]
    sh["vecs"] = vecs
    sh["consts"] = make_consts()
    w_in = f(inp["w_in"])[:NL]
    sh["w_in"] = np.ascontiguousarray(w_in[:, :, :IN_QKV])
    gates = w_in[:, :, IN_QKV:]
    wbr = f(inp["w_branch"])[:NL]
    wpc = np.empty((NL, 16, D, 512), np.float32)
    for n in range(16):
        for g3 in range(3):
            wpc[:, n, :, g3 * 128:(g3 + 1) * 128] = gates[:, :, g3 * 2048 + n * 128:g3 * 2048 + (n + 1) * 128]
        wpc[:, n, :, 384:512] = wbr[:, :, n * 128:(n + 1) * 128]
    sh["w_pc"] = wpc
    sh["w_out"] = f(inp["w_out"])[:NL]
    sh["w_uq"] = f(inp["mla_w_uq"])[:NL]
    sh["w_ukv"] = f(inp["mla_w_ukv"])[:NL]
    sh["xa_wq"] = f(inp["xa_wq"])[:NL]
    sh["xa_wkv"] = f(inp["xa_wkv"])[:NL]
    sh["xa_wo"] = f(inp["xa_wo"])[:NL]
    wgu = f(inp["ffn_w_gu"])[:NL]
    blk = np.empty((NL, 22, D, 512), np.float32)
    for bi in range(22):
        blk[:, bi, :, 0:256] = wgu[:, :, bi * 256:(bi + 1) * 256]
        blk[:, bi, :, 256:512] = wgu[:, :, D_FF + bi * 256:D_FF + (bi + 1) * 256]
    sh["w_gu"] = blk
    sh["w_down"] = f(inp["ffn_w_down"])[:NL]
    return sh


def prep_core(inp, bsel):
    x = np.ascontiguousarray(np.asarray(inp["x"], np.float32)[bsel])
    mem = np.ascontiguousarray(np.asarray(inp["mem"], np.float32)[bsel])
    pos = np.asarray(inp["positions"]).astype(np.int32)[bsel]
    pos = np.ascontiguousarray(np.repeat(pos[:, None, :], 64, axis=1))
    return {"x": x, "mem": mem, "pos": pos}


_CACHE = {}


def kernel(**inputs):
    n_cores = 8
    NS = 2
    key = (DEPTH, NS)
    if key not in _CACHE:
        _CACHE[key] = build_program(DEPTH, NS)[0]
    nc = _CACHE[key]
    sh = prep_shared(inputs, DEPTH)
    in_maps = []
    for c in range(n_cores):
        m = dict(sh)
        m.update(prep_core(inputs, slice(c * NS, (c + 1) * NS)))
        in_maps.append(m)
    res = run_bass_kernel_spmd(nc, in_maps, core_ids=list(range(n_cores)))
    out = np.concatenate([np.asarray(r["out"]) for r in res.results], axis=0)
    return out.astype(np.float32)
```

```python
import math
from contextlib import ExitStack

import numpy as np
import concourse.bass as bass
import concourse.mybir as mybir
from concourse.bass_utils import run_bass_kernel_spmd

F32 = mybir.dt.float32
BF16 = mybir.dt.bfloat16
I32 = mybir.dt.int32
U8 = mybir.dt.uint8
AF = mybir.ActivationFunctionType
ALU = mybir.AluOpType

ENGS = ("pe", "act", "dve", "pool", "sp")
DMA_RING = 8

D = 2048
SEQ = 2048
NTT = 4
TT = 512
DEPTH = 4
IN_QKV = 4160
D_FF = 5632
EPS = 1e-6
NVEC = 80
NEG = -30000.0


class Sched:
    def __init__(self):
        self.ops = []
        self.strict = set()
        self.nobar = set()

    def op(self, eng, fn, reads=(), writes=(), dma=False, strict=False, nobar=False):
        self.ops.append((eng, fn, tuple(reads), tuple(writes), dma))
        if strict:
            self.strict.add(len(self.ops) - 1)
        if nobar:
            self.nobar.add(len(self.ops) - 1)

    def barrier(self):
        self.ops.append(("bar", None, (), (), False))

    def emit(self, nc, sems, dma_sems):
        ops = self.ops
        n = len(ops)
        last_w = {}
        readers = {}
        deps = [None] * n
        last_on = {}
        last_dma = {}
        dctr = {e: 0 for e in ENGS}
        pending_bar = {}
        for i, (eng, fn, reads, writes, dma) in enumerate(ops):
            if eng == "bar":
                bd = set(last_on.values()) | set(last_dma.values())
                for e in ENGS:
                    pending_bar[e] = bd
                last_w = {k_: v_ for k_, v_ in last_w.items() if k_[0] == "wb"}
                readers = {k_: v_ for k_, v_ in readers.items() if k_[0] == "wb"}
                deps[i] = set()
                continue
            d = set()
            pb = None if i in self.nobar else pending_bar.pop(eng, None)
            if pb:
                d.update(pb)
            for r in reads:
                w = last_w.get(r)
                if w is not None:
                    d.add(w)
            for w_ in writes:
                w = last_w.get(w_)
                if w is not None:
                    d.add(w)
                rs = readers.get(w_)
                if rs:
                    d.update(rs)
            for r in reads:
                rl = readers.setdefault(r, [])
                if not dma:
                    for q in range(len(rl)):
                        if ops[rl[q]][0] == eng and not ops[rl[q]][4]:
                            rl[q] = i
                            break
                    else:
                        rl.append(i)
                else:
                    rl.append(i)
            for w_ in writes:
                last_w[w_] = i
                readers[w_] = []
            d.discard(i)
            deps[i] = d
            if dma:
                last_dma[(eng, dctr[eng] % DMA_RING)] = i
                dctr[eng] += 1
            else:
                last_on[eng] = i
        signal = [False] * n
        for i in range(n):
            ei = ops[i][0]
            if ei == "bar":
                continue
            for j in deps[i]:
                ej, _, _, _, dj = ops[j]
                if dj:
                    continue
                if ej != ei or i in self.strict:
                    signal[j] = True
        tok = [None] * n
        cnt = {e: 0 for e in ENGS}
        dcnt = {e: 0 for e in ENGS}
        dval = {}
        ring_prev = [None] * n
        ring_hist = {e: [] for e in ENGS}
        for i, (eng, fn, reads, writes, dma) in enumerate(ops):
            if eng == "bar":
                continue
            if dma:
                k = dcnt[eng]
                slot = k % DMA_RING
                dcnt[eng] = k + 1
                v = dval.get((eng, slot), 0) + 16
                dval[(eng, slot)] = v
                tok[i] = (("d", eng, slot), v)
                h = ring_hist[eng]
                if len(h) >= DMA_RING:
                    ring_prev[i] = h[-DMA_RING]
                h.append(i)
            elif signal[i]:
                cnt[eng] += 1
                tok[i] = (("e", eng), cnt[eng])
        known = {e: {} for e in ENGS}
        clock_at = {}
        streams = {e: [] for e in ENGS}
        for i, (eng, fn, reads, writes, dma) in enumerate(ops):
            if eng == "bar":
                continue
            kn = known[eng]
            need = {}
            dl = list(deps[i])
            if ring_prev[i] is not None:
                dl.append(ring_prev[i])
            strict_w = []
            for j in dl:
                ej, _, _, _, dj = ops[j]
                if (not dj) and ej == eng:
                    if i in self.strict:
                        strict_w.append(tok[j])
                    continue
                sk, v = tok[j]
                if kn.get(sk, 0) >= v:
                    continue
                if need.get(sk, 0) < v:
                    need[sk] = v
            waits = list(strict_w)
            for sk, v in need.items():
                if kn.get(sk, 0) >= v:
                    continue
                waits.append((sk, v))
                kn[sk] = v
                snap = clock_at.get((sk, v))
                if snap:
                    for a, b in snap.items():
                        if kn.get(a, 0) < b:
                            kn[a] = b
            t = tok[i]
            if t is not None:
                if not dma:
                    kn[t[0]] = t[1]
                clock_at[t] = dict(kn)
            streams[eng].append((waits, fn, t, dma))
        self.n_waits = sum(len(w) for e in ENGS for (w, _, _, _) in streams[e])
        self.n_signal = sum(1 for s in signal if s)
        self.max_tok = dict(cnt)
        final_tokens = dict(dval)

        def semof(sk):
            if sk[0] == "e":
                return sems[sk[1]]
            return dma_sems[(sk[1], sk[2])]

        handles = {"pe": "tensor", "act": "scalar", "dve": "vector", "pool": "gpsimd", "sp": "sync"}
        with nc.Block() as block:
            for e in ENGS:
                st = streams[e]
                final = e == "sp"

                def body(h, st=st, final=final):
                    for waits, fn, t, dma in st:
                        for sk, v in waits:
                            h.wait_ge(semof(sk), v)
                        ins = fn(h)
                        if t is not None:
                            ins.then_inc(semof(t[0]), 16 if dma else 1)
                    if final:
                        for (q, slot), v in final_tokens.items():
                            h.wait_ge(dma_sems[(q, slot)], v)

                if st or final:
                    getattr(block, handles[e])(body)


class Tile:
    __slots__ = ("ap", "key")

    def __init__(self, ap, key):
        self.ap = ap
        self.key = key


class Builder:
    def __init__(self, nc, S, es, NL, NS, debug):
        self.nc, self.S, self.es, self.NL, self.NS, self.debug = nc, S, es, NL, NS, debug
        self.uid = 0
        self.ps_i = 0
        self.rot_ctr = {}
        self.ps_limit = 8
        self.act_dve = 0

    def static(self, name, shape, dt):
        t = self.es.enter_context(self.nc.sbuf_tensor(name, shape, dt))
        return t

    def arena_reset(self):
        self.S.barrier()
        self.aoff = 0

    def alloc(self, name, shape, dt):
        bpe = 4 if dt in (F32, I32) else 2
        per = int(np.prod(shape[1:])) * bpe
        off = (self.aoff + 31) // 32 * 32
        assert off + per <= self.ARENA, (name, off, per, self.ARENA)
        self.aoff = off + per
        ap = self.arena[:, off:off + per].bitcast(dt)
        if len(shape) == 3:
            ap = ap.rearrange("p (a b) -> p a b", a=shape[1])
        elif len(shape) == 4:
            ap = ap.rearrange("p (a b c) -> p a b c", a=shape[1], b=shape[2])
        if shape[0] < 128:
            ap = ap[0:shape[0]]
        self.uid += 1
        return Tile(ap, (name, self.uid))

    def ring(self, name, shape, dt, n):
        return [self.alloc(f"{name}{i}", shape, dt) for i in range(n)]

    def psum(self):
        i = self.ps_i % self.ps_limit
        self.ps_i += 1
        return Tile(self.ps[i][:, :], ("ps", i))

    def psum_at(self, i):
        return Tile(self.ps[i][:, :], ("ps", i))

    def psum_rot(self, lo, n):
        c = self.rot_ctr.get((lo, n), 0)
        self.rot_ctr[(lo, n)] = c + 1
        i = lo + c % n
        return Tile(self.ps[i][:, :], ("ps", i))

    def pipeline(self, units, oldest_first=False):
        n = len(units)
        ns = max(len(u) for u in units)
        for t in range(n + ns - 1):
            for k in (range(ns - 1, -1, -1) if oldest_first else range(ns)):
                u = t - k
                if 0 <= u < n and k < len(units[u]) and units[u][k] is not None:
                    units[u][k]()

    def norm_unit(self, mm_fn, srcs_fn, gains, outs, Dn, P=128, N=TT, extra_reads=(), post=None, pslo=0, psn=4, sslo=4, ssn=4):
        st = {}

        def s0():
            p = self.psum_rot(pslo, psn) if mm_fn is not None else None
            if mm_fn is not None:
                mm_fn(p)
            srcs = srcs_fn(p)
            sqs = []
            for (sap, skey) in srcs:
                sq = self.sqring[self.sq_i % len(self.sqring)]
                self.sq_i += 1
                self.act(sq.ap[0:P, 0:N], sap, AF.Square, [skey], [sq.key])
                sqs.append(sq)
            st["srcs"], st["sqs"] = srcs, sqs

        def s1():
            ss = self.psum_rot(sslo, ssn)
            sqs = st["sqs"]
            for c, sq in enumerate(sqs):
                self.mm(ss.ap[0:P, 0:N], self.ones_bf[0:P, 0:P], sq.ap[0:P, 0:N], c == 0, c == len(sqs) - 1, [sq.key], [ss.key])
            rs = self.rsring[self.rs_i % len(self.rsring)]
            self.rs_i += 1
            self.act(rs.ap[0:P, 0:N], ss.ap[0:P, 0:N], AF.Ln, [ss.key], [rs.key], scale=1.0 / Dn, bias=self.eps_col[0:P, :])
            self.act(rs.ap[0:P, 0:N], rs.ap[0:P, 0:N], AF.Exp, [rs.key], [rs.key], scale=-0.5)
            st["rs"] = rs

        def s2():
            rs = st["rs"]
            for (sap, skey), g, (oap, okey) in zip(st["srcs"], gains, outs):
                self.stt(oap, sap, g, rs.ap[0:P, 0:N], ALU.mult, ALU.mult, [skey, rs.key] + list(extra_reads), [okey])
            if post is not None:
                post()

        return [s0, s1, s2]

    def mm(self, out, lhsT, rhs, start, stop, reads, writes):
        self.S.op("pe", lambda t: t.matmul(out, lhsT=lhsT, rhs=rhs, start=start, stop=stop), reads, writes)

    def tr(self, out, in_, ident, reads, writes):
        self.S.op("pe", lambda t: t.transpose(out, in_, ident), reads, writes)

    def act(self, out, in_, func, reads, writes, scale=1.0, bias=None, accum_out=None, strict=False):
        kw = {}
        if bias is not None:
            kw["bias"] = bias
        if accum_out is not None:
            kw["accum_out"] = accum_out
        self.S.op("act", lambda a: a.activation(out, in_, func, scale=scale, **kw), reads, writes, strict=strict)

    def tt(self, eng, out, in0, in1, op, reads, writes):
        self.S.op(eng, lambda v: v.tensor_tensor(out, in0, in1, op), reads, writes)

    def ts(self, eng, out, in0, s1, s2, op0, op1, reads, writes):
        if op1 is None:
            self.S.op(eng, lambda v: v.tensor_scalar(out, in0, s1, None, op0), reads, writes)
        else:
            self.S.op(eng, lambda v: v.tensor_scalar(out, in0, s1, s2, op0, op1), reads, writes)

    def stt(self, out, in0, scalar, in1, op0, op1, reads, writes):
        self.S.op("dve", lambda v: v.scalar_tensor_tensor(out, in0, scalar, in1, op0, op1), reads, writes)

    def copy(self, eng, out, in_, reads, writes):
        if eng == "act":
            self.S.op("act", lambda a: a.activation(out, in_, AF.Copy), reads, writes)
        else:
            self.S.op(eng, lambda v: v.tensor_copy(out, in_), reads, writes)

    def evac(self, out, in_, reads, writes):
        self.act_dve += 1
        self.copy("act" if self.act_dve % 2 else "dve", out, in_, reads, writes)

    def recip(self, out, in_, reads, writes):
        self.S.op("dve", lambda v: v.reciprocal(out, in_), reads, writes)

    def dma(self, q, out, in_, reads, writes, nobar=False):
        self.S.op(q, lambda g: g.dma_start(out=out, in_=in_), reads, writes, dma=True, nobar=nobar)

    def pnorm(self, srcs, gains, outs, Dn, P=128, N=TT, extra_reads=(), post_scale=1.0):
        ss = self.psum()
        nsrc = len(srcs)
        for c, (sap, skey) in enumerate(srcs):
            sq = self.sqring[self.sq_i % len(self.sqring)]
            self.sq_i += 1
            self.act(sq.ap[0:P, 0:N], sap, AF.Square, [skey], [sq.key])
            self.mm(ss.ap[0:P, 0:N], self.ones_bf[0:P, 0:P], sq.ap[0:P, 0:N], c == 0, c == nsrc - 1,
                    [sq.key], [ss.key])
        rs = self.rsring[self.rs_i % len(self.rsring)]
        self.rs_i += 1
        self.act(rs.ap[0:P, 0:N], ss.ap[0:P, 0:N], AF.Ln, [ss.key], [rs.key], scale=1.0 / Dn, bias=self.eps_col[0:P, :])
        if post_scale != 1.0:
            self.act(rs.ap[0:P, 0:N], rs.ap[0:P, 0:N], AF.Exp, [rs.key], [rs.key], scale=-0.5,
                     bias=self.lnps_col[0:P, :])
        else:
            self.act(rs.ap[0:P, 0:N], rs.ap[0:P, 0:N], AF.Exp, [rs.key], [rs.key], scale=-0.5)
        for (sap, skey), g, (oap, okey) in zip(srcs, gains, outs):
            self.stt(oap, sap, g, rs.ap[0:P, 0:N], ALU.mult, ALU.mult, [skey, rs.key] + list(extra_reads), [okey])

    def wload(self, src_ap, ncols, kc=16):
        i = self.w_i % len(self.wb)
        self.w_i += 1
        t = self.wb[i]
        view = t.ap[:, 0:kc * ncols].rearrange("p (c n) -> p c n", c=kc)
        self.dma("pool", view, src_ap.rearrange("(c p) n -> p c n", p=128), [], [t.key], nobar=True)
        return view, t.key


def norm_steps(b, x_d, tsl, xkey, gains, hf, xp, xpc, rsn, K_VEC, sqr, sqc):
    ss = b.psum_at(7)
    st = {}

    def p1a(g):
        def f():
            xq = xp[xpc[0] % len(xp)]
            xpc[0] += 1
            b.dma("sp", xq.ap, x_d[:, g * 4:(g + 1) * 4, tsl], [xkey], [xq.key])
            sqs = []
            for j in range(4):
                sq = sqr[sqc[0] % len(sqr)]
                sqc[0] += 1
                b.act(sq.ap, xq.ap[:, j, :], AF.Square, [xq.key], [sq.key])
                sqs.append(sq)
            st[g] = sqs
        return f

    def p1b(g):
        def f():
            for j, sq in enumerate(st[g]):
                b.mm(ss.ap, b.ones_bf, sq.ap, g == 0 and j == 0, g == 3 and j == 3, [sq.key], [ss.key])
        return f

    def pr():
        b.act(rsn.ap, ss.ap, AF.Ln, [ss.key], [rsn.key], scale=1.0 / D, bias=b.eps_col)
        b.act(rsn.ap, rsn.ap, AF.Exp, [rsn.key], [rsn.key], scale=-0.5)

    def p2(g):
        def f():
            xq = xp[xpc[0] % len(xp)]
            xpc[0] += 1
            b.dma("sp", xq.ap, x_d[:, g * 4:(g + 1) * 4, tsl], [xkey], [xq.key])
            for j in range(4):
                c = g * 4 + j
                b.stt(hf.ap[:, c, :], xq.ap[:, j, :], gains[c], rsn.ap, ALU.mult, ALU.mult, [xq.key, rsn.key, K_VEC], [hf.key])
        return f

    def seq(*fs):
        def f():
            for x in fs:
                x()
        return f

    steps = [p1a(0), seq(p1a(1), p1b(0)), seq(p1a(2), p1b(1)), seq(p1a(3), p1b(2)), seq(p1b(3), pr)]
    steps += [p2(g) for g in range(4)]
    return steps


def vcol(b, l, j):
    return b.vec[:, l * NVEC + j:l * NVEC + j + 1]


def build_program(NL=DEPTH, NS=2, debug=False):
    nc = bass.Bass("TRN2", target_bir_lowering=False)
    S = Sched()
    es = ExitStack()
    b = Builder(nc, S, es, NL, NS, debug)

    def din(name, shape, dt=F32):
        return nc.dram_tensor(name, list(shape), dt, kind="ExternalInput").ap()

    def dscr(name, shape, dt):
        return nc.dram_tensor(name, list(shape), dt, kind=("ExternalOutput" if debug else "Internal")).ap()

    x_d = din("x", [NS, SEQ, D])
    mem_d = din("mem", [NS, 256, D])
    pos_d = din("pos", [NS, 64, SEQ], I32)
    biasT_d = din("biasT", [128, 2 * 8 * 128])
    sinks_d = din("sinks", [NL, 1, 1024])
    vecs_d = din("vecs", [128, NL * NVEC])
    NCST = 128 * 4 + 256 + 64 + 4
    cst_d = din("consts", [128, NCST])
    win_d = din("w_in", [NL, D, IN_QKV])
    wpc_d = din("w_pc", [NL, 16, D, 512])
    wout_d = din("w_out", [NL, D, D])
    wuq_d = din("w_uq", [NL, 512, 768])
    wukv_d = din("w_ukv", [NL, 512, 1024])
    xwq_d = din("xa_wq", [NL, D, 512])
    xwkv_d = din("xa_wkv", [NL, D, 1024])
    xwo_d = din("xa_wo", [NL, 512, D])
    wgu_d = din("w_gu", [NL, 22, D, 512])
    wdn_d = din("w_down", [NL, D_FF, D])
    out_d = nc.dram_tensor("out", [NS, SEQ, D], F32, kind="ExternalOutput").ap()

    xT_d = dscr("xT", [NS, 128, 16, SEQ], F32)
    memT_d = dscr("memT", [NS, 128, 16, 256], F32)
    rope_d = dscr("ropeT", [NS, 2, 64, SEQ], F32)
    QA_d = dscr("QA", [8, 128, SEQ], BF16)
    KA_d = dscr("KA", [2, 128, SEQ], BF16)
    VA_d = dscr("VA", [SEQ, 256], BF16)
    QB_d = dscr("QB", [4, 128, SEQ], BF16)
    KB_d = dscr("KB", [4, 128, SEQ], BF16)
    VB_d = dscr("VB", [SEQ, 512], BF16)
    CQ_d = dscr("CQ", [4, 128, SEQ], F32)
    CKV_d = dscr("CKV", [4, 128, SEQ], F32)
    KRr_d = dscr("KRr", [64, SEQ], F32)
    QN_d = dscr("QN", [4, 128, SEQ], BF16)
    QR_d = dscr("QR", [4, 64, SEQ], BF16)
    KN_d = dscr("KN", [4, 128, SEQ], BF16)
    KR_d = dscr("KR", [64, SEQ], BF16)
    VC_d = dscr("VC", [SEQ, 512], BF16)
    OT_d = dscr("OT", [16, 128, SEQ], BF16)

    if debug:
        dbg1_d = dscr("dbg_x1", [128, 16, SEQ], F32)
        dbg2_d = dscr("dbg_x2", [128, 16, SEQ], F32)
    cst = b.static("cst", [128, NCST], F32)
    vec = b.static("vec", [128, NL * NVEC], F32)
    b.vec = vec
    cbf = b.static("cbf", [128, 128 * 5], BF16)
    onesneg = b.static("onesneg", [128, 128], BF16)
    biasM = b.static("biasM", [128, 2, 8, 128], F32)
    eskhl = b.static("eskhl", [1, 2048], BF16)
    small = b.static("small", [128, 8], F32)
    hT_flat = b.static("hT", [128, 4 * 16 * TT], BF16)
    wb_t = [b.static(f"wb{i}", [128, 16 * 512], BF16) for i in range(3)]
    b.wb = [Tile(t[:, :], ("wb", i)) for i, t in enumerate(wb_t)]
    b.w_i = 0
    used = NCST * 4 + NL * NVEC * 4 + 640 * 2 + 256 + 8192 + 4096 + 32 + 65536 + 3 * 16384
    b.ARENA = (nc.sbuf_bytes_remaining - 1024) // 32 * 32
    b.arena = b.static("arena", [128, b.ARENA], U8)
    b.aoff = 0
    b.ps = [es.enter_context(nc.psum_tensor(f"ps{i}", [128, 512], F32)) for i in range(8)]
    sems = {e: es.enter_context(nc.semaphore(f"s_{e}")) for e in ENGS}
    dsems = {(e, k): es.enter_context(nc.semaphore(f"d_{e}{k}")) for e in ("sp", "pool", "act") for k in range(DMA_RING)}

    ident_f = cst[:, 0:128]
    uineg_f = cst[:, 128:256]
    tri_incl_f = cst[:, 256:384]
    tri_strict_f = cst[:, 384:512]
    maskneg_f = cst[:, 512:768]
    rrot_f = cst[0:64, 768:832]
    invf = cst[0:64, 832:833]
    ident_bf = cbf[:, 0:128]
    b.ones_bf = cbf[:, 128:256]
    uineg_bf = cbf[:, 256:384]
    tri_incl_bf = cbf[:, 384:512]
    tri_strict_bf = cbf[:, 512:640]
    b.eps_col = small[:, 0:1]
    b.lnps_col = small[:, 1:2]
    b.one_col = small[:, 2:3]
    K_CST, K_VEC, K_CBF = ("cst",), ("vec",), ("cbf",)

    hT = [Tile(hT_flat[:, tt * 16 * TT:(tt + 1) * 16 * TT].rearrange("p (c t) -> p c t", c=16), ("hT", tt))
          for tt in range(NTT)]

    def phase_begin(nsq=4, nrs=3):
        b.arena_reset()
        b.sqring = b.ring("sq", [128, TT], BF16, nsq)
        b.rsring = b.ring("rs", [128, TT], F32, nrs)
        b.sq_i = 0
        b.rs_i = 0

    phase_begin()
    b.dma("sp", cst[:, :], cst_d, [], [K_CST])
    b.dma("sp", vec[:, :], vecs_d, [], [K_VEC])
    S.op("dve", lambda v: v.memset(small[:, 0:1], EPS), [], [("small",)])
    S.op("dve", lambda v: v.memset(small[:, 1:2], 0.0), [], [("small",)])
    S.op("dve", lambda v: v.memset(small[:, 2:3], 1.0), [], [("small",)])
    S.op("dve", lambda v: v.tensor_copy(cbf[:, 0:128], ident_f), [K_CST], [K_CBF])
    S.op("dve", lambda v: v.memset(cbf[:, 128:256], 1.0), [], [K_CBF])
    S.op("dve", lambda v: v.memset(onesneg[:, :], -1.0), [], [K_CBF])
    S.op("dve", lambda v: v.tensor_copy(cbf[:, 256:640], cst[:, 128:512]), [K_CST], [K_CBF])
    bt = b.alloc("biasT", [128, 2, 8, 128], F32)
    b.dma("sp", bt.ap.rearrange("p a b c -> p (a b c)"), biasT_d, [], [bt.key])
    for c in range(2):
        mk = maskneg_f[:, c * 128:(c + 1) * 128].unsqueeze(1).broadcast_to([128, 8, 128])
        b.tt("dve", biasM[:, c, :, :], bt.ap[:, c, :, :], mk, ALU.add, [bt.key, K_CST], [("biasM",)])

    phase_begin()
    xin = b.ring("xin", [128, D], F32, 2)
    xst = b.ring("xst", [128, 16, 256], F32, 2)
    for s in range(NS):
        for tp in range(8):
            xo = xst[tp % 2]
            for half in range(2):
                tc = tp * 2 + half
                xi = xin[tc % 2]
                b.dma("sp", xi.ap, x_d[s, tc * 128:(tc + 1) * 128, :], [], [xi.key])
                for g in range(4):
                    p = b.psum()
                    for j in range(4):
                        c = g * 4 + j
                        b.tr(p.ap[:, j * 128:(j + 1) * 128], xi.ap[:, c * 128:(c + 1) * 128], ident_f, [xi.key, K_CST], [p.key])
                    b.evac(xo.ap[:, g * 4:(g + 1) * 4, half * 128:(half + 1) * 128], p.ap.rearrange("p (a b) -> p a b", a=4), [p.key], [xo.key])
            b.dma("sp", xT_d[s, :, :, tp * 256:(tp + 1) * 256], xo.ap, [xo.key], [("xT", s, tp // 2)])
        for mc in range(2):
            xi = xin[mc % 2]
            xo = xst[mc % 2]
            b.dma("sp", xi.ap, mem_d[s, mc * 128:(mc + 1) * 128, :], [], [xi.key])
            junk = b.alloc("junk", [128, D], BF16) if (s == 0 and mc == 0) else junk
            ssq = b.alloc("ssq", [128, 4], F32) if (s == 0 and mc == 0) else ssq
            b.act(junk.ap, xi.ap, AF.Square, [xi.key], [junk.key, ssq.key], accum_out=ssq.ap[:, 0:1])
            b.act(ssq.ap[:, 1:2], ssq.ap[:, 0:1], AF.Ln, [ssq.key], [ssq.key], scale=1.0 / D, bias=b.eps_col, strict=True)
            b.act(ssq.ap[:, 2:3], ssq.ap[:, 1:2], AF.Exp, [ssq.key], [ssq.key], scale=-0.5, strict=True)
            b.ts("dve", xi.ap, xi.ap, ssq.ap[:, 2:3], None, ALU.mult, None, [xi.key, ssq.key], [xi.key])
            for g in range(4):
                p = b.psum()
                for j in range(4):
                    c = g * 4 + j
                    b.tr(p.ap[:, j * 128:(j + 1) * 128], xi.ap[:, c * 128:(c + 1) * 128], ident_f, [xi.key, K_CST], [p.key])
                b.evac(xo.ap[:, g * 4:(g + 1) * 4, 0:128], p.ap.rearrange("p (a b) -> p a b", a=4), [p.key], [xo.key])
            b.dma("sp", memT_d[s, :, :, mc * 128:(mc + 1) * 128], xo.ap[:, :, 0:128], [xo.key], [("memT", s)])
        if s == 0:
            QS = 512
            posi = b.alloc("posi", [64, QS], I32)
            ang = b.alloc("ang", [64, QS], F32)
            kf = b.alloc("kf", [64, QS], F32)
            ki = b.alloc("ki", [64, QS], I32)
            rr = b.alloc("rr", [64, QS], F32)
            msk = b.alloc("msk", [64, QS], F32)
        C1 = 6.28125
        C2 = 2 * math.pi - C1
        for qq in range(SEQ // QS):
            qsl = slice(qq * QS, (qq + 1) * QS)
            b.dma("sp", posi.ap, pos_d[s][:, qsl], [], [posi.key])
            b.copy("dve", ang.ap, posi.ap, [posi.key], [ang.key])
            b.ts("dve", ang.ap, ang.ap, invf, None, ALU.mult, None, [ang.key, K_CST], [ang.key])
            for which, shift in ((1, 0.0), (0, math.pi / 2)):
                b.ts("dve", kf.ap, ang.ap, shift, 1.0 / (2 * math.pi), ALU.add, ALU.mult, [ang.key], [kf.key])
                b.copy("dve", ki.ap, kf.ap, [kf.key], [ki.key])
                b.copy("dve", kf.ap, ki.ap, [ki.key], [kf.key])
                b.stt(rr.ap, kf.ap, -C1, ang.ap, ALU.mult, ALU.add, [kf.key, ang.key], [rr.key])
                b.stt(rr.ap, kf.ap, -C2, rr.ap, ALU.mult, ALU.add, [kf.key, rr.key], [rr.key])
                if shift:
                    b.ts("dve", rr.ap, rr.ap, shift, None, ALU.add, None, [rr.key], [rr.key])
                b.ts("dve", msk.ap, rr.ap, math.pi, -2 * math.pi, ALU.is_gt, ALU.mult, [rr.key], [msk.key])
                b.tt("dve", rr.ap, rr.ap, msk.ap, ALU.add, [rr.key, msk.key], [rr.key])
                b.ts("dve", msk.ap, rr.ap, -math.pi, 2 * math.pi, ALU.is_lt, ALU.mult, [rr.key], [msk.key])
                b.tt("dve", rr.ap, rr.ap, msk.ap, ALU.add, [rr.key, msk.key], [rr.key])
                b.ts("dve", rr.ap, rr.ap, math.pi, -math.pi, ALU.min, ALU.max, [rr.key], [rr.key])
                b.act(msk.ap, rr.ap, AF.Sin, [rr.key], [msk.key])
                b.dma("sp", rope_d[s, which][:, qsl], msk.ap, [msk.key], [("rope", s)])

    for l in range(NL):
        phase_begin()
        sk = b.alloc("sk", [1, 1024], F32)
        b.dma("sp", sk.ap, sinks_d[l], [], [sk.key])
        b.act(sk.ap, sk.ap, AF.Exp, [sk.key], [sk.key])
        skh = b.alloc("skh", [1, 1024], F32)
        b.copy("dve", eskhl[0:1, 0:1024], sk.ap, [sk.key], [("eskhl",)])
        b.copy("dve", skh.ap, eskhl[0:1, 0:1024], [("eskhl",)], [skh.key])
        b.tt("dve", eskhl[0:1, 1024:2048], sk.ap, skh.ap, ALU.subtract, [sk.key, skh.key], [("eskhl",)])

        for s in range(NS):
            phase_begin()
            xt = b.alloc("xt", [128, 16, TT], F32)
            for tt in range(NTT):
                for hf in range(2):
                    b.dma("sp", xt.ap[:, hf * 8:(hf + 1) * 8, :], xT_d[s, :, hf * 8:(hf + 1) * 8, tt * TT:(tt + 1) * TT],
                          [("xT", s, tt)], [xt.key])
                b.pnorm([(xt.ap[:, c, :], xt.key) for c in range(16)], [vcol(b, l, c) for c in range(16)],
                        [(hT[tt].ap[:, c, :], hT[tt].key) for c in range(16)], D, extra_reads=[K_VEC])
            stg = b.ring("stg", [128, TT], BF16, 4)
            stf = b.ring("stf", [128, TT], F32, 3)
            si = [0, 0]
            blocks = [(0, 512, "qa", 0), (512, 512, "qa", 4), (1024, 256, "ka", 0), (1536, 512, "qb", 0),
                      (2048, 512, "kb", 0), (3072, 512, "cq", 0), (3584, 512, "ckv", 0), (4096, 64, "kr", 0)]
            for (c0, ncols, kind, hbase) in blocks:
                wv, wk = b.wload(win_d[l][:, c0:c0 + ncols], ncols)
                for j in range(max(1, ncols // 128)):
                    M = min(128, ncols)
                    for tt in range(NTT):
                        p = b.psum()
                        for k in range(16):
                            b.mm(p.ap[0:M, :], wv[:, k, j * 128:j * 128 + M], hT[tt].ap[:, k, :], k == 0, k == 15,
                                 [wk, hT[tt].key], [p.key])
                        tsl = slice(tt * TT, (tt + 1) * TT)
                        if kind in ("qa", "ka"):
                            o = stg[si[0] % 4]
                            si[0] += 1
                            b.pnorm([(p.ap, p.key)], [vcol(b, l, 64 if kind == "qa" else 65)], [(o.ap, o.key)], 128,
                                    extra_reads=[K_VEC])
                            dst = QA_d[hbase + j] if kind == "qa" else KA_d[j]
                            b.dma("sp", dst[:, tsl], o.ap, [o.key], [(kind, hbase + j, tt)])
                        elif kind in ("qb", "kb"):
                            o = stg[si[0] % 4]
                            si[0] += 1
                            if kind == "qb":
                                b.act(o.ap, p.ap, AF.Copy, [p.key], [o.key], scale=128 ** -0.5)
                            else:
                                b.evac(o.ap, p.ap, [p.key], [o.key])
                            dst = QB_d[j] if kind == "qb" else KB_d[j]
                            b.dma("sp", dst[:, tsl], o.ap, [o.key], [(kind, j, tt)])
                        else:
                            o = stf[si[1] % 3]
                            si[1] += 1
                            b.evac(o.ap[0:M, :], p.ap[0:M, :], [p.key], [o.key])
                            dst = {"cq": CQ_d, "ckv": CKV_d}.get(kind)
                            if kind == "kr":
                                b.dma("sp", KRr_d[:, tsl], o.ap[0:64, :], [o.key], [("krr", tt)])
                            else:
                                b.dma("sp", dst[j][:, tsl], o.ap, [o.key], [(kind, j, tt)])
            for (c0, ncols, dst, nm) in ((1280, 256, VA_d, "va"), (2560, 512, VB_d, "vb")):
                wv, wk = b.wload(win_d[l][:, c0:c0 + ncols], ncols)
                for tc in range(16):
                    p = b.psum()
                    for k in range(16):
                        b.mm(p.ap[:, 0:ncols], hT[tc // 4].ap[:, k, (tc % 4) * 128:(tc % 4 + 1) * 128], wv[:, k, :],
                             k == 0, k == 15, [wk, hT[tc // 4].key], [p.key])
                    o = stg[si[0] % 4]
                    si[0] += 1
                    b.evac(o.ap[:, 0:ncols], p.ap[:, 0:ncols], [p.key], [o.key])
                    b.dma("sp", dst[tc * 128:(tc + 1) * 128, :], o.ap[:, 0:ncols], [o.key], [(nm, tc // 4)])

            phase_begin(nsq=9, nrs=3)
            wuq = b.alloc("wuq", [128, 4, 768], BF16)
            wukv = b.alloc("wukv", [128, 4, 1024], BF16)
            b.dma("pool", wuq.ap, wuq_d[l].rearrange("(c p) n -> p c n", p=128), [], [wuq.key])
            b.dma("pool", wukv.ap, wukv_d[l].rearrange("(c p) n -> p c n", p=128), [], [wukv.key])
            cqr = b.alloc("cqr", [128, 4, TT], F32)
            ckvr = b.alloc("ckvr", [128, 4, TT], F32)
            krr = b.alloc("krr", [64, TT], F32)
            cs = b.alloc("cs", [64, 2, TT], F32)
            cqn = b.alloc("cqn", [128, 4, TT], BF16)
            ckvn = b.alloc("ckvn", [128, 4, TT], BF16)
            stg = b.ring("stg", [128, TT], BF16, 5)
            rn = b.ring("rn", [64, TT], F32, 3)
            r1 = b.ring("r1", [64, TT], F32, 2)
            sic = [0, 0, 0]

            def rope_post(src, dst_ap, dkey):
                def f():
                    p = b.psum_rot(0, 4)
                    b.mm(p.ap[0:64, :], rrot_f, src.ap, True, True, [src.key, K_CST], [p.key])
                    t1 = r1[sic[1] % 2]
                    sic[1] += 1
                    b.tt("dve", t1.ap, src.ap, cs.ap[:, 0, :], ALU.mult, [src.key, cs.key], [t1.key])
                    b.tt("dve", src.ap, p.ap[0:64, :], cs.ap[:, 1, :], ALU.mult, [p.key, cs.key, src.key], [src.key])
                    o = stg[sic[0] % 5]
                    sic[0] += 1
                    b.tt("dve", o.ap[0:64, :], t1.ap, src.ap, ALU.add, [t1.key, src.key], [o.key])
                    b.dma("sp", dst_ap, o.ap[0:64, :], [o.key], [dkey])
                return f

            for tt in range(NTT):
                tsl = slice(tt * TT, (tt + 1) * TT)
                b.dma("sp", cqr.ap, CQ_d[:, :, tsl].rearrange("c p t -> p c t"), [("cq", c, tt) for c in range(4)], [cqr.key])
                b.dma("sp", ckvr.ap, CKV_d[:, :, tsl].rearrange("c p t -> p c t"), [("ckv", c, tt) for c in range(4)], [ckvr.key])
                b.dma("sp", krr.ap, KRr_d[:, tsl], [("krr", tt)], [krr.key])
                b.dma("sp", cs.ap, rope_d[s, :, :, tsl].rearrange("w p t -> p w t"), [("rope", s)], [cs.key])
                units = []
                units.append(b.norm_unit(None, lambda p: [(cqr.ap[:, c, :], cqr.key) for c in range(4)],
                                         [vcol(b, l, 68 + c) for c in range(4)], [(cqn.ap[:, c, :], cqn.key) for c in range(4)], 512,
                                         extra_reads=[K_VEC]))
                units.append(b.norm_unit(None, lambda p: [(ckvr.ap[:, c, :], ckvr.key) for c in range(4)],
                                         [vcol(b, l, 72 + c) for c in range(4)], [(ckvn.ap[:, c, :], ckvn.key) for c in range(4)], 512,
                                         extra_reads=[K_VEC]))
                qk = rn[2]
                units.append(b.norm_unit(None, lambda p: [(krr.ap, krr.key)], [vcol(b, l, 79)[0:64, :]], [(qk.ap, qk.key)], 64, P=64,
                                         extra_reads=[K_VEC], post=rope_post(qk, KR_d[:, tsl], ("kr", tt))))
                b.pipeline(units)
                units = []
                for h in range(4):
                    def mm_qn(p, h=h):
                        for c in range(4):
                            b.mm(p.ap, wuq.ap[:, c, h * 192:h * 192 + 128], cqn.ap[:, c, :], c == 0, c == 3, [wuq.key, cqn.key], [p.key])

                    def mm_qr(p, h=h):
                        for c in range(4):
                            b.mm(p.ap[0:64, :], wuq.ap[:, c, h * 192 + 128:h * 192 + 192], cqn.ap[:, c, :], c == 0, c == 3,
                                 [wuq.key, cqn.key], [p.key])

                    def mm_kn(p, h=h):
                        for c in range(4):
                            b.mm(p.ap, wukv.ap[:, c, h * 256:h * 256 + 128], ckvn.ap[:, c, :], c == 0, c == 3, [wukv.key, ckvn.key], [p.key])

                    o1 = stg[sic[0] % 5]
                    sic[0] += 1
                    units.append(b.norm_unit(mm_qn, lambda p: [(p.ap, p.key)], [vcol(b, l, 76)], [(o1.ap, o1.key)], 128, extra_reads=[K_VEC],
                                             post=(lambda o1=o1, h=h: b.dma("sp", QN_d[h][:, tsl], o1.ap, [o1.key], [("qn", h, tt)]))))
                    q = rn[h % 2]
                    units.append(b.norm_unit(mm_qr, lambda p: [(p.ap[0:64, :], p.key)], [vcol(b, l, 77)[0:64, :]], [(q.ap, q.key)], 64, P=64,
                                             extra_reads=[K_VEC], post=rope_post(q, QR_d[h][:, tsl], ("qr", h, tt))))
                    o2 = stg[sic[0] % 5]
                    sic[0] += 1
                    units.append(b.norm_unit(mm_kn, lambda p: [(p.ap, p.key)], [vcol(b, l, 78)], [(o2.ap, o2.key)], 128, extra_reads=[K_VEC],
                                             post=(lambda o2=o2, h=h: b.dma("sp", KN_d[h][:, tsl], o2.ap, [o2.key], [("kn", h, tt)]))))
                for tq in range(4):
                    def vunit(tq=tq):
                        p = b.psum_rot(0, 4)
                        for c in range(4):
                            rhs = wukv.ap[:, c, :].rearrange("p (h two d) -> p h two d", h=4, two=2)[:, :, 1, :]
                            b.mm(p.ap.rearrange("p (h d) -> p h d", h=4), ckvn.ap[:, c, tq * 128:(tq + 1) * 128], rhs, c == 0, c == 3,
                                 [wukv.key, ckvn.key], [p.key])
                        o = stg[sic[0] % 5]
                        sic[0] += 1
                        b.evac(o.ap, p.ap, [p.key], [o.key])
                        r0 = (tt * 4 + tq) * 128
                        b.dma("sp", VC_d[r0:r0 + 128, :], o.ap, [o.key], [("vc", tt)])
                    units.append([vunit])
                b.pipeline(units)

            phase_begin()
            Qt = b.ring("Qt", [128, 8, TT], BF16, 2)
            Kt = b.ring("Kt", [128, 2, 640], BF16, 2)
            Vt = b.ring("Vt", [128, 5, 256], BF16, 2)
            OAt = b.ring("OAt", [128, 8, TT], BF16, 2)
            tf = b.ring("tf", [128, TT], F32, 4)
            Pt = b.ring("Pt", [128, TT], BF16, 6)
            rd = b.ring("rd", [128, TT], F32, 2)
            cic = [0, 0]
            scale_a = 128 ** -0.5
            units = []
            for tt in range(NTT):
                for bl in range(4):
                    for kvh in range(2):
                        st = {}

                        def s0(tt=tt, bl=bl, kvh=kvh, st=st):
                            tsl = slice(tt * TT, (tt + 1) * TT)
                            qt, kt, vt, oa = Qt[tt % 2], Kt[tt % 2], Vt[tt % 2], OAt[tt % 2]
                            t0 = tt * TT - 128 if tt > 0 else 0
                            ln = (tt + 1) * TT - t0
                            if bl == 0 and kvh == 0:
                                b.dma("sp", qt.ap, QA_d[:, :, tsl].rearrange("h p t -> p h t"), [("qa", h, tt) for h in range(8)], [qt.key])
                                b.dma("sp", kt.ap[:, :, 0:ln], KA_d[:, :, t0:t0 + ln].rearrange("h p t -> p h t"),
                                      [("ka", h, t_) for h in range(2) for t_ in range(max(0, tt - 1), tt + 1)], [kt.key])
                                b.dma("sp", vt.ap[:, 0:ln // 128, :], VA_d[t0:t0 + ln, :].rearrange("(c p) v -> p c v", p=128),
                                      [("va", t_) for t_ in range(max(0, tt - 1), tt + 1)], [vt.key])
                            gb = tt * 4 + bl
                            cur = gb * 128 - t0
                            chunks = ([] if gb == 0 else [(0, cur - 128)]) + [(1, cur)]
                            pts = []
                            for (c, off) in chunks:
                                sp_ = b.psum_rot(0, 4)
                                b.mm(sp_.ap.rearrange("p (h i) -> p h i", h=4), kt.ap[:, kvh, off:off + 128],
                                     qt.ap[:, kvh * 4:(kvh + 1) * 4, bl * 128:(bl + 1) * 128], True, True, [kt.key, qt.key], [sp_.key])
                                t_ = tf[cic[0] % 4]
                                pt = Pt[cic[0] % 6]
                                cic[0] += 1
                                b.stt(t_.ap.rearrange("p (h i) -> p h i", h=4), sp_.ap.rearrange("p (h i) -> p h i", h=4), scale_a,
                                      biasM[:, c, kvh * 4:(kvh + 1) * 4, :], ALU.mult, ALU.add, [sp_.key, ("biasM",)], [t_.key])
                                b.act(pt.ap, t_.ap, AF.Exp, [t_.key], [pt.key])
                                pts.append((pt, off))
                            st["pts"] = pts

                        def s1(tt=tt, bl=bl, kvh=kvh, st=st):
                            tsl = slice(tt * TT, (tt + 1) * TT)
                            vt, oa = Vt[tt % 2], OAt[tt % 2]
                            pts = st["pts"]
                            po = b.psum_rot(4, 2)
                            pd = b.psum_rot(6, 2)
                            for i_, (pt, off) in enumerate(pts):
                                b.mm(po.ap, vt.ap[:, off // 128, kvh * 128:(kvh + 1) * 128], pt.ap, i_ == 0, i_ == len(pts) - 1,
                                     [vt.key, pt.key], [po.key])
                            for i_, (pt, off) in enumerate(pts):
                                b.mm(pd.ap, b.ones_bf, pt.ap, i_ == 0, False, [pt.key, K_CBF], [pd.key])
                            for hl in range(2):
                                b.mm(pd.ap, b.ones_bf[0:1, :], eskhl[0:1, hl * 1024 + kvh * 512:hl * 1024 + (kvh + 1) * 512], False, hl == 1,
                                     [("eskhl",), K_CBF], [pd.key])
                            r_ = rd[cic[1] % 2]
                            cic[1] += 1
                            b.act(r_.ap, pd.ap, AF.Ln, [pd.key], [r_.key])
                            b.act(r_.ap, r_.ap, AF.Exp, [r_.key], [r_.key], scale=-1.0)
                            b.tt("dve", oa.ap[:, kvh * 4:(kvh + 1) * 4, bl * 128:(bl + 1) * 128],
                                 po.ap.rearrange("p (h i) -> p h i", h=4), r_.ap.rearrange("p (h i) -> p h i", h=4), ALU.mult,
                                 [po.key, r_.key], [oa.key])
                            if bl == 3 and kvh == 1:
                                b.dma("sp", OT_d[0:8, :, tsl].rearrange("h p t -> p h t"), oa.ap, [oa.key], [("ot", c, tt) for c in range(8)])

                        units.append([s0, s1])
            b.pipeline(units)

            phase_begin()
            Kb = b.ring("Kb", [128, SEQ], BF16, 2)
            Vb = b.ring("Vb", [128, 16, 128], BF16, 2)
            Qb = b.ring("Qb", [128, TT], BF16, 2)
            SPr = [b.ring(f"SP{i}_", [128, TT], BF16, 16) for i in range(2)]
            ef = b.ring("ef", [128, TT], F32, 3)
            Wt = b.ring("Wt", [128, TT], BF16, 5)
            ob = b.ring("ob", [128, TT], BF16, 2)
            wic = [0, 0]
            items = [(h, qt) for h in range(4) for qt in range(NTT)]
            Aunits = []
            Bunits = []
            for it, (h, qt) in enumerate(items):
                kb_, vb_ = Kb[h % 2], Vb[h % 2]
                qb_ = Qb[it % 2]
                SP = SPr[it % 2]
                nk = 4 * qt + 4
                c0s = [max(0, (kc - 4 * qt) * 128) for kc in range(nk)]
                A = []
                for kc in range(nk):
                    def a0(h=h, qt=qt, kc=kc, kb_=kb_, vb_=vb_, qb_=qb_, SP=SP, c0s=c0s):
                        if kc == 0:
                            if qt == 0:
                                b.dma("sp", kb_.ap, KB_d[h], [("kb", h, t_) for t_ in range(4)], [kb_.key])
                                b.dma("sp", vb_.ap, VB_d[:, h * 128:(h + 1) * 128].rearrange("(c p) v -> p c v", p=128),
                                      [("vb", t_) for t_ in range(4)], [vb_.key])
                            b.dma("sp", qb_.ap, QB_d[h][:, qt * TT:(qt + 1) * TT], [("qb", h, qt)], [qb_.key])
                        c0 = c0s[kc]
                        z = b.psum_rot(5, 3)
                        b.mm(z.ap[:, c0:], kb_.ap[:, kc * 128:(kc + 1) * 128], qb_.ap[:, c0:], True, True, [kb_.key, qb_.key], [z.key])
                        e = ef[wic[1] % 3]
                        wic[1] += 1
                        sp_ = SP[kc]
                        b.act(e.ap[:, c0:], z.ap[:, c0:], AF.Exp, [z.key], [e.key])
                        b.act(sp_.ap[:, c0:], e.ap[:, c0:], AF.Ln, [e.key], [sp_.key], bias=b.one_col)
                        if kc >= 4 * qt:
                            b.tt("pool", sp_.ap[:, c0:c0 + 128], sp_.ap[:, c0:c0 + 128], tri_strict_bf, ALU.mult,
                                 [sp_.key, K_CBF], [sp_.key])
                    A.append([a0])
                Bq = []
                po_i = it % 2
                order = list(range(nk - 1, -1, -1))
                for idx, kc in enumerate(order):
                    st = {}

                    def b0(h=h, qt=qt, kc=kc, kb_=kb_, qb_=qb_, SP=SP, c0s=c0s, nk=nk, st=st):
                        c0 = c0s[kc]
                        a = b.psum_rot(2, 3)
                        b.mm(a.ap[:, c0:], kb_.ap[:, kc * 128:(kc + 1) * 128], qb_.ap[:, c0:], True, False, [kb_.key, qb_.key], [a.key])
                        later = list(range(kc + 1, nk))
                        b.mm(a.ap[:, c0:], uineg_bf, SP[kc].ap[:, c0:], False, len(later) == 0, [SP[kc].key, K_CBF], [a.key])
                        for li, k2 in enumerate(later):
                            c2 = max(c0, c0s[k2])
                            b.mm(a.ap[:, c2:], onesneg[:, :], SP[k2].ap[:, c2:], False, li == len(later) - 1,
                                 [SP[k2].key, K_CBF], [a.key])
                        w = Wt[wic[0] % 5]
                        wic[0] += 1
                        b.act(w.ap[:, c0:], a.ap[:, c0:], AF.Exp, [a.key], [w.key])
                        if kc >= 4 * qt:
                            b.tt("pool", w.ap[:, c0:c0 + 128], w.ap[:, c0:c0 + 128], tri_strict_bf, ALU.mult, [w.key, K_CBF], [w.key])
                        st["w"] = w

                    def b2(h=h, qt=qt, kc=kc, vb_=vb_, c0s=c0s, idx=idx, nk=nk, st=st, po_i=po_i, it=it):
                        c0 = c0s[kc]
                        w = st["w"]
                        po = b.psum_at(po_i)
                        b.mm(po.ap[:, c0:], vb_.ap[:, kc, :], w.ap[:, c0:], idx == 0, idx == nk - 1, [vb_.key, w.key], [po.key])
                        if idx == nk - 1:
                            o = ob[it % 2]
                            b.evac(o.ap, po.ap, [po.key], [o.key])
                            b.dma("sp", OT_d[8 + h][:, qt * TT:(qt + 1) * TT], o.ap, [o.key], [("ot", 8 + h, qt)])

                    Bq.append([b0, None, b2])
                Aunits.append(A)
                Bunits.append(Bq)
            allu = list(Aunits[0])
            for it in range(len(items)):
                Bq = Bunits[it]
                An = Aunits[it + 1] if it + 1 < len(items) else []
                i1 = i2 = 0
                while i1 < len(Bq) or i2 < len(An):
                    if i1 < len(Bq):
                        allu.append(Bq[i1])
                        i1 += 1
                    if i2 < len(An):
                        allu.append(An[i2])
                        i2 += 1
            b.pipeline(allu)

            phase_begin()
            Kn = b.ring("Kn", [128, SEQ], BF16, 2)
            Kr = b.alloc("Kr", [64, SEQ], BF16)
            Vc = b.ring("Vc", [128, 16, 128], BF16, 2)
            Qn = b.ring("Qn", [128, TT], BF16, 2)
            Qr = b.ring("Qr", [64, TT], BF16, 2)
            Pm = b.ring("Pm", [128, TT], BF16, 6)
            rdm = b.ring("rdm", [128, TT], F32, 2)
            om = b.ring("om", [128, TT], BF16, 2)
            b.dma("sp", Kr.ap, KR_d, [("kr", t_) for t_ in range(4)], [Kr.key])
            scale_c = 192 ** -0.5
            pic = [0]
            units = []
            for it, (h, qt) in enumerate([(h, qt) for h in range(4) for qt in range(NTT)]):
                kn_, vc_ = Kn[h % 2], Vc[h % 2]
                qn_, qr_ = Qn[it % 2], Qr[it % 2]
                nk = 4 * qt + 4
                for kc in range(nk):
                    st = {}

                    def m0(h=h, qt=qt, kc=kc, kn_=kn_, vc_=vc_, qn_=qn_, qr_=qr_, st=st):
                        if kc == 0:
                            if qt == 0:
                                b.dma("sp", kn_.ap, KN_d[h], [("kn", h, t_) for t_ in range(4)], [kn_.key])
                                b.dma("sp", vc_.ap, VC_d[:, h * 128:(h + 1) * 128].rearrange("(c p) v -> p c v", p=128),
                                      [("vc", t_) for t_ in range(4)], [vc_.key])
                            b.dma("sp", qn_.ap, QN_d[h][:, qt * TT:(qt + 1) * TT], [("qn", h, qt)], [qn_.key])
                            b.dma("sp", qr_.ap, QR_d[h][:, qt * TT:(qt + 1) * TT], [("qr", h, qt)], [qr_.key])
                        c0 = max(0, (kc - 4 * qt) * 128)
                        sc = b.psum_rot(4, 4)
                        b.mm(sc.ap[:, c0:], kn_.ap[:, kc * 128:(kc + 1) * 128], qn_.ap[:, c0:], True, False, [kn_.key, qn_.key], [sc.key])
                        b.mm(sc.ap[:, c0:], Kr.ap[:, kc * 128:(kc + 1) * 128], qr_.ap[:, c0:], False, True, [Kr.key, qr_.key], [sc.key])
                        pm = Pm[pic[0] % 6]
                        pic[0] += 1
                        b.act(pm.ap[:, c0:], sc.ap[:, c0:], AF.Exp, [sc.key], [pm.key], scale=scale_c)
                        if kc >= 4 * qt:
                            b.tt("pool", pm.ap[:, c0:c0 + 128], pm.ap[:, c0:c0 + 128], tri_incl_bf, ALU.mult, [pm.key, K_CBF], [pm.key])
                        st["pm"] = pm

                    def m2(h=h, qt=qt, kc=kc, vc_=vc_, nk=nk, st=st, it=it):
                        c0 = max(0, (kc - 4 * qt) * 128)
                        pm = st["pm"]
                        po = b.psum_at(it % 2)
                        pd = b.psum_at(2 + it % 2)
                        b.mm(po.ap[:, c0:], vc_.ap[:, kc, :], pm.ap[:, c0:], kc == 0, kc == nk - 1, [vc_.key, pm.key], [po.key])
                        b.mm(pd.ap[:, c0:], b.ones_bf, pm.ap[:, c0:], kc == 0, kc == nk - 1, [pm.key, K_CBF], [pd.key])
                        if kc == nk - 1:
                            r_ = rdm[it % 2]
                            b.recip(r_.ap, pd.ap, [pd.key], [r_.key])
                            o = om[it % 2]
                            b.tt("dve", o.ap, po.ap, r_.ap, ALU.mult, [po.key, r_.key], [o.key])
                            b.dma("sp", OT_d[12 + h][:, qt * TT:(qt + 1) * TT], o.ap, [o.key], [("ot", 12 + h, qt)])

                    units.append([m0, None, m2])
            b.pipeline(units)

            phase_begin()
            OTt = b.alloc("OTt", [128, 16, TT], BF16)
            mg = b.alloc("mg", [128, 16, TT], BF16)
            gt = b.ring("gt", [128, TT], F32, 4)
            tm = b.ring("tm", [128, TT], F32, 4)
            xp = b.ring("xp", [128, 4, TT], F32, 2)
            gi = 0
            def otbuf(tt):
                return OTt if tt == 0 else Tile(hT[tt - 1].ap, hT[tt - 1].key)

            def otload(tt):
                ot = otbuf(tt)
                b.dma("sp", ot.ap, OT_d[:, :, tt * TT:(tt + 1) * TT].rearrange("c p t -> p c t"), [("ot", c, tt) for c in range(16)], [ot.key])

            otload(0)
            for tt in range(NTT):
                tsl = slice(tt * TT, (tt + 1) * TT)
                OTc = otbuf(tt)
                for n in range(16):
                    wv, wk = b.wload(wpc_d[l, n], 512)
                    gts = []
                    for g3 in range(3):
                        p = b.psum()
                        for k in range(16):
                            b.mm(p.ap, wv[:, k, g3 * 128:(g3 + 1) * 128], hT[tt].ap[:, k, :], k == 0, k == 15, [wk, hT[tt].key], [p.key])
                        g_ = gt[gi % 4]
                        gi += 1
                        b.act(g_.ap, p.ap, AF.Sigmoid, [p.key], [g_.key])
                        gts.append(g_)
                    kr_ = [(0, 8), (8, 12), (12, 16)]
                    tms = []
                    for g3 in range(3):
                        p = b.psum()
                        k0, k1 = kr_[g3]
                        for k in range(k0, k1):
                            b.mm(p.ap, wv[:, k, 384:512], OTc.ap[:, k, :], k == k0, k == k1 - 1, [wk, OTc.key], [p.key])
                        if g3 < 2:
                            t_ = tm[(n * 2 + g3) % 4]
                            b.tt("dve", t_.ap, p.ap, gts[g3].ap, ALU.mult, [p.key, gts[g3].key], [t_.key])
                            tms.append(t_)
                        else:
                            b.tt("dve", gts[2].ap, p.ap, gts[2].ap, ALU.mult, [p.key, gts[2].key], [gts[2].key])
                    b.tt("dve", tms[0].ap, tms[0].ap, tms[1].ap, ALU.add, [tms[0].key, tms[1].key], [tms[0].key])
                    b.tt("dve", mg.ap[:, n, :], tms[0].ap, gts[2].ap, ALU.add, [tms[0].key, gts[2].key], [mg.key])
                if tt + 1 < NTT:
                    otload(tt + 1)
                for nb in range(4):
                    wv, wk = b.wload(wout_d[l][:, nb * 512:(nb + 1) * 512], 512)
                    xq = xp[nb % 2]
                    b.dma("sp", xq.ap, xT_d[s, :, nb * 4:(nb + 1) * 4, tsl], [("xT", s, tt)], [xq.key])
                    for j in range(4):
                        p = b.psum()
                        for k in range(16):
                            b.mm(p.ap, wv[:, k, j * 128:(j + 1) * 128], mg.ap[:, k, :], k == 0, k == 15, [wk, mg.key], [p.key])
                        b.tt("dve", xq.ap[:, j, :], xq.ap[:, j, :], p.ap, ALU.add, [xq.key, p.key], [xq.key])
                    b.dma("sp", xT_d[s, :, nb * 4:(nb + 1) * 4, tsl], xq.ap, [xq.key], [("xT", s, tt)])

            phase_begin(nsq=4, nrs=2)
            if debug and l == 0 and s == 0:
                b.dma("sp", dbg1_d, xT_d[s], [("xT", s, t_) for t_ in range(4)], [("dbg1",)])
            hx = [Tile(hT[i].ap, ("hx", i)) for i in range(4)]
            b.ps_limit = 7
            memn = b.alloc("memn", [128, 16, 256], BF16)
            Km = b.alloc("Km", [128, 4, 256], BF16)
            Vm = b.alloc("Vm", [128, 2, 512], BF16)
            Qx = b.ring("Qx", [128, 4, TT], BF16, 2)
            Ox = b.ring("Ox", [128, 4, TT], BF16, 2)
            Px = b.ring("Px", [128, TT], BF16, 2)
            nsqr = b.ring("nsq", [128, TT], BF16, 8)
            nsqc = [0]
            rdx = b.ring("rdx", [128, TT], F32, 2)
            xp = b.ring("xp", [128, 4, TT], F32, 3)
            rsn = b.alloc("rsn", [128, TT], F32)
            xpc = [0]
            normq = []
            for t_ in range(NTT):
                for f in norm_steps(b, xT_d[s], slice(t_ * TT, (t_ + 1) * TT), ("xT", s, t_), [vcol(b, l, 16 + c) for c in range(16)],
                                    hx[t_], xp, xpc, rsn, K_VEC, nsqr, nsqc):
                    normq.append((t_, f))

            def npop(k):
                for _ in range(k):
                    if normq:
                        normq.pop(0)[1]()

            def nensure(t_):
                while normq and normq[0][0] <= t_:
                    normq.pop(0)[1]()

            npop(2)
            for g in range(4):
                npop(1)
                m_ = Tile(xp[g % 2].ap[:, :, 0:256], xp[g % 2].key)
                b.dma("sp", m_.ap, memT_d[s, :, g * 4:(g + 1) * 4, :], [("memT", s)], [m_.key])
                for j in range(4):
                    c = g * 4 + j
                    b.ts("dve", memn.ap[:, c, :], m_.ap[:, j, :], vcol(b, l, 32 + c), None, ALU.mult, None, [m_.key, K_VEC], [memn.key])
            for blk in range(2):
                wv, wk = b.wload(xwkv_d[l][:, blk * 512:(blk + 1) * 512], 512)
                for hh in range(2):
                    h = blk * 2 + hh
                    p = b.psum()
                    for k in range(16):
                        b.mm(p.ap[:, 0:256], wv[:, k, hh * 256:hh * 256 + 128], memn.ap[:, k, :], k == 0, k == 15, [wk, memn.key], [p.key])
                    b.pnorm([(p.ap[:, 0:256], p.key)], [vcol(b, l, 67)], [(Km.ap[:, h, :], Km.key)], 128, N=256, extra_reads=[K_VEC])
                    npop(3)
                for mc in range(2):
                    p = b.psum()
                    for k in range(16):
                        rhs = wv[:, k, :].rearrange("p (h two d) -> p h two d", h=2, two=2)[:, :, 1, :]
                        b.mm(p.ap[:, 0:256].rearrange("p (h d) -> p h d", h=2), memn.ap[:, k, mc * 128:(mc + 1) * 128], rhs,
                             k == 0, k == 15, [wk, memn.key], [p.key])
                    b.evac(Vm.ap[:, mc, blk * 256:(blk + 1) * 256], p.ap[:, 0:256], [p.key], [Vm.key])
                    npop(3)
            wqv, wqk = b.wload(xwq_d[l], 512)
            woi = b.w_i % len(b.wb)
            b.w_i += 1
            wot = b.wb[woi]
            wov = wot.ap.rearrange("p (c n) -> p c n", c=4)
            for q4 in range(4):
                b.dma("pool", wov[:, :, q4 * 512:(q4 + 1) * 512],
                      xwo_d[l][:, q4 * 512:(q4 + 1) * 512].rearrange("(c p) n -> p c n", p=128), [], [wot.key])
            scale_x = 128 ** -0.5
            pxc = [0]
            units = []
            for tt in range(NTT):
                def st1(tt=tt):
                    nensure(tt)
                    hxx = hx[tt]
                    qx = Qx[tt % 2]
                    us = []
                    for h in range(4):
                        def mmq(p, h=h):
                            for k in range(16):
                                b.mm(p.ap, wqv[:, k, h * 128:(h + 1) * 128], hxx.ap[:, k, :], k == 0, k == 15, [wqk, hxx.key], [p.key])
                        us.append(b.norm_unit(mmq, lambda p: [(p.ap, p.key)], [vcol(b, l, 66)], [(qx.ap[:, h, :], qx.key)], 128,
                                              extra_reads=[K_VEC], pslo=0, psn=3, sslo=3, ssn=2))
                    b.pipeline(us)
                    npop(3)

                def st2(tt=tt):
                    qx = Qx[tt % 2]
                    ox = Ox[tt % 2]
                    for h in range(4):
                        pts = []
                        for mc in range(2):
                            sc = b.psum_rot(0, 7)
                            b.mm(sc.ap, Km.ap[:, h, mc * 128:(mc + 1) * 128], qx.ap[:, h, :], True, True, [Km.key, qx.key], [sc.key])
                            px = Px[pxc[0] % 2]
                            pxc[0] += 1
                            b.act(px.ap, sc.ap, AF.Exp, [sc.key], [px.key], scale=scale_x)
                            pts.append(px)
                        po = b.psum_rot(0, 7)
                        pd = b.psum_rot(0, 7)
                        for mc in range(2):
                            b.mm(po.ap, Vm.ap[:, mc, h * 128:(h + 1) * 128], pts[mc].ap, mc == 0, mc == 1, [Vm.key, pts[mc].key], [po.key])
                        for mc in range(2):
                            b.mm(pd.ap, b.ones_bf, pts[mc].ap, mc == 0, mc == 1, [pts[mc].key, K_CBF], [pd.key])
                        r_ = rdx[h % 2]
                        b.recip(r_.ap, pd.ap, [pd.key], [r_.key])
                        b.tt("dve", ox.ap[:, h, :], po.ap, r_.ap, ALU.mult, [po.key, r_.key], [ox.key])

                def st3(tt=tt):
                    tsl = slice(tt * TT, (tt + 1) * TT)
                    ox = Ox[tt % 2]
                    for nb in range(4):
                        xq = xp[xpc[0] % 3]
                        xpc[0] += 1
                        b.dma("sp", xq.ap, xT_d[s, :, nb * 4:(nb + 1) * 4, tsl], [("xT", s, tt)], [xq.key])
                        for j in range(4):
                            n = nb * 4 + j
                            p = b.psum_rot(0, 7)
                            for h in range(4):
                                b.mm(p.ap, wov[:, h, n * 128:(n + 1) * 128], ox.ap[:, h, :], h == 0, h == 3, [wot.key, ox.key], [p.key])
                            b.tt("dve", xq.ap[:, j, :], xq.ap[:, j, :], p.ap, ALU.add, [xq.key, p.key], [xq.key])
                        b.dma("sp", xT_d[s, :, nb * 4:(nb + 1) * 4, tsl], xq.ap, [xq.key], [("xT", s, tt)])

                units.append([st1, st2, st3])
            b.pipeline(units, oldest_first=True)
            b.ps_limit = 8

            phase_begin(nsq=1, nrs=1)
            if debug and l == 0 and s == 0:
                b.dma("sp", dbg2_d, xT_d[s], [("xT", s, t_) for t_ in range(4)], [("dbg2",)])
            b.ps_limit = 7
            hfA = Tile(hT_flat[:, 0:16 * TT].rearrange("p (c t) -> p c t", c=16), ("hfA",))
            actt = Tile(hT_flat[:, 16 * TT:(16 + 44) * TT].rearrange("p (c t) -> p c t", c=44), ("actt",))
            hfB = b.alloc("hfB", [128, 16, TT], BF16)
            hfs = [hfA, hfB]
            xp = b.ring("xp", [128, 4, TT], F32, 3)
            rsn = b.alloc("rsn", [128, TT], F32)
            nsqr = b.ring("nsq", [128, TT], BF16, 8)
            nsqc = [0]
            sg = b.ring("sg", [128, TT], F32, 3)
            xpc = [0]
            sgi = 0

            def pf_norm(tt):
                return norm_steps(b, xT_d[s], slice(tt * TT, (tt + 1) * TT), ("xT", s, tt), [vcol(b, l, 48 + c) for c in range(16)],
                                  hfs[tt % 2], xp, xpc, rsn, K_VEC, nsqr, nsqc)

            for f in pf_norm(0):
                f()
            for tt in range(NTT):
                tsl = slice(tt * TT, (tt + 1) * TT)
                hf_t = hfs[tt % 2]
                nxt = pf_norm(tt + 1) if tt + 1 < NTT else []
                for blk in range(22):
                    wv, wk = b.wload(wgu_d[l, blk], 512)
                    for jj in range(2):
                        j = blk * 2 + jj
                        pg = b.psum_rot(0, 7)
                        pu = b.psum_rot(0, 7)
                        for k in range(16):
                            b.mm(pg.ap, wv[:, k, jj * 128:(jj + 1) * 128], hf_t.ap[:, k, :], k == 0, k == 15, [wk, hf_t.key], [pg.key])
                        for k in range(16):
                            b.mm(pu.ap, wv[:, k, 256 + jj * 128:256 + (jj + 1) * 128], hf_t.ap[:, k, :], k == 0, k == 15, [wk, hf_t.key], [pu.key])
                        s_ = sg[sgi % 3]
                        sgi += 1
                        b.act(s_.ap, pg.ap, AF.Silu, [pg.key], [s_.key])
                        b.tt("dve", actt.ap[:, j, :], s_.ap, pu.ap, ALU.mult, [s_.key, pu.key], [actt.key])
                    if blk >= 4 and blk % 2 == 0 and nxt:
                        nxt.pop(0)()
                for nb in range(4):
                    pss = [b.psum_rot(0, 7) for _ in range(4)]
                    for kp in range(3):
                        kc = 16 if kp < 2 else 12
                        wv, wk = b.wload(wdn_d[l][kp * 2048:kp * 2048 + kc * 128, nb * 512:(nb + 1) * 512], 512, kc=kc)
                        for j in range(4):
                            for k in range(kc):
                                kk = kp * 16 + k
                                b.mm(pss[j].ap, wv[:, k, j * 128:(j + 1) * 128], actt.ap[:, kk, :], kk == 0, kk == 43, [wk, actt.key], [pss[j].key])
                        if nxt:
                            nxt.pop(0)()
                    xq = xp[xpc[0] % 3]
                    xpc[0] += 1
                    b.dma("sp", xq.ap, xT_d[s, :, nb * 4:(nb + 1) * 4, tsl], [("xT", s, tt)], [xq.key])
                    for j in range(4):
                        b.tt("dve", xq.ap[:, j, :], xq.ap[:, j, :], pss[j].ap, ALU.add, [xq.key, pss[j].key], [xq.key])
                    b.dma("sp", xT_d[s, :, nb * 4:(nb + 1) * 4, tsl], xq.ap, [xq.key], [("xT", s, tt)])
                while nxt:
                    nxt.pop(0)()
            b.ps_limit = 8

    phase_begin()
    xi2 = b.ring("xi2", [128, 16, 256], F32, 2)
    xo2 = b.ring("xo2", [128, D], F32, 2)
    for s in range(NS):
        for tp in range(8):
            xi = xi2[tp % 2]
            b.dma("sp", xi.ap, xT_d[s, :, :, tp * 256:(tp + 1) * 256], [("xT", s, tp // 2)], [xi.key])
            for half in range(2):
                tc = tp * 2 + half
                xo = xo2[tc % 2]
                for g in range(4):
                    p = b.psum()
                    for j in range(4):
                        c = g * 4 + j
                        b.tr(p.ap[:, j * 128:(j + 1) * 128], xi.ap[:, c, half * 128:(half + 1) * 128], ident_f, [xi.key, K_CST], [p.key])
                    b.evac(xo.ap[:, g * 512:(g + 1) * 512], p.ap, [p.key], [xo.key])
                b.dma("sp", out_d[s, tc * 128:(tc + 1) * 128, :], xo.ap, [xo.key], [("out", s, tc)])

    S.emit(nc, sems, dsems)
    es.close()
    return nc, S


def t5_bucket_np(rel):
    n = np.maximum(rel, 0)
    exact = 16
    nf = np.maximum(n, exact).astype(np.float32)
    large = exact + (np.log(nf / np.float32(exact)) / np.float32(math.log(128 / exact)) * np.float32(32 - exact)).astype(np.int32)
    large = np.minimum(large, 31)
    return np.where(n < exact, n, large)


def make_consts():
    NC = 128 * 4 + 256 + 64 + 4
    c = np.zeros((128, NC), np.float32)
    c[:, 0:128] = np.eye(128, dtype=np.float32)
    j = np.arange(128)[:, None]
    k = np.arange(128)[None, :]
    c[:, 128:256] = np.where(j >= k, -1.0, 0.0)
    c[:, 256:384] = np.where(k >= j, 1.0, 0.0)
    c[:, 384:512] = np.where(k > j, 1.0, 0.0)
    c[:, 512:640] = np.where(j > k, 0.0, NEG)
    c[:, 640:768] = np.where(j <= k, 0.0, NEG)
    rr = np.zeros((64, 64), np.float32)
    for m in range(64):
        if m < 32:
            rr[m + 32, m] = -1.0
        else:
            rr[m - 32, m] = 1.0
    c[0:64, 768:832] = rr
    half = 32
    inv = (np.float32(10000.0) ** (-np.arange(half, dtype=np.float32) / np.float32(half))).astype(np.float32)
    c[0:64, 832] = np.concatenate([inv, inv])
    return c


def prep_shared(inp, NL):
    f = lambda a: np.ascontiguousarray(np.asarray(a, dtype=np.float32))
    sh = {}
    rel_bias = f(inp["rel_bias"])
    jj = np.arange(128)[:, None]
    ii = np.arange(128)[None, :]
    bt = np.zeros((128, 2, 8, 128), np.float32)
    for c in range(2):
        rel = 128 + ii - (jj + 128 * c)
        bk = t5_bucket_np(rel)
        bt[:, c, :, :] = np.transpose(rel_bias[bk], (0, 2, 1))
    sh["biasT"] = bt.reshape(128, -1)
    sh["sinks"] = np.ascontiguousarray(np.repeat(f(inp["swa_sinks"])[:NL, None, :, None], 128, axis=3).reshape(NL, 1, 1024))
    vecs = np.zeros((128, NL * NVEC), np.float32)
    for l in range(NL):
        o = l * NVEC
        vecs[:, o:o + 16] = f(inp["norm_mix"])[l].reshape(16, 128).T
        vecs[:, o + 16:o + 32] = f(inp["norm_xa"])[l].reshape(16, 128).T
        vecs[:, o + 32:o + 48] = f(inp["norm_mem"])[l].reshape(16, 128).T
        vecs[:, o + 48:o + 64] = f(inp["norm_ffn"])[l].reshape(16, 128).T
        vecs[:, o + 64] = f(inp["swa_q_norm"])[l]
        vecs[:, o + 65] = f(inp["swa_k_norm"])[l]
        vecs[:, o + 66] = f(inp["xa_q_norm"])[l]
        vecs[:, o + 67] = f(inp["xa_k_norm"])[l]
        vecs[:, o + 68:o + 72] = f(inp["mla_cq_norm"])[l].reshape(4, 128).T
        vecs[:, o + 72:o + 76] = f(inp["mla_ckv_norm"])[l].reshape(4, 128).T
        qg = f(inp["mla_q_norm"])[l]
        kg = f(inp["mla_k_norm"])[l]
        vecs[:, o + 76] = qg[:128]
        vecs[0:64, o + 77] = qg[128:]
        vecs[:, o + 78] = kg[:128]
        vecs[0:64, o + 79] = kg[128:]
    sh["vecs"] = vecs
    sh["consts"] = make_consts()
    w_in = f(inp["w_in"])[:NL]
    sh["w_in"] = np.ascontiguousarray(w_in[:, :, :IN_QKV])
    gates = w_in[:, :, IN_QKV:]
    wbr = f(inp["w_branch"])[:NL]
    wpc = np.empty((NL, 16, D, 512), np.float32)
    for n in range(16):
        for g3 in range(3):
            wpc[:, n, :, g3 * 128:(g3 + 1) * 128] = gates[:, :, g3 * 2048 + n * 128:g3 * 2048 + (n + 1) * 128]
        wpc[:, n, :, 384:512] = wbr[:, :, n * 128:(n + 1) * 128]
    sh["w_pc"] = wpc
    sh["w_out"] = f(inp["w_out"])[:NL]
    sh["w_uq"] = f(inp["mla_w_uq"])[:NL]
    sh["w_ukv"] = f(inp["mla_w_ukv"])[:NL]
    sh["xa_wq"] = f(inp["xa_wq"])[:NL]
    sh["xa_wkv"] = f(inp["xa_wkv"])[:NL]
    sh["xa_wo"] = f(inp["xa_wo"])[:NL]
    wgu = f(inp["ffn_w_gu"])[:NL]
    blk = np.empty((NL, 22, D, 512), np.float32)
    for bi in range(22):
        blk[:, bi, :, 0:256] = wgu[:, :, bi * 256:(bi + 1) * 256]
        blk[:, bi, :, 256:512] = wgu[:, :, D_FF + bi * 256:D_FF + (bi + 1) * 256]
    sh["w_gu"] = blk
    sh["w_down"] = f(inp["ffn_w_down"])[:NL]
    return sh


def prep_core(inp, bsel):
    x = np.ascontiguousarray(np.asarray(inp["x"], np.float32)[bsel])
    mem = np.ascontiguousarray(np.asarray(inp["mem"], np.float32)[bsel])
    pos = np.asarray(inp["positions"]).astype(np.int32)[bsel]
    pos = np.ascontiguousarray(np.repeat(pos[:, None, :], 64, axis=1))
    return {"x": x, "mem": mem, "pos": pos}


_CACHE = {}


def kernel(**inputs):
    n_cores = 8
    NS = 2
    key = (DEPTH, NS)
    if key not in _CACHE:
        _CACHE[key] = build_program(DEPTH, NS)[0]
    nc = _CACHE[key]
    sh = prep_shared(inputs, DEPTH)
    in_maps = []
    for c in range(n_cores):
        m = dict(sh)
        m.update(prep_core(inputs, slice(c * NS, (c + 1) * NS)))
        in_maps.append(m)
    res = run_bass_kernel_spmd(nc, in_maps, core_ids=list(range(n_cores)))
    out = np.concatenate([np.asarray(r["out"]) for r in res.results], axis=0)
    return out.astype(np.float32)
```

```python
import math
from contextlib import ExitStack

import numpy as np
import concourse.bass as bass
import concourse.mybir as mybir
from concourse.bass_utils import run_bass_kernel_spmd

F32 = mybir.dt.float32
BF16 = mybir.dt.bfloat16
I32 = mybir.dt.int32
U8 = mybir.dt.uint8
AF = mybir.ActivationFunctionType
ALU = mybir.AluOpType

ENGS = ("pe", "act", "dve", "pool", "sp")
DMA_RING = 16

D = 2048
SEQ = 2048
NTT = 4
TT = 512
DEPTH = 4
IN_QKV = 4160
D_FF = 5632
EPS = 1e-6
NVEC = 80
NEG = -30000.0


class Sched:
    def __init__(self):
        self.ops = []
        self.strict = set()
        self.nobar = set()

    def op(self, eng, fn, reads=(), writes=(), dma=False, strict=False, nobar=False):
        self.ops.append((eng, fn, tuple(reads), tuple(writes), dma))
        if strict:
            self.strict.add(len(self.ops) - 1)
        if nobar:
            self.nobar.add(len(self.ops) - 1)

    def barrier(self):
        self.ops.append(("bar", None, (), (), False))

    def emit(self, nc, sems, dma_sems):
        ops = self.ops
        n = len(ops)
        last_w = {}
        readers = {}
        deps = [None] * n
        last_on = {}
        last_dma = {}
        dctr = {e: 0 for e in ENGS}
        pending_bar = {}
        for i, (eng, fn, reads, writes, dma) in enumerate(ops):
            if eng == "bar":
                bd = set(last_on.values()) | set(last_dma.values())
                for e in ENGS:
                    pending_bar[e] = bd
                last_w = {k_: v_ for k_, v_ in last_w.items() if k_[0] == "wb"}
                readers = {k_: v_ for k_, v_ in readers.items() if k_[0] == "wb"}
                deps[i] = set()
                continue
            d = set()
            pb = None if i in self.nobar else pending_bar.pop(eng, None)
            if pb:
                d.update(pb)
            for r in reads:
                w = last_w.get(r)
                if w is not None:
                    d.add(w)
            for w_ in writes:
                w = last_w.get(w_)
                if w is not None:
                    d.add(w)
                rs = readers.get(w_)
                if rs:
                    d.update(rs)
            for r in reads:
                rl = readers.setdefault(r, [])
                if not dma:
                    for q in range(len(rl)):
                        if ops[rl[q]][0] == eng and not ops[rl[q]][4]:
                            rl[q] = i
                            break
                    else:
                        rl.append(i)
                else:
                    rl.append(i)
            for w_ in writes:
                last_w[w_] = i
                readers[w_] = []
            d.discard(i)
            deps[i] = d
            if dma:
                last_dma[(eng, dctr[eng] % DMA_RING)] = i
                dctr[eng] += 1
            else:
                last_on[eng] = i
        signal = [False] * n
        for i in range(n):
            ei = ops[i][0]
            if ei == "bar":
                continue
            for j in deps[i]:
                ej, _, _, _, dj = ops[j]
                if dj:
                    continue
                if ej != ei or i in self.strict:
                    signal[j] = True
        tok = [None] * n
        cnt = {e: 0 for e in ENGS}
        dcnt = {e: 0 for e in ENGS}
        dval = {}
        ring_prev = [None] * n
        ring_hist = {e: [] for e in ENGS}
        for i, (eng, fn, reads, writes, dma) in enumerate(ops):
            if eng == "bar":
                continue
            if dma:
                k = dcnt[eng]
                slot = k % DMA_RING
                dcnt[eng] = k + 1
                v = dval.get((eng, slot), 0) + 16
                dval[(eng, slot)] = v
                tok[i] = (("d", eng, slot), v)
                h = ring_hist[eng]
                if len(h) >= DMA_RING:
                    ring_prev[i] = h[-DMA_RING]
                h.append(i)
            elif signal[i]:
                cnt[eng] += 1
                tok[i] = (("e", eng), cnt[eng])
        known = {e: {} for e in ENGS}
        clock_at = {}
        streams = {e: [] for e in ENGS}
        for i, (eng, fn, reads, writes, dma) in enumerate(ops):
            if eng == "bar":
                continue
            kn = known[eng]
            need = {}
            dl = list(deps[i])
            if ring_prev[i] is not None:
                dl.append(ring_prev[i])
            strict_w = []
            for j in dl:
                ej, _, _, _, dj = ops[j]
                if (not dj) and ej == eng:
                    if i in self.strict:
                        strict_w.append(tok[j])
                    continue
                sk, v = tok[j]
                if kn.get(sk, 0) >= v:
                    continue
                if need.get(sk, 0) < v:
                    need[sk] = v
            waits = list(strict_w)
            for sk, v in need.items():
                if kn.get(sk, 0) >= v:
                    continue
                waits.append((sk, v))
                kn[sk] = v
                snap = clock_at.get((sk, v))
                if snap:
                    for a, b in snap.items():
                        if kn.get(a, 0) < b:
                            kn[a] = b
            t = tok[i]
            if t is not None:
                if not dma:
                    kn[t[0]] = t[1]
                clock_at[t] = dict(kn)
            streams[eng].append((waits, fn, t, dma))
        self.n_waits = sum(len(w) for e in ENGS for (w, _, _, _) in streams[e])
        self.n_signal = sum(1 for s in signal if s)
        self.max_tok = dict(cnt)
        final_tokens = dict(dval)

        def semof(sk):
            if sk[0] == "e":
                return sems[sk[1]]
            return dma_sems[(sk[1], sk[2])]

        handles = {"pe": "tensor", "act": "scalar", "dve": "vector", "pool": "gpsimd", "sp": "sync"}
        with nc.Block() as block:
            for e in ENGS:
                st = streams[e]
                final = e == "sp"

                def body(h, st=st, final=final):
                    for waits, fn, t, dma in st:
                        for sk, v in waits:
                            h.wait_ge(semof(sk), v)
                        ins = fn(h)
                        if t is not None:
                            ins.then_inc(semof(t[0]), 16 if dma else 1)
                    if final:
                        for (q, slot), v in final_tokens.items():
                            h.wait_ge(dma_sems[(q, slot)], v)

                if st or final:
                    getattr(block, handles[e])(body)


class Tile:
    __slots__ = ("ap", "key")

    def __init__(self, ap, key):
        self.ap = ap
        self.key = key


class Builder:
    def __init__(self, nc, S, es, NL, NS, debug):
        self.nc, self.S, self.es, self.NL, self.NS, self.debug = nc, S, es, NL, NS, debug
        self.uid = 0
        self.ps_i = 0
        self.rot_ctr = {}
        self.ps_limit = 8
        self.act_dve = 0

    def static(self, name, shape, dt):
        t = self.es.enter_context(self.nc.sbuf_tensor(name, shape, dt))
        return t

    def arena_reset(self):
        self.S.barrier()
        self.aoff = 0

    def alloc(self, name, shape, dt):
        bpe = 4 if dt in (F32, I32) else 2
        per = int(np.prod(shape[1:])) * bpe
        off = (self.aoff + 31) // 32 * 32
        assert off + per <= self.ARENA, (name, off, per, self.ARENA)
        self.aoff = off + per
        ap = self.arena[:, off:off + per].bitcast(dt)
        if len(shape) == 3:
            ap = ap.rearrange("p (a b) -> p a b", a=shape[1])
        elif len(shape) == 4:
            ap = ap.rearrange("p (a b c) -> p a b c", a=shape[1], b=shape[2])
        if shape[0] < 128:
            ap = ap[0:shape[0]]
        self.uid += 1
        return Tile(ap, (name, self.uid))

    def ring(self, name, shape, dt, n):
        return [self.alloc(f"{name}{i}", shape, dt) for i in range(n)]

    def psum(self):
        i = self.ps_i % self.ps_limit
        self.ps_i += 1
        return Tile(self.ps[i][:, :], ("ps", i))

    def psum_at(self, i):
        return Tile(self.ps[i][:, :], ("ps", i))

    def psum_rot(self, lo, n):
        c = self.rot_ctr.get((lo, n), 0)
        self.rot_ctr[(lo, n)] = c + 1
        i = lo + c % n
        return Tile(self.ps[i][:, :], ("ps", i))

    def pipeline(self, units, oldest_first=False):
        n = len(units)
        ns = max(len(u) for u in units)
        for t in range(n + ns - 1):
            for k in (range(ns - 1, -1, -1) if oldest_first else range(ns)):
                u = t - k
                if 0 <= u < n and k < len(units[u]) and units[u][k] is not None:
                    units[u][k]()

    def norm_unit(self, mm_fn, srcs_fn, gains, outs, Dn, P=128, N=TT, extra_reads=(), post=None, pslo=0, psn=4, sslo=4, ssn=4):
        st = {}

        def s0():
            p = self.psum_rot(pslo, psn) if mm_fn is not None else None
            if mm_fn is not None:
                mm_fn(p)
            srcs = srcs_fn(p)
            sqs = []
            for (sap, skey) in srcs:
                sq = self.sqring[self.sq_i % len(self.sqring)]
                self.sq_i += 1
                self.act(sq.ap[0:P, 0:N], sap, AF.Square, [skey], [sq.key])
                sqs.append(sq)
            st["srcs"], st["sqs"] = srcs, sqs

        def s1():
            ss = self.psum_rot(sslo, ssn)
            sqs = st["sqs"]
            for c, sq in enumerate(sqs):
                self.mm(ss.ap[0:P, 0:N], self.ones_bf[0:P, 0:P], sq.ap[0:P, 0:N], c == 0, c == len(sqs) - 1, [sq.key], [ss.key])
            rs = self.rsring[self.rs_i % len(self.rsring)]
            self.rs_i += 1
            self.act(rs.ap[0:P, 0:N], ss.ap[0:P, 0:N], AF.Ln, [ss.key], [rs.key], scale=1.0 / Dn, bias=self.eps_col[0:P, :])
            self.act(rs.ap[0:P, 0:N], rs.ap[0:P, 0:N], AF.Exp, [rs.key], [rs.key], scale=-0.5)
            st["rs"] = rs

        def s2():
            rs = st["rs"]
            for (sap, skey), g, (oap, okey) in zip(st["srcs"], gains, outs):
                self.stt(oap, sap, g, rs.ap[0:P, 0:N], ALU.mult, ALU.mult, [skey, rs.key] + list(extra_reads), [okey])
            if post is not None:
                post()

        return [s0, s1, s2]

    def mm(self, out, lhsT, rhs, start, stop, reads, writes):
        self.S.op("pe", lambda t: t.matmul(out, lhsT=lhsT, rhs=rhs, start=start, stop=stop), reads, writes)

    def tr(self, out, in_, ident, reads, writes):
        self.S.op("pe", lambda t: t.transpose(out, in_, ident), reads, writes)

    def act(self, out, in_, func, reads, writes, scale=1.0, bias=None, accum_out=None, strict=False):
        kw = {}
        if bias is not None:
            kw["bias"] = bias
        if accum_out is not None:
            kw["accum_out"] = accum_out
        self.S.op("act", lambda a: a.activation(out, in_, func, scale=scale, **kw), reads, writes, strict=strict)

    def tt(self, eng, out, in0, in1, op, reads, writes):
        self.S.op(eng, lambda v: v.tensor_tensor(out, in0, in1, op), reads, writes)

    def ts(self, eng, out, in0, s1, s2, op0, op1, reads, writes):
        if op1 is None:
            self.S.op(eng, lambda v: v.tensor_scalar(out, in0, s1, None, op0), reads, writes)
        else:
            self.S.op(eng, lambda v: v.tensor_scalar(out, in0, s1, s2, op0, op1), reads, writes)

    def stt(self, out, in0, scalar, in1, op0, op1, reads, writes):
        self.S.op("dve", lambda v: v.scalar_tensor_tensor(out, in0, scalar, in1, op0, op1), reads, writes)

    def copy(self, eng, out, in_, reads, writes):
        if eng == "act":
            self.S.op("act", lambda a: a.activation(out, in_, AF.Copy), reads, writes)
        else:
            self.S.op(eng, lambda v: v.tensor_copy(out, in_), reads, writes)

    def evac(self, out, in_, reads, writes):
        self.act_dve += 1
        self.copy("act" if self.act_dve % 2 else "dve", out, in_, reads, writes)

    def recip(self, out, in_, reads, writes):
        self.S.op("dve", lambda v: v.reciprocal(out, in_), reads, writes)

    def dma(self, q, out, in_, reads, writes, nobar=False):
        self.S.op(q, lambda g: g.dma_start(out=out, in_=in_), reads, writes, dma=True, nobar=nobar)

    def pnorm(self, srcs, gains, outs, Dn, P=128, N=TT, extra_reads=(), post_scale=1.0):
        ss = self.psum()
        nsrc = len(srcs)
        for c, (sap, skey) in enumerate(srcs):
            sq = self.sqring[self.sq_i % len(self.sqring)]
            self.sq_i += 1
            self.act(sq.ap[0:P, 0:N], sap, AF.Square, [skey], [sq.key])
            self.mm(ss.ap[0:P, 0:N], self.ones_bf[0:P, 0:P], sq.ap[0:P, 0:N], c == 0, c == nsrc - 1,
                    [sq.key], [ss.key])
        rs = self.rsring[self.rs_i % len(self.rsring)]
        self.rs_i += 1
        self.act(rs.ap[0:P, 0:N], ss.ap[0:P, 0:N], AF.Ln, [ss.key], [rs.key], scale=1.0 / Dn, bias=self.eps_col[0:P, :])
        if post_scale != 1.0:
            self.act(rs.ap[0:P, 0:N], rs.ap[0:P, 0:N], AF.Exp, [rs.key], [rs.key], scale=-0.5,
                     bias=self.lnps_col[0:P, :])
        else:
            self.act(rs.ap[0:P, 0:N], rs.ap[0:P, 0:N], AF.Exp, [rs.key], [rs.key], scale=-0.5)
        for (sap, skey), g, (oap, okey) in zip(srcs, gains, outs):
            self.stt(oap, sap, g, rs.ap[0:P, 0:N], ALU.mult, ALU.mult, [skey, rs.key] + list(extra_reads), [okey])

    def wload(self, src_ap, ncols, kc=16):
        i = self.w_i % len(self.wb)
        self.w_i += 1
        t = self.wb[i]
        view = t.ap[:, 0:kc * ncols].rearrange("p (c n) -> p c n", c=kc)
        self.dma("pool", view, src_ap.rearrange("(c p) n -> p c n", p=128), [], [t.key], nobar=True)
        return view, t.key


def norm_steps(b, x_d, tsl, xkey, gains, hf, xp, xpc, rsn, K_VEC, sqr, sqc):
    ss = b.psum_at(7)
    st = {}

    def p1a(g):
        def f():
            xq = xp[xpc[0] % len(xp)]
            xpc[0] += 1
            b.dma("sp", xq.ap, x_d[:, g * 4:(g + 1) * 4, tsl], [xkey], [xq.key])
            sqs = []
            for j in range(4):
                sq = sqr[sqc[0] % len(sqr)]
                sqc[0] += 1
                b.act(sq.ap, xq.ap[:, j, :], AF.Square, [xq.key], [sq.key])
                sqs.append(sq)
            st[g] = sqs
        return f

    def p1b(g):
        def f():
            for j, sq in enumerate(st[g]):
                b.mm(ss.ap, b.ones_bf, sq.ap, g == 0 and j == 0, g == 3 and j == 3, [sq.key], [ss.key])
        return f

    def pr():
        b.act(rsn.ap, ss.ap, AF.Ln, [ss.key], [rsn.key], scale=1.0 / D, bias=b.eps_col)
        b.act(rsn.ap, rsn.ap, AF.Exp, [rsn.key], [rsn.key], scale=-0.5)

    def p2(g):
        def f():
            xq = xp[xpc[0] % len(xp)]
            xpc[0] += 1
            b.dma("sp", xq.ap, x_d[:, g * 4:(g + 1) * 4, tsl], [xkey], [xq.key])
            for j in range(4):
                c = g * 4 + j
                b.stt(hf.ap[:, c, :], xq.ap[:, j, :], gains[c], rsn.ap, ALU.mult, ALU.mult, [xq.key, rsn.key, K_VEC], [hf.key])
        return f

    def seq(*fs):
        def f():
            for x in fs:
                x()
        return f

    steps = [p1a(0), seq(p1a(1), p1b(0)), seq(p1a(2), p1b(1)), seq(p1a(3), p1b(2)), seq(p1b(3), pr)]
    steps += [p2(g) for g in range(4)]
    return steps


def vcol(b, l, j):
    return b.vec[:, l * NVEC + j:l * NVEC + j + 1]


def build_program(NL=DEPTH, NS=2, debug=False):
    nc = bass.Bass("TRN2", target_bir_lowering=False)
    S = Sched()
    es = ExitStack()
    b = Builder(nc, S, es, NL, NS, debug)

    def din(name, shape, dt=F32):
        return nc.dram_tensor(name, list(shape), dt, kind="ExternalInput").ap()

    def dscr(name, shape, dt):
        return nc.dram_tensor(name, list(shape), dt, kind=("ExternalOutput" if debug else "Internal")).ap()

    x_d = din("x", [NS, SEQ, D])
    mem_d = din("mem", [NS, 256, D])
    pos_d = din("pos", [NS, 64, SEQ], I32)
    biasT_d = din("biasT", [128, 2 * 8 * 128])
    sinks_d = din("sinks", [NL, 1, 1024])
    vecs_d = din("vecs", [128, NL * NVEC])
    NCST = 128 * 4 + 256 + 64 + 4
    cst_d = din("consts", [128, NCST])
    win_d = din("w_in", [NL, D, IN_QKV])
    wpc_d = din("w_pc", [NL, 16, D, 512])
    wout_d = din("w_out", [NL, D, D])
    wuq_d = din("w_uq", [NL, 512, 768])
    wukv_d = din("w_ukv", [NL, 512, 1024])
    xwq_d = din("xa_wq", [NL, D, 512])
    xwkv_d = din("xa_wkv", [NL, D, 1024])
    xwo_d = din("xa_wo", [NL, 512, D])
    wgu_d = din("w_gu", [NL, 22, D, 512])
    wdn_d = din("w_down", [NL, D_FF, D])
    out_d = nc.dram_tensor("out", [NS, SEQ, D], F32, kind="ExternalOutput").ap()

    xT_d = dscr("xT", [NS, 128, 16, SEQ], F32)
    memT_d = dscr("memT", [NS, 128, 16, 256], F32)
    rope_d = dscr("ropeT", [NS, 2, 64, SEQ], F32)
    QA_d = dscr("QA", [8, 128, SEQ], BF16)
    KA_d = dscr("KA", [2, 128, SEQ], BF16)
    VA_d = dscr("VA", [SEQ, 256], BF16)
    QB_d = dscr("QB", [4, 128, SEQ], BF16)
    KB_d = dscr("KB", [4, 128, SEQ], BF16)
    VB_d = dscr("VB", [SEQ, 512], BF16)
    CQ_d = dscr("CQ", [4, 128, SEQ], F32)
    CKV_d = dscr("CKV", [4, 128, SEQ], F32)
    KRr_d = dscr("KRr", [64, SEQ], F32)
    QN_d = dscr("QN", [4, 128, SEQ], BF16)
    QR_d = dscr("QR", [4, 64, SEQ], BF16)
    KN_d = dscr("KN", [4, 128, SEQ], BF16)
    KR_d = dscr("KR", [64, SEQ], BF16)
    VC_d = dscr("VC", [SEQ, 512], BF16)
    OT_d = dscr("OT", [16, 128, SEQ], BF16)

    if debug:
        dbg1_d = dscr("dbg_x1", [128, 16, SEQ], F32)
        dbg2_d = dscr("dbg_x2", [128, 16, SEQ], F32)
    cst = b.static("cst", [128, NCST], F32)
    vec = b.static("vec", [128, NL * NVEC], F32)
    b.vec = vec
    cbf = b.static("cbf", [128, 128 * 5], BF16)
    onesneg = b.static("onesneg", [128, 128], BF16)
    biasM = b.static("biasM", [128, 2, 8, 128], F32)
    eskhl = b.static("eskhl", [1, 2048], BF16)
    small = b.static("small", [128, 8], F32)
    hT_flat = b.static("hT", [128, 4 * 16 * TT], BF16)
    wb_t = [b.static(f"wb{i}", [128, 16 * 512], BF16) for i in range(3)]
    b.wb = [Tile(t[:, :], ("wb", i)) for i, t in enumerate(wb_t)]
    b.w_i = 0
    used = NCST * 4 + NL * NVEC * 4 + 640 * 2 + 256 + 8192 + 4096 + 32 + 65536 + 3 * 16384
    b.ARENA = (nc.sbuf_bytes_remaining - 1024) // 32 * 32
    b.arena = b.static("arena", [128, b.ARENA], U8)
    b.aoff = 0
    b.ps = [es.enter_context(nc.psum_tensor(f"ps{i}", [128, 512], F32)) for i in range(8)]
    sems = {e: es.enter_context(nc.semaphore(f"s_{e}")) for e in ENGS}
    dsems = {(e, k): es.enter_context(nc.semaphore(f"d_{e}{k}")) for e in ("sp", "pool", "act") for k in range(DMA_RING)}

    ident_f = cst[:, 0:128]
    uineg_f = cst[:, 128:256]
    tri_incl_f = cst[:, 256:384]
    tri_strict_f = cst[:, 384:512]
    maskneg_f = cst[:, 512:768]
    rrot_f = cst[0:64, 768:832]
    invf = cst[0:64, 832:833]
    ident_bf = cbf[:, 0:128]
    b.ones_bf = cbf[:, 128:256]
    uineg_bf = cbf[:, 256:384]
    tri_incl_bf = cbf[:, 384:512]
    tri_strict_bf = cbf[:, 512:640]
    b.eps_col = small[:, 0:1]
    b.lnps_col = small[:, 1:2]
    b.one_col = small[:, 2:3]
    K_CST, K_VEC, K_CBF = ("cst",), ("vec",), ("cbf",)

    hT = [Tile(hT_flat[:, tt * 16 * TT:(tt + 1) * 16 * TT].rearrange("p (c t) -> p c t", c=16), ("hT", tt))
          for tt in range(NTT)]

    def phase_begin(nsq=4, nrs=3):
        b.arena_reset()
        b.sqring = b.ring("sq", [128, TT], BF16, nsq)
        b.rsring = b.ring("rs", [128, TT], F32, nrs)
        b.sq_i = 0
        b.rs_i = 0

    phase_begin()
    b.dma("sp", cst[:, :], cst_d, [], [K_CST])
    b.dma("sp", vec[:, :], vecs_d, [], [K_VEC])
    S.op("dve", lambda v: v.memset(small[:, 0:1], EPS), [], [("small",)])
    S.op("dve", lambda v: v.memset(small[:, 1:2], 0.0), [], [("small",)])
    S.op("dve", lambda v: v.memset(small[:, 2:3], 1.0), [], [("small",)])
    S.op("dve", lambda v: v.tensor_copy(cbf[:, 0:128], ident_f), [K_CST], [K_CBF])
    S.op("dve", lambda v: v.memset(cbf[:, 128:256], 1.0), [], [K_CBF])
    S.op("dve", lambda v: v.memset(onesneg[:, :], -1.0), [], [K_CBF])
    S.op("dve", lambda v: v.tensor_copy(cbf[:, 256:640], cst[:, 128:512]), [K_CST], [K_CBF])
    bt = b.alloc("biasT", [128, 2, 8, 128], F32)
    b.dma("sp", bt.ap.rearrange("p a b c -> p (a b c)"), biasT_d, [], [bt.key])
    for c in range(2):
        mk = maskneg_f[:, c * 128:(c + 1) * 128].unsqueeze(1).broadcast_to([128, 8, 128])
        b.tt("dve", biasM[:, c, :, :], bt.ap[:, c, :, :], mk, ALU.add, [bt.key, K_CST], [("biasM",)])

    phase_begin()
    xin = b.ring("xin", [128, D], F32, 2)
    xst = b.ring("xst", [128, 16, 256], F32, 2)
    for s in range(NS):
        for tp in range(8):
            xo = xst[tp % 2]
            for half in range(2):
                tc = tp * 2 + half
                xi = xin[tc % 2]
                b.dma("sp", xi.ap, x_d[s, tc * 128:(tc + 1) * 128, :], [], [xi.key])
                for g in range(4):
                    p = b.psum()
                    for j in range(4):
                        c = g * 4 + j
                        b.tr(p.ap[:, j * 128:(j + 1) * 128], xi.ap[:, c * 128:(c + 1) * 128], ident_f, [xi.key, K_CST], [p.key])
                    b.evac(xo.ap[:, g * 4:(g + 1) * 4, half * 128:(half + 1) * 128], p.ap.rearrange("p (a b) -> p a b", a=4), [p.key], [xo.key])
            b.dma("sp", xT_d[s, :, :, tp * 256:(tp + 1) * 256], xo.ap, [xo.key], [("xT", s, tp // 2)])
        for mc in range(2):
            xi = xin[mc % 2]
            xo = xst[mc % 2]
            b.dma("sp", xi.ap, mem_d[s, mc * 128:(mc + 1) * 128, :], [], [xi.key])
            junk = b.alloc("junk", [128, D], BF16) if (s == 0 and mc == 0) else junk
            ssq = b.alloc("ssq", [128, 4], F32) if (s == 0 and mc == 0) else ssq
            b.act(junk.ap, xi.ap, AF.Square, [xi.key], [junk.key, ssq.key], accum_out=ssq.ap[:, 0:1])
            b.act(ssq.ap[:, 1:2], ssq.ap[:, 0:1], AF.Ln, [ssq.key], [ssq.key], scale=1.0 / D, bias=b.eps_col, strict=True)
            b.act(ssq.ap[:, 2:3], ssq.ap[:, 1:2], AF.Exp, [ssq.key], [ssq.key], scale=-0.5, strict=True)
            b.ts("dve", xi.ap, xi.ap, ssq.ap[:, 2:3], None, ALU.mult, None, [xi.key, ssq.key], [xi.key])
            for g in range(4):
                p = b.psum()
                for j in range(4):
                    c = g * 4 + j
                    b.tr(p.ap[:, j * 128:(j + 1) * 128], xi.ap[:, c * 128:(c + 1) * 128], ident_f, [xi.key, K_CST], [p.key])
                b.evac(xo.ap[:, g * 4:(g + 1) * 4, 0:128], p.ap.rearrange("p (a b) -> p a b", a=4), [p.key], [xo.key])
            b.dma("sp", memT_d[s, :, :, mc * 128:(mc + 1) * 128], xo.ap[:, :, 0:128], [xo.key], [("memT", s)])
        if s == 0:
            QS = 512
            posi = b.alloc("posi", [64, QS], I32)
            ang = b.alloc("ang", [64, QS], F32)
            kf = b.alloc("kf", [64, QS], F32)
            ki = b.alloc("ki", [64, QS], I32)
            rr = b.alloc("rr", [64, QS], F32)
            msk = b.alloc("msk", [64, QS], F32)
        C1 = 6.28125
        C2 = 2 * math.pi - C1
        for qq in range(SEQ // QS):
            qsl = slice(qq * QS, (qq + 1) * QS)
            b.dma("sp", posi.ap, pos_d[s][:, qsl], [], [posi.key])
            b.copy("dve", ang.ap, posi.ap, [posi.key], [ang.key])
            b.ts("dve", ang.ap, ang.ap, invf, None, ALU.mult, None, [ang.key, K_CST], [ang.key])
            for which, shift in ((1, 0.0), (0, math.pi / 2)):
                b.ts("dve", kf.ap, ang.ap, shift, 1.0 / (2 * math.pi), ALU.add, ALU.mult, [ang.key], [kf.key])
                b.copy("dve", ki.ap, kf.ap, [kf.key], [ki.key])
                b.copy("dve", kf.ap, ki.ap, [ki.key], [kf.key])
                b.stt(rr.ap, kf.ap, -C1, ang.ap, ALU.mult, ALU.add, [kf.key, ang.key], [rr.key])
                b.stt(rr.ap, kf.ap, -C2, rr.ap, ALU.mult, ALU.add, [kf.key, rr.key], [rr.key])
                if shift:
                    b.ts("dve", rr.ap, rr.ap, shift, None, ALU.add, None, [rr.key], [rr.key])
                b.ts("dve", msk.ap, rr.ap, math.pi, -2 * math.pi, ALU.is_gt, ALU.mult, [rr.key], [msk.key])
                b.tt("dve", rr.ap, rr.ap, msk.ap, ALU.add, [rr.key, msk.key], [rr.key])
                b.ts("dve", msk.ap, rr.ap, -math.pi, 2 * math.pi, ALU.is_lt, ALU.mult, [rr.key], [msk.key])
                b.tt("dve", rr.ap, rr.ap, msk.ap, ALU.add, [rr.key, msk.key], [rr.key])
                b.ts("dve", rr.ap, rr.ap, math.pi, -math.pi, ALU.min, ALU.max, [rr.key], [rr.key])
                b.act(msk.ap, rr.ap, AF.Sin, [rr.key], [msk.key])
                b.dma("sp", rope_d[s, which][:, qsl], msk.ap, [msk.key], [("rope", s)])

    for l in range(NL):
        phase_begin()
        sk = b.alloc("sk", [1, 1024], F32)
        b.dma("sp", sk.ap, sinks_d[l], [], [sk.key])
        b.act(sk.ap, sk.ap, AF.Exp, [sk.key], [sk.key])
        skh = b.alloc("skh", [1, 1024], F32)
        b.copy("dve", eskhl[0:1, 0:1024], sk.ap, [sk.key], [("eskhl",)])
        b.copy("dve", skh.ap, eskhl[0:1, 0:1024], [("eskhl",)], [skh.key])
        b.tt("dve", eskhl[0:1, 1024:2048], sk.ap, skh.ap, ALU.subtract, [sk.key, skh.key], [("eskhl",)])

        for s in range(NS):
            phase_begin()
            xt = b.alloc("xt", [128, 16, TT], F32)
            for tt in range(NTT):
                for hf in range(2):
                    b.dma("sp", xt.ap[:, hf * 8:(hf + 1) * 8, :], xT_d[s, :, hf * 8:(hf + 1) * 8, tt * TT:(tt + 1) * TT],
                          [("xT", s, tt)], [xt.key])
                b.pnorm([(xt.ap[:, c, :], xt.key) for c in range(16)], [vcol(b, l, c) for c in range(16)],
                        [(hT[tt].ap[:, c, :], hT[tt].key) for c in range(16)], D, extra_reads=[K_VEC])
            stg = b.ring("stg", [128, TT], BF16, 4)
            stf = b.ring("stf", [128, TT], F32, 3)
            si = [0, 0]
            blocks = [(0, 512, "qa", 0), (512, 512, "qa", 4), (1024, 256, "ka", 0), (1536, 512, "qb", 0),
                      (2048, 512, "kb", 0), (3072, 512, "cq", 0), (3584, 512, "ckv", 0), (4096, 64, "kr", 0)]
            for (c0, ncols, kind, hbase) in blocks:
                wv, wk = b.wload(win_d[l][:, c0:c0 + ncols], ncols)
                for j in range(max(1, ncols // 128)):
                    M = min(128, ncols)
                    for tt in range(NTT):
                        p = b.psum()
                        for k in range(16):
                            b.mm(p.ap[0:M, :], wv[:, k, j * 128:j * 128 + M], hT[tt].ap[:, k, :], k == 0, k == 15,
                                 [wk, hT[tt].key], [p.key])
                        tsl = slice(tt * TT, (tt + 1) * TT)
                        if kind in ("qa", "ka"):
                            o = stg[si[0] % 4]
                            si[0] += 1
                            b.pnorm([(p.ap, p.key)], [vcol(b, l, 64 if kind == "qa" else 65)], [(o.ap, o.key)], 128,
                                    extra_reads=[K_VEC])
                            dst = QA_d[hbase + j] if kind == "qa" else KA_d[j]
                            b.dma("sp", dst[:, tsl], o.ap, [o.key], [(kind, hbase + j, tt)])
                        elif kind in ("qb", "kb"):
                            o = stg[si[0] % 4]
                            si[0] += 1
                            if kind == "qb":
                                b.act(o.ap, p.ap, AF.Copy, [p.key], [o.key], scale=128 ** -0.5)
                            else:
                                b.evac(o.ap, p.ap, [p.key], [o.key])
                            dst = QB_d[j] if kind == "qb" else KB_d[j]
                            b.dma("sp", dst[:, tsl], o.ap, [o.key], [(kind, j, tt)])
                        else:
                            o = stf[si[1] % 3]
                            si[1] += 1
                            b.evac(o.ap[0:M, :], p.ap[0:M, :], [p.key], [o.key])
                            dst = {"cq": CQ_d, "ckv": CKV_d}.get(kind)
                            if kind == "kr":
                                b.dma("sp", KRr_d[:, tsl], o.ap[0:64, :], [o.key], [("krr", tt)])
                            else:
                                b.dma("sp", dst[j][:, tsl], o.ap, [o.key], [(kind, j, tt)])
            for (c0, ncols, dst, nm) in ((1280, 256, VA_d, "va"), (2560, 512, VB_d, "vb")):
                wv, wk = b.wload(win_d[l][:, c0:c0 + ncols], ncols)
                for tc in range(16):
                    p = b.psum()
                    for k in range(16):
                        b.mm(p.ap[:, 0:ncols], hT[tc // 4].ap[:, k, (tc % 4) * 128:(tc % 4 + 1) * 128], wv[:, k, :],
                             k == 0, k == 15, [wk, hT[tc // 4].key], [p.key])
                    o = stg[si[0] % 4]
                    si[0] += 1
                    b.evac(o.ap[:, 0:ncols], p.ap[:, 0:ncols], [p.key], [o.key])
                    b.dma("sp", dst[tc * 128:(tc + 1) * 128, :], o.ap[:, 0:ncols], [o.key], [(nm, tc // 4)])

            phase_begin(nsq=9, nrs=3)
            wuq = b.alloc("wuq", [128, 4, 768], BF16)
            wukv = b.alloc("wukv", [128, 4, 1024], BF16)
            b.dma("pool", wuq.ap, wuq_d[l].rearrange("(c p) n -> p c n", p=128), [], [wuq.key])
            b.dma("pool", wukv.ap, wukv_d[l].rearrange("(c p) n -> p c n", p=128), [], [wukv.key])
            cqr = b.alloc("cqr", [128, 4, TT], F32)
            ckvr = b.alloc("ckvr", [128, 4, TT], F32)
            krr = b.alloc("krr", [64, TT], F32)
            cs = b.alloc("cs", [64, 2, TT], F32)
            cqn = b.alloc("cqn", [128, 4, TT], BF16)
            ckvn = b.alloc("ckvn", [128, 4, TT], BF16)
            stg = b.ring("stg", [128, TT], BF16, 5)
            rn = b.ring("rn", [64, TT], F32, 3)
            r1 = b.ring("r1", [64, TT], F32, 2)
            sic = [0, 0, 0]

            def rope_post(src, dst_ap, dkey):
                def f():
                    p = b.psum_rot(0, 4)
                    b.mm(p.ap[0:64, :], rrot_f, src.ap, True, True, [src.key, K_CST], [p.key])
                    t1 = r1[sic[1] % 2]
                    sic[1] += 1
                    b.tt("dve", t1.ap, src.ap, cs.ap[:, 0, :], ALU.mult, [src.key, cs.key], [t1.key])
                    b.tt("dve", src.ap, p.ap[0:64, :], cs.ap[:, 1, :], ALU.mult, [p.key, cs.key, src.key], [src.key])
                    o = stg[sic[0] % 5]
                    sic[0] += 1
                    b.tt("dve", o.ap[0:64, :], t1.ap, src.ap, ALU.add, [t1.key, src.key], [o.key])
                    b.dma("sp", dst_ap, o.ap[0:64, :], [o.key], [dkey])
                return f

            for tt in range(NTT):
                tsl = slice(tt * TT, (tt + 1) * TT)
                b.dma("sp", cqr.ap, CQ_d[:, :, tsl].rearrange("c p t -> p c t"), [("cq", c, tt) for c in range(4)], [cqr.key])
                b.dma("sp", ckvr.ap, CKV_d[:, :, tsl].rearrange("c p t -> p c t"), [("ckv", c, tt) for c in range(4)], [ckvr.key])
                b.dma("sp", krr.ap, KRr_d[:, tsl], [("krr", tt)], [krr.key])
                b.dma("sp", cs.ap, rope_d[s, :, :, tsl].rearrange("w p t -> p w t"), [("rope", s)], [cs.key])
                units = []
                units.append(b.norm_unit(None, lambda p: [(cqr.ap[:, c, :], cqr.key) for c in range(4)],
                                         [vcol(b, l, 68 + c) for c in range(4)], [(cqn.ap[:, c, :], cqn.key) for c in range(4)], 512,
                                         extra_reads=[K_VEC]))
                units.append(b.norm_unit(None, lambda p: [(ckvr.ap[:, c, :], ckvr.key) for c in range(4)],
                                         [vcol(b, l, 72 + c) for c in range(4)], [(ckvn.ap[:, c, :], ckvn.key) for c in range(4)], 512,
                                         extra_reads=[K_VEC]))
                qk = rn[2]
                units.append(b.norm_unit(None, lambda p: [(krr.ap, krr.key)], [vcol(b, l, 79)[0:64, :]], [(qk.ap, qk.key)], 64, P=64,
                                         extra_reads=[K_VEC], post=rope_post(qk, KR_d[:, tsl], ("kr", tt))))
                b.pipeline(units)
                units = []
                for h in range(4):
                    def mm_qn(p, h=h):
                        for c in range(4):
                            b.mm(p.ap, wuq.ap[:, c, h * 192:h * 192 + 128], cqn.ap[:, c, :], c == 0, c == 3, [wuq.key, cqn.key], [p.key])

                    def mm_qr(p, h=h):
                        for c in range(4):
                            b.mm(p.ap[0:64, :], wuq.ap[:, c, h * 192 + 128:h * 192 + 192], cqn.ap[:, c, :], c == 0, c == 3,
                                 [wuq.key, cqn.key], [p.key])

                    def mm_kn(p, h=h):
                        for c in range(4):
                            b.mm(p.ap, wukv.ap[:, c, h * 256:h * 256 + 128], ckvn.ap[:, c, :], c == 0, c == 3, [wukv.key, ckvn.key], [p.key])

                    o1 = stg[sic[0] % 5]
                    sic[0] += 1
                    units.append(b.norm_unit(mm_qn, lambda p: [(p.ap, p.key)], [vcol(b, l, 76)], [(o1.ap, o1.key)], 128, extra_reads=[K_VEC],
                                             post=(lambda o1=o1, h=h: b.dma("sp", QN_d[h][:, tsl], o1.ap, [o1.key], [("qn", h, tt)]))))
                    q = rn[h % 2]
                    units.append(b.norm_unit(mm_qr, lambda p: [(p.ap[0:64, :], p.key)], [vcol(b, l, 77)[0:64, :]], [(q.ap, q.key)], 64, P=64,
                                             extra_reads=[K_VEC], post=rope_post(q, QR_d[h][:, tsl], ("qr", h, tt))))
                    o2 = stg[sic[0] % 5]
                    sic[0] += 1
                    units.append(b.norm_unit(mm_kn, lambda p: [(p.ap, p.key)], [vcol(b, l, 78)], [(o2.ap, o2.key)], 128, extra_reads=[K_VEC],
                                             post=(lambda o2=o2, h=h: b.dma("sp", KN_d[h][:, tsl], o2.ap, [o2.key], [("kn", h, tt)]))))
                for tq in range(4):
                    def vunit(tq=tq):
                        p = b.psum_rot(0, 4)
                        for c in range(4):
                            rhs = wukv.ap[:, c, :].rearrange("p (h two d) -> p h two d", h=4, two=2)[:, :, 1, :]
                            b.mm(p.ap.rearrange("p (h d) -> p h d", h=4), ckvn.ap[:, c, tq * 128:(tq + 1) * 128], rhs, c == 0, c == 3,
                                 [wukv.key, ckvn.key], [p.key])
                        o = stg[sic[0] % 5]
                        sic[0] += 1
                        b.evac(o.ap, p.ap, [p.key], [o.key])
                        r0 = (tt * 4 + tq) * 128
                        b.dma("sp", VC_d[r0:r0 + 128, :], o.ap, [o.key], [("vc", tt)])
                    units.append([vunit])
                b.pipeline(units)

            phase_begin()
            Qt = b.ring("Qt", [128, 8, TT], BF16, 2)
            Kt = b.ring("Kt", [128, 2, 640], BF16, 2)
            Vt = b.ring("Vt", [128, 5, 256], BF16, 2)
            OAt = b.ring("OAt", [128, 8, TT], BF16, 2)
            tf = b.ring("tf", [128, TT], F32, 4)
            Pt = b.ring("Pt", [128, TT], BF16, 6)
            rd = b.ring("rd", [128, TT], F32, 2)
            cic = [0, 0]
            scale_a = 128 ** -0.5
            units = []
            for tt in range(NTT):
                for bl in range(4):
                    for kvh in range(2):
                        st = {}

                        def s0(tt=tt, bl=bl, kvh=kvh, st=st):
                            tsl = slice(tt * TT, (tt + 1) * TT)
                            qt, kt, vt, oa = Qt[tt % 2], Kt[tt % 2], Vt[tt % 2], OAt[tt % 2]
                            t0 = tt * TT - 128 if tt > 0 else 0
                            ln = (tt + 1) * TT - t0
                            if bl == 0 and kvh == 0:
                                b.dma("sp", qt.ap, QA_d[:, :, tsl].rearrange("h p t -> p h t"), [("qa", h, tt) for h in range(8)], [qt.key])
                                b.dma("sp", kt.ap[:, :, 0:ln], KA_d[:, :, t0:t0 + ln].rearrange("h p t -> p h t"),
                                      [("ka", h, t_) for h in range(2) for t_ in range(max(0, tt - 1), tt + 1)], [kt.key])
                                b.dma("sp", vt.ap[:, 0:ln // 128, :], VA_d[t0:t0 + ln, :].rearrange("(c p) v -> p c v", p=128),
                                      [("va", t_) for t_ in range(max(0, tt - 1), tt + 1)], [vt.key])
                            gb = tt * 4 + bl
                            cur = gb * 128 - t0
                            chunks = ([] if gb == 0 else [(0, cur - 128)]) + [(1, cur)]
                            pts = []
                            for (c, off) in chunks:
                                sp_ = b.psum_rot(0, 4)
                                b.mm(sp_.ap.rearrange("p (h i) -> p h i", h=4), kt.ap[:, kvh, off:off + 128],
                                     qt.ap[:, kvh * 4:(kvh + 1) * 4, bl * 128:(bl + 1) * 128], True, True, [kt.key, qt.key], [sp_.key])
                                t_ = tf[cic[0] % 4]
                                pt = Pt[cic[0] % 6]
                                cic[0] += 1
                                b.stt(t_.ap.rearrange("p (h i) -> p h i", h=4), sp_.ap.rearrange("p (h i) -> p h i", h=4), scale_a,
                                      biasM[:, c, kvh * 4:(kvh + 1) * 4, :], ALU.mult, ALU.add, [sp_.key, ("biasM",)], [t_.key])
                                b.act(pt.ap, t_.ap, AF.Exp, [t_.key], [pt.key])
                                pts.append((pt, off))
                            st["pts"] = pts

                        def s1(tt=tt, bl=bl, kvh=kvh, st=st):
                            tsl = slice(tt * TT, (tt + 1) * TT)
                            vt, oa = Vt[tt % 2], OAt[tt % 2]
                            pts = st["pts"]
                            po = b.psum_rot(4, 2)
                            pd = b.psum_rot(6, 2)
                            for i_, (pt, off) in enumerate(pts):
                                b.mm(po.ap, vt.ap[:, off // 128, kvh * 128:(kvh + 1) * 128], pt.ap, i_ == 0, i_ == len(pts) - 1,
                                     [vt.key, pt.key], [po.key])
                            for i_, (pt, off) in enumerate(pts):
                                b.mm(pd.ap, b.ones_bf, pt.ap, i_ == 0, False, [pt.key, K_CBF], [pd.key])
                            for hl in range(2):
                                b.mm(pd.ap, b.ones_bf[0:1, :], eskhl[0:1, hl * 1024 + kvh * 512:hl * 1024 + (kvh + 1) * 512], False, hl == 1,
                                     [("eskhl",), K_CBF], [pd.key])
                            r_ = rd[cic[1] % 2]
                            cic[1] += 1
                            b.act(r_.ap, pd.ap, AF.Ln, [pd.key], [r_.key])
                            b.act(r_.ap, r_.ap, AF.Exp, [r_.key], [r_.key], scale=-1.0)
                            b.tt("dve", oa.ap[:, kvh * 4:(kvh + 1) * 4, bl * 128:(bl + 1) * 128],
                                 po.ap.rearrange("p (h i) -> p h i", h=4), r_.ap.rearrange("p (h i) -> p h i", h=4), ALU.mult,
                                 [po.key, r_.key], [oa.key])
                            if bl == 3 and kvh == 1:
                                b.dma("sp", OT_d[0:8, :, tsl].rearrange("h p t -> p h t"), oa.ap, [oa.key], [("ot", c, tt) for c in range(8)])

                        units.append([s0, s1])
            b.pipeline(units)

            phase_begin()
            Kb = b.ring("Kb", [128, SEQ], BF16, 2)
            Vb = b.ring("Vb", [128, 16, 128], BF16, 2)
            Qb = b.ring("Qb", [128, TT], BF16, 2)
            SPr = [b.ring(f"SP{i}_", [128, TT], BF16, 16) for i in range(2)]
            ef = b.ring("ef", [128, TT], F32, 3)
            Wt = b.ring("Wt", [128, TT], BF16, 5)
            ob = b.ring("ob", [128, TT], BF16, 2)
            wic = [0, 0]
            items = [(h, qt) for h in range(4) for qt in range(NTT)]
            Aunits = []
            Bunits = []
            for it, (h, qt) in enumerate(items):
                kb_, vb_ = Kb[h % 2], Vb[h % 2]
                qb_ = Qb[it % 2]
                SP = SPr[it % 2]
                nk = 4 * qt + 4
                c0s = [max(0, (kc - 4 * qt) * 128) for kc in range(nk)]
                A = []
                for kc in range(nk):
                    def a0(h=h, qt=qt, kc=kc, kb_=kb_, vb_=vb_, qb_=qb_, SP=SP, c0s=c0s):
                        if kc == 0:
                            if qt == 0:
                                b.dma("sp", kb_.ap, KB_d[h], [("kb", h, t_) for t_ in range(4)], [kb_.key])
                                b.dma("sp", vb_.ap, VB_d[:, h * 128:(h + 1) * 128].rearrange("(c p) v -> p c v", p=128),
                                      [("vb", t_) for t_ in range(4)], [vb_.key])
                            b.dma("sp", qb_.ap, QB_d[h][:, qt * TT:(qt + 1) * TT], [("qb", h, qt)], [qb_.key])
                        c0 = c0s[kc]
                        z = b.psum_rot(5, 3)
                        b.mm(z.ap[:, c0:], kb_.ap[:, kc * 128:(kc + 1) * 128], qb_.ap[:, c0:], True, True, [kb_.key, qb_.key], [z.key])
                        e = ef[wic[1] % 3]
                        wic[1] += 1
                        sp_ = SP[kc]
                        b.act(e.ap[:, c0:], z.ap[:, c0:], AF.Exp, [z.key], [e.key])
                        b.act(sp_.ap[:, c0:], e.ap[:, c0:], AF.Ln, [e.key], [sp_.key], bias=b.one_col)
                        if kc >= 4 * qt:
                            b.tt("pool", sp_.ap[:, c0:c0 + 128], sp_.ap[:, c0:c0 + 128], tri_strict_bf, ALU.mult,
                                 [sp_.key, K_CBF], [sp_.key])
                    A.append([a0])
                Bq = []
                po_i = it % 2
                order = list(range(nk - 1, -1, -1))
                for idx, kc in enumerate(order):
                    st = {}

                    def b0(h=h, qt=qt, kc=kc, kb_=kb_, qb_=qb_, SP=SP, c0s=c0s, nk=nk, st=st):
                        c0 = c0s[kc]
                        a = b.psum_rot(2, 3)
                        b.mm(a.ap[:, c0:], kb_.ap[:, kc * 128:(kc + 1) * 128], qb_.ap[:, c0:], True, False, [kb_.key, qb_.key], [a.key])
                        later = list(range(kc + 1, nk))
                        b.mm(a.ap[:, c0:], uineg_bf, SP[kc].ap[:, c0:], False, len(later) == 0, [SP[kc].key, K_CBF], [a.key])
                        for li, k2 in enumerate(later):
                            c2 = max(c0, c0s[k2])
                            b.mm(a.ap[:, c2:], onesneg[:, :], SP[k2].ap[:, c2:], False, li == len(later) - 1,
                                 [SP[k2].key, K_CBF], [a.key])
                        w = Wt[wic[0] % 5]
                        wic[0] += 1
                        b.act(w.ap[:, c0:], a.ap[:, c0:], AF.Exp, [a.key], [w.key])
                        if kc >= 4 * qt:
                            b.tt("pool", w.ap[:, c0:c0 + 128], w.ap[:, c0:c0 + 128], tri_strict_bf, ALU.mult, [w.key, K_CBF], [w.key])
                        st["w"] = w

                    def b2(h=h, qt=qt, kc=kc, vb_=vb_, c0s=c0s, idx=idx, nk=nk, st=st, po_i=po_i, it=it):
                        c0 = c0s[kc]
                        w = st["w"]
                        po = b.psum_at(po_i)
                        b.mm(po.ap[:, c0:], vb_.ap[:, kc, :], w.ap[:, c0:], idx == 0, idx == nk - 1, [vb_.key, w.key], [po.key])
                        if idx == nk - 1:
                            o = ob[it % 2]
                            b.evac(o.ap, po.ap, [po.key], [o.key])
                            b.dma("sp", OT_d[8 + h][:, qt * TT:(qt + 1) * TT], o.ap, [o.key], [("ot", 8 + h, qt)])

                    Bq.append([b0, None, b2])
                Aunits.append(A)
                Bunits.append(Bq)
            allu = list(Aunits[0])
            for it in range(len(items)):
                Bq = Bunits[it]
                An = Aunits[it + 1] if it + 1 < len(items) else []
                i1 = i2 = 0
                while i1 < len(Bq) or i2 < len(An):
                    if i1 < len(Bq):
                        allu.append(Bq[i1])
                        i1 += 1
                    if i2 < len(An):
                        allu.append(An[i2])
                        i2 += 1
            b.pipeline(allu)

            phase_begin()
            Kn = b.ring("Kn", [128, SEQ], BF16, 2)
            Kr = b.alloc("Kr", [64, SEQ], BF16)
            Vc = b.ring("Vc", [128, 16, 128], BF16, 2)
            Qn = b.ring("Qn", [128, TT], BF16, 2)
            Qr = b.ring("Qr", [64, TT], BF16, 2)
            Pm = b.ring("Pm", [128, TT], BF16, 6)
            rdm = b.ring("rdm", [128, TT], F32, 2)
            om = b.ring("om", [128, TT], BF16, 2)
            b.dma("sp", Kr.ap, KR_d, [("kr", t_) for t_ in range(4)], [Kr.key])
            scale_c = 192 ** -0.5
            pic = [0]
            units = []
            for it, (h, qt) in enumerate([(h, qt) for h in range(4) for qt in range(NTT)]):
                kn_, vc_ = Kn[h % 2], Vc[h % 2]
                qn_, qr_ = Qn[it % 2], Qr[it % 2]
                nk = 4 * qt + 4
                for kc in range(nk):
                    st = {}

                    def m0(h=h, qt=qt, kc=kc, kn_=kn_, vc_=vc_, qn_=qn_, qr_=qr_, st=st):
                        if kc == 0:
                            if qt == 0:
                                b.dma("sp", kn_.ap, KN_d[h], [("kn", h, t_) for t_ in range(4)], [kn_.key])
                                b.dma("sp", vc_.ap, VC_d[:, h * 128:(h + 1) * 128].rearrange("(c p) v -> p c v", p=128),
                                      [("vc", t_) for t_ in range(4)], [vc_.key])
                            b.dma("sp", qn_.ap, QN_d[h][:, qt * TT:(qt + 1) * TT], [("qn", h, qt)], [qn_.key])
                            b.dma("sp", qr_.ap, QR_d[h][:, qt * TT:(qt + 1) * TT], [("qr", h, qt)], [qr_.key])
                        c0 = max(0, (kc - 4 * qt) * 128)
                        sc = b.psum_rot(4, 4)
                        b.mm(sc.ap[:, c0:], kn_.ap[:, kc * 128:(kc + 1) * 128], qn_.ap[:, c0:], True, False, [kn_.key, qn_.key], [sc.key])
                        b.mm(sc.ap[:, c0:], Kr.ap[:, kc * 128:(kc + 1) * 128], qr_.ap[:, c0:], False, True, [Kr.key, qr_.key], [sc.key])
                        pm = Pm[pic[0] % 6]
                        pic[0] += 1
                        b.act(pm.ap[:, c0:], sc.ap[:, c0:], AF.Exp, [sc.key], [pm.key], scale=scale_c)
                        if kc >= 4 * qt:
                            b.tt("pool", pm.ap[:, c0:c0 + 128], pm.ap[:, c0:c0 + 128], tri_incl_bf, ALU.mult, [pm.key, K_CBF], [pm.key])
                        st["pm"] = pm

                    def m2(h=h, qt=qt, kc=kc, vc_=vc_, nk=nk, st=st, it=it):
                        c0 = max(0, (kc - 4 * qt) * 128)
                        pm = st["pm"]
                        po = b.psum_at(it % 2)
                        pd = b.psum_at(2 + it % 2)
                        b.mm(po.ap[:, c0:], vc_.ap[:, kc, :], pm.ap[:, c0:], kc == 0, kc == nk - 1, [vc_.key, pm.key], [po.key])
                        b.mm(pd.ap[:, c0:], b.ones_bf, pm.ap[:, c0:], kc == 0, kc == nk - 1, [pm.key, K_CBF], [pd.key])
                        if kc == nk - 1:
                            r_ = rdm[it % 2]
                            b.recip(r_.ap, pd.ap, [pd.key], [r_.key])
                            o = om[it % 2]
                            b.tt("dve", o.ap, po.ap, r_.ap, ALU.mult, [po.key, r_.key], [o.key])
                            b.dma("sp", OT_d[12 + h][:, qt * TT:(qt + 1) * TT], o.ap, [o.key], [("ot", 12 + h, qt)])

                    units.append([m0, None, m2])
            b.pipeline(units)

            phase_begin()
            OTt = b.alloc("OTt", [128, 16, TT], BF16)
            mg = b.alloc("mg", [128, 16, TT], BF16)
            gt = b.ring("gt", [128, TT], F32, 4)
            tm = b.ring("tm", [128, TT], F32, 4)
            xp = b.ring("xp", [128, 4, TT], F32, 2)
            gi = 0
            def otbuf(tt):
                return OTt if tt == 0 else Tile(hT[tt - 1].ap, hT[tt - 1].key)

            def otload(tt):
                ot = otbuf(tt)
                b.dma("sp", ot.ap, OT_d[:, :, tt * TT:(tt + 1) * TT].rearrange("c p t -> p c t"), [("ot", c, tt) for c in range(16)], [ot.key])

            otload(0)
            for tt in range(NTT):
                tsl = slice(tt * TT, (tt + 1) * TT)
                OTc = otbuf(tt)
                for n in range(16):
                    wv, wk = b.wload(wpc_d[l, n], 512)
                    gts = []
                    for g3 in range(3):
                        p = b.psum()
                        for k in range(16):
                            b.mm(p.ap, wv[:, k, g3 * 128:(g3 + 1) * 128], hT[tt].ap[:, k, :], k == 0, k == 15, [wk, hT[tt].key], [p.key])
                        g_ = gt[gi % 4]
                        gi += 1
                        b.act(g_.ap, p.ap, AF.Sigmoid, [p.key], [g_.key])
                        gts.append(g_)
                    kr_ = [(0, 8), (8, 12), (12, 16)]
                    tms = []
                    for g3 in range(3):
                        p = b.psum()
                        k0, k1 = kr_[g3]
                        for k in range(k0, k1):
                            b.mm(p.ap, wv[:, k, 384:512], OTc.ap[:, k, :], k == k0, k == k1 - 1, [wk, OTc.key], [p.key])
                        if g3 < 2:
                            t_ = tm[(n * 2 + g3) % 4]
                            b.tt("dve", t_.ap, p.ap, gts[g3].ap, ALU.mult, [p.key, gts[g3].key], [t_.key])
                            tms.append(t_)
                        else:
                            b.tt("dve", gts[2].ap, p.ap, gts[2].ap, ALU.mult, [p.key, gts[2].key], [gts[2].key])
                    b.tt("dve", tms[0].ap, tms[0].ap, tms[1].ap, ALU.add, [tms[0].key, tms[1].key], [tms[0].key])
                    b.tt("dve", mg.ap[:, n, :], tms[0].ap, gts[2].ap, ALU.add, [tms[0].key, gts[2].key], [mg.key])
                if tt + 1 < NTT:
                    otload(tt + 1)
                for nb in range(4):
                    wv, wk = b.wload(wout_d[l][:, nb * 512:(nb + 1) * 512], 512)
                    xq = xp[nb % 2]
                    b.dma("sp", xq.ap, xT_d[s, :, nb * 4:(nb + 1) * 4, tsl], [("xT", s, tt)], [xq.key])
                    for j in range(4):
                        p = b.psum()
                        for k in range(16):
                            b.mm(p.ap, wv[:, k, j * 128:(j + 1) * 128], mg.ap[:, k, :], k == 0, k == 15, [wk, mg.key], [p.key])
                        b.tt("dve", xq.ap[:, j, :], xq.ap[:, j, :], p.ap, ALU.add, [xq.key, p.key], [xq.key])
                    b.dma("sp", xT_d[s, :, nb * 4:(nb + 1) * 4, tsl], xq.ap, [xq.key], [("xT", s, tt)])

            phase_begin(nsq=4, nrs=2)
            if debug and l == 0 and s == 0:
                b.dma("sp", dbg1_d, xT_d[s], [("xT", s, t_) for t_ in range(4)], [("dbg1",)])
            hx = [Tile(hT[i].ap, ("hx", i)) for i in range(4)]
            b.ps_limit = 7
            memn = b.alloc("memn", [128, 16, 256], BF16)
            Km = b.alloc("Km", [128, 4, 256], BF16)
            Vm = b.alloc("Vm", [128, 2, 512], BF16)
            Qx = b.ring("Qx", [128, 4, TT], BF16, 2)
            Ox = b.ring("Ox", [128, 4, TT], BF16, 2)
            Px = b.ring("Px", [128, TT], BF16, 2)
            nsqr = b.ring("nsq", [128, TT], BF16, 8)
            nsqc = [0]
            rdx = b.ring("rdx", [128, TT], F32, 2)
            xp = b.ring("xp", [128, 4, TT], F32, 3)
            rsn = b.alloc("rsn", [128, TT], F32)
            xpc = [0]
            normq = []
            for t_ in range(NTT):
                for f in norm_steps(b, xT_d[s], slice(t_ * TT, (t_ + 1) * TT), ("xT", s, t_), [vcol(b, l, 16 + c) for c in range(16)],
                                    hx[t_], xp, xpc, rsn, K_VEC, nsqr, nsqc):
                    normq.append((t_, f))

            def npop(k):
                for _ in range(k):
                    if normq:
                        normq.pop(0)[1]()

            def nensure(t_):
                while normq and normq[0][0] <= t_:
                    normq.pop(0)[1]()

            npop(2)
            for g in range(4):
                npop(1)
                m_ = Tile(xp[g % 2].ap[:, :, 0:256], xp[g % 2].key)
                b.dma("sp", m_.ap, memT_d[s, :, g * 4:(g + 1) * 4, :], [("memT", s)], [m_.key])
                for j in range(4):
                    c = g * 4 + j
                    b.ts("dve", memn.ap[:, c, :], m_.ap[:, j, :], vcol(b, l, 32 + c), None, ALU.mult, None, [m_.key, K_VEC], [memn.key])
            for blk in range(2):
                wv, wk = b.wload(xwkv_d[l][:, blk * 512:(blk + 1) * 512], 512)
                for hh in range(2):
                    h = blk * 2 + hh
                    p = b.psum()
                    for k in range(16):
                        b.mm(p.ap[:, 0:256], wv[:, k, hh * 256:hh * 256 + 128], memn.ap[:, k, :], k == 0, k == 15, [wk, memn.key], [p.key])
                    b.pnorm([(p.ap[:, 0:256], p.key)], [vcol(b, l, 67)], [(Km.ap[:, h, :], Km.key)], 128, N=256, extra_reads=[K_VEC])
                    npop(3)
                for mc in range(2):
                    p = b.psum()
                    for k in range(16):
                        rhs = wv[:, k, :].rearrange("p (h two d) -> p h two d", h=2, two=2)[:, :, 1, :]
                        b.mm(p.ap[:, 0:256].rearrange("p (h d) -> p h d", h=2), memn.ap[:, k, mc * 128:(mc + 1) * 128], rhs,
                             k == 0, k == 15, [wk, memn.key], [p.key])
                    b.evac(Vm.ap[:, mc, blk * 256:(blk + 1) * 256], p.ap[:, 0:256], [p.key], [Vm.key])
                    npop(3)
            wqv, wqk = b.wload(xwq_d[l], 512)
            woi = b.w_i % len(b.wb)
            b.w_i += 1
            wot = b.wb[woi]
            wov = wot.ap.rearrange("p (c n) -> p c n", c=4)
            for q4 in range(4):
                b.dma("pool", wov[:, :, q4 * 512:(q4 + 1) * 512],
                      xwo_d[l][:, q4 * 512:(q4 + 1) * 512].rearrange("(c p) n -> p c n", p=128), [], [wot.key])
            scale_x = 128 ** -0.5
            pxc = [0]
            units = []
            for tt in range(NTT):
                def st1(tt=tt):
                    nensure(tt)
                    hxx = hx[tt]
                    qx = Qx[tt % 2]
                    us = []
                    for h in range(4):
                        def mmq(p, h=h):
                            for k in range(16):
                                b.mm(p.ap, wqv[:, k, h * 128:(h + 1) * 128], hxx.ap[:, k, :], k == 0, k == 15, [wqk, hxx.key], [p.key])
                        us.append(b.norm_unit(mmq, lambda p: [(p.ap, p.key)], [vcol(b, l, 66)], [(qx.ap[:, h, :], qx.key)], 128,
                                              extra_reads=[K_VEC], pslo=0, psn=3, sslo=3, ssn=2))
                    b.pipeline(us)
                    npop(3)

                def st2(tt=tt):
                    qx = Qx[tt % 2]
                    ox = Ox[tt % 2]
                    for h in range(4):
                        pts = []
                        for mc in range(2):
                            sc = b.psum_rot(0, 7)
                            b.mm(sc.ap, Km.ap[:, h, mc * 128:(mc + 1) * 128], qx.ap[:, h, :], True, True, [Km.key, qx.key], [sc.key])
                            px = Px[pxc[0] % 2]
                            pxc[0] += 1
                            b.act(px.ap, sc.ap, AF.Exp, [sc.key], [px.key], scale=scale_x)
                            pts.append(px)
                        po = b.psum_rot(0, 7)
                        pd = b.psum_rot(0, 7)
                        for mc in range(2):
                            b.mm(po.ap, Vm.ap[:, mc, h * 128:(h + 1) * 128], pts[mc].ap, mc == 0, mc == 1, [Vm.key, pts[mc].key], [po.key])
                        for mc in range(2):
                            b.mm(pd.ap, b.ones_bf, pts[mc].ap, mc == 0, mc == 1, [pts[mc].key, K_CBF], [pd.key])
                        r_ = rdx[h % 2]
                        b.recip(r_.ap, pd.ap, [pd.key], [r_.key])
                        b.tt("dve", ox.ap[:, h, :], po.ap, r_.ap, ALU.mult, [po.key, r_.key], [ox.key])

                def st3(tt=tt):
                    tsl = slice(tt * TT, (tt + 1) * TT)
                    ox = Ox[tt % 2]
                    for nb in range(4):
                        xq = xp[xpc[0] % 3]
                        xpc[0] += 1
                        b.dma("sp", xq.ap, xT_d[s, :, nb * 4:(nb + 1) * 4, tsl], [("xT", s, tt)], [xq.key])
                        for j in range(4):
                            n = nb * 4 + j
                            p = b.psum_rot(0, 7)
                            for h in range(4):
                                b.mm(p.ap, wov[:, h, n * 128:(n + 1) * 128], ox.ap[:, h, :], h == 0, h == 3, [wot.key, ox.key], [p.key])
                            b.tt("dve", xq.ap[:, j, :], xq.ap[:, j, :], p.ap, ALU.add, [xq.key, p.key], [xq.key])
                        b.dma("sp", xT_d[s, :, nb * 4:(nb + 1) * 4, tsl], xq.ap, [xq.key], [("xT", s, tt)])

                units.append([st1, st2, st3])
            b.pipeline(units, oldest_first=True)
            b.ps_limit = 8

            phase_begin(nsq=1, nrs=1)
            if debug and l == 0 and s == 0:
                b.dma("sp", dbg2_d, xT_d[s], [("xT", s, t_) for t_ in range(4)], [("dbg2",)])
            b.ps_limit = 7
            hfA = Tile(hT_flat[:, 0:16 * TT].rearrange("p (c t) -> p c t", c=16), ("hfA",))
            actt = Tile(hT_flat[:, 16 * TT:(16 + 44) * TT].rearrange("p (c t) -> p c t", c=44), ("actt",))
            hfB = b.alloc("hfB", [128, 16, TT], BF16)
            hfs = [hfA, hfB]
            xp = b.ring("xp", [128, 4, TT], F32, 3)
            rsn = b.alloc("rsn", [128, TT], F32)
            nsqr = b.ring("nsq", [128, TT], BF16, 8)
            nsqc = [0]
            sg = b.ring("sg", [128, TT], F32, 3)
            xpc = [0]
            sgi = 0

            def pf_norm(tt):
                return norm_steps(b, xT_d[s], slice(tt * TT, (tt + 1) * TT), ("xT", s, tt), [vcol(b, l, 48 + c) for c in range(16)],
                                  hfs[tt % 2], xp, xpc, rsn, K_VEC, nsqr, nsqc)

            for f in pf_norm(0):
                f()
            for tt in range(NTT):
                tsl = slice(tt * TT, (tt + 1) * TT)
                hf_t = hfs[tt % 2]
                nxt = pf_norm(tt + 1) if tt + 1 < NTT else []
                for blk in range(22):
                    wv, wk = b.wload(wgu_d[l, blk], 512)
                    for jj in range(2):
                        j = blk * 2 + jj
                        pg = b.psum_rot(0, 7)
                        pu = b.psum_rot(0, 7)
                        for k in range(16):
                            b.mm(pg.ap, wv[:, k, jj * 128:(jj + 1) * 128], hf_t.ap[:, k, :], k == 0, k == 15, [wk, hf_t.key], [pg.key])
                        for k in range(16):
                            b.mm(pu.ap, wv[:, k, 256 + jj * 128:256 + (jj + 1) * 128], hf_t.ap[:, k, :], k == 0, k == 15, [wk, hf_t.key], [pu.key])
                        s_ = sg[sgi % 3]
                        sgi += 1
                        b.act(s_.ap, pg.ap, AF.Silu, [pg.key], [s_.key])
                        b.tt("dve", actt.ap[:, j, :], s_.ap, pu.ap, ALU.mult, [s_.key, pu.key], [actt.key])
                    if blk >= 4 and blk % 2 == 0 and nxt:
                        nxt.pop(0)()
                for nb in range(4):
                    pss = [b.psum_rot(0, 7) for _ in range(4)]
                    for kp in range(3):
                        kc = 16 if kp < 2 else 12
                        wv, wk = b.wload(wdn_d[l][kp * 2048:kp * 2048 + kc * 128, nb * 512:(nb + 1) * 512], 512, kc=kc)
                        for j in range(4):
                            for k in range(kc):
                                kk = kp * 16 + k
                                b.mm(pss[j].ap, wv[:, k, j * 128:(j + 1) * 128], actt.ap[:, kk, :], kk == 0, kk == 43, [wk, actt.key], [pss[j].key])
                        if nxt:
                            nxt.pop(0)()
                    xq = xp[xpc[0] % 3]
                    xpc[0] += 1
                    b.dma("sp", xq.ap, xT_d[s, :, nb * 4:(nb + 1) * 4, tsl], [("xT", s, tt)], [xq.key])
                    for j in range(4):
                        b.tt("dve", xq.ap[:, j, :], xq.ap[:, j, :], pss[j].ap, ALU.add, [xq.key, pss[j].key], [xq.key])
                    b.dma("sp", xT_d[s, :, nb * 4:(nb + 1) * 4, tsl], xq.ap, [xq.key], [("xT", s, tt)])
                while nxt:
                    nxt.pop(0)()
            b.ps_limit = 8

    phase_begin()
    xi2 = b.ring("xi2", [128, 16, 256], F32, 2)
    xo2 = b.ring("xo2", [128, D], F32, 2)
    for s in range(NS):
        for tp in range(8):
            xi = xi2[tp % 2]
            b.dma("sp", xi.ap, xT_d[s, :, :, tp * 256:(tp + 1) * 256], [("xT", s, tp // 2)], [xi.key])
            for half in range(2):
                tc = tp * 2 + half
                xo = xo2[tc % 2]
                for g in range(4):
                    p = b.psum()
                    for j in range(4):
                        c = g * 4 + j
                        b.tr(p.ap[:, j * 128:(j + 1) * 128], xi.ap[:, c, half * 128:(half + 1) * 128], ident_f, [xi.key, K_CST], [p.key])
                    b.evac(xo.ap[:, g * 512:(g + 1) * 512], p.ap, [p.key], [xo.key])
                b.dma("sp", out_d[s, tc * 128:(tc + 1) * 128, :], xo.ap, [xo.key], [("out", s, tc)])

    S.emit(nc, sems, dsems)
    es.close()
    return nc, S


def t5_bucket_np(rel):
    n = np.maximum(rel, 0)
    exact = 16
    nf = np.maximum(n, exact).astype(np.float32)
    large = exact + (np.log(nf / np.float32(exact)) / np.float32(math.log(128 / exact)) * np.float32(32 - exact)).astype(np.int32)
    large = np.minimum(large, 31)
    return np.where(n < exact, n, large)


def make_consts():
    NC = 128 * 4 + 256 + 64 + 4
    c = np.zeros((128, NC), np.float32)
    c[:, 0:128] = np.eye(128, dtype=np.float32)
    j = np.arange(128)[:, None]
    k = np.arange(128)[None, :]
    c[:, 128:256] = np.where(j >= k, -1.0, 0.0)
    c[:, 256:384] = np.where(k >= j, 1.0, 0.0)
    c[:, 384:512] = np.where(k > j, 1.0, 0.0)
    c[:, 512:640] = np.where(j > k, 0.0, NEG)
    c[:, 640:768] = np.where(j <= k, 0.0, NEG)
    rr = np.zeros((64, 64), np.float32)
    for m in range(64):
        if m < 32:
            rr[m + 32, m] = -1.0
        else:
            rr[m - 32, m] = 1.0
    c[0:64, 768:832] = rr
    half = 32
    inv = (np.float32(10000.0) ** (-np.arange(half, dtype=np.float32) / np.float32(half))).astype(np.float32)
    c[0:64, 832] = np.concatenate([inv, inv])
    return c


def prep_shared(inp, NL):
    f = lambda a: np.ascontiguousarray(np.asarray(a, dtype=np.float32))
    sh = {}
    rel_bias = f(inp["rel_bias"])
    jj = np.arange(128)[:, None]
    ii = np.arange(128)[None, :]
    bt = np.zeros((128, 2, 8, 128), np.float32)
    for c in range(2):
        rel = 128 + ii - (jj + 128 * c)
        bk = t5_bucket_np(rel)
        bt[:, c, :, :] = np.transpose(rel_bias[bk], (0, 2, 1))
    sh["biasT"] = bt.reshape(128, -1)
    sh["sinks"] = np.ascontiguousarray(np.repeat(f(inp["swa_sinks"])[:NL, None, :, None], 128, axis=3).reshape(NL, 1, 1024))
    vecs = np.zeros((128, NL * NVEC), np.float32)
    for l in range(NL):
        o = l * NVEC
        vecs[:, o:o + 16] = f(inp["norm_mix"])[l].reshape(16, 128).T
        vecs[:, o + 16:o + 32] = f(inp["norm_xa"])[l].reshape(16, 128).T
        vecs[:, o + 32:o + 48] = f(inp["norm_mem"])[l].reshape(16, 128).T
        vecs[:, o + 48:o + 64] = f(inp["norm_ffn"])[l].reshape(16, 128).T
        vecs[:, o + 64] = f(inp["swa_q_norm"])[l]
        vecs[:, o + 65] = f(inp["swa_k_norm"])[l]
        vecs[:, o + 66] = f(inp["xa_q_norm"])[l]
        vecs[:, o + 67] = f(inp["xa_k_norm"])[l]
        vecs[:, o + 68:o + 72] = f(inp["mla_cq_norm"])[l].reshape(4, 128).T
        vecs[:, o + 72:o + 76] = f(inp["mla_ckv_norm"])[l].reshape(4, 128).T
        qg = f(inp["mla_q_norm"])[l]
        kg = f(inp["mla_k_norm"])[l]
        vecs[:, o + 76] = qg[:128]
        vecs[0:64, o + 77] = qg[128:]
        vecs[:, o + 78] = kg[:128]
        vecs[0:64, o + 79] = kg[128:]
    sh["vecs"] = vecs
    sh["consts"] = make_consts()
    w_in = f(inp["w_in"])[:NL]
    sh["w_in"] = np.ascontiguousarray(w_in[:, :, :IN_QKV])
    gates = w_in[:, :, IN_QKV:]
    wbr = f(inp["w_branch"])[:NL]
    wpc = np.empty((NL, 16, D, 512), np.float32)
    for n in range(16):
        for g3 in range(3):
            wpc[:, n, :, g3 * 128:(g3 + 1) * 128] = gates[:, :, g3 * 2048 + n * 128:g3 * 2048 + (n + 1) * 128]
        wpc[:, n, :, 384:512] = wbr[:, :, n * 128:(n + 1) * 128]
    sh["w_pc"] = wpc
    sh["w_out"] = f(inp["w_out"])[:NL]
    sh["w_uq"] = f(inp["mla_w_uq"])[:NL]
    sh["w_ukv"] = f(inp["mla_w_ukv"])[:NL]
    sh["xa_wq"] = f(inp["xa_wq"])[:NL]
    sh["xa_wkv"] = f(inp["xa_wkv"])[:NL]
    sh["xa_wo"] = f(inp["xa_wo"])[:NL]
    wgu = f(inp["ffn_w_gu"])[:NL]
    blk = np.empty((NL, 22, D, 512), np.float32)
    for bi in range(22):
        blk[:, bi, :, 0:256] = wgu[:, :, bi * 256:(bi + 1) * 256]
        blk[:, bi, :, 256:512] = wgu[:, :, D_FF + bi * 256:D_FF + (bi + 1) * 256]
    sh["w_gu"] = blk
    sh["w_down"] = f(inp["ffn_w_down"])[:NL]
    return sh


def prep_core(inp, bsel):
    x = np.ascontiguousarray(np.asarray(inp["x"], np.float32)[bsel])
    mem = np.ascontiguousarray(np.asarray(inp["mem"], np.float32)[bsel])
    pos = np.asarray(inp["positions"]).astype(np.int32)[bsel]
    pos = np.ascontiguousarray(np.repeat(pos[:, None, :], 64, axis=1))
    return {"x": x, "mem": mem, "pos": pos}


_CACHE = {}


def kernel(**inputs):
    n_cores = 8
    NS = 2
    key = (DEPTH, NS)
    if key not in _CACHE:
        _CACHE[key] = build_program(DEPTH, NS)[0]
    nc = _CACHE[key]
    sh = prep_shared(inputs, DEPTH)
    in_maps = []
    for c in range(n_cores):
        m = dict(sh)
        m.update(prep_core(inputs, slice(c * NS, (c + 1) * NS)))
        in_maps.append(m)
    res = run_bass_kernel_spmd(nc, in_maps, core_ids=list(range(n_cores)))
    out = np.concatenate([np.asarray(r["out"]) for r in res.results], axis=0)
    return out.astype(np.float32)
```
